# Optimizing a Trainium2 kernel written in Bass

```python
import math
import jax, jax.numpy as jnp
from jax import lax
import numpy as np

D_MODEL = 1024
BATCH = 8
SEQ = 8192
DEPTH = 2

GRID_W = 64
CTX_LEN = 256
N_BRANCH = 4
BRANCH_W = 256
EPS = 1e-6
ROPE_BASE = 10000.0
BLOCK = 128
NEG_INF = -1e30

POOL_WINDOWS = (2, 4, 8, 16)
POOL_GROUP = BRANCH_W // len(POOL_WINDOWS)

MLA_HEADS = 4
MLA_NOPE = 64
MLA_ROPE = 32
MLA_V = 64
MLA_Q_RANK = 192
MLA_KV_RANK = 128

DIFF_HEADS = 4
DIFF_QK = 32
DIFF_V = 2 * DIFF_QK

SWA_HEADS = 4
SWA_KV_HEADS = 2
SWA_HEAD = 64
SWA_WINDOW = 128

SPLITS = (
    ('pool_in', BRANCH_W),
    ('mla_cq', MLA_Q_RANK),
    ('mla_ckv', MLA_KV_RANK),
    ('mla_kr', MLA_ROPE),
    ('diff_q', DIFF_HEADS * 2 * DIFF_QK),
    ('diff_k', DIFF_HEADS * 2 * DIFF_QK),
    ('diff_v', DIFF_HEADS * DIFF_V),
    ('swa_q', SWA_HEADS * SWA_HEAD),
    ('swa_k', SWA_KV_HEADS * SWA_HEAD),
    ('swa_v', SWA_KV_HEADS * SWA_HEAD),
    ('gates', N_BRANCH * BRANCH_W),
    ('merge', N_BRANCH * D_MODEL),
)
IN_COLS = sum(w for _, w in SPLITS)

kernel_name = 'hybrid_pool_mla_diff_swa_prefix_block'


def rms_norm(x, g):
    xf = x.astype(jnp.float32)
    y = xf * lax.rsqrt(jnp.mean(xf * xf, axis=-1, keepdims=True) + EPS)
    return (y * g.astype(jnp.float32)).astype(x.dtype)


def split_cols(z):
    out, off = {}, 0
    for name, w in SPLITS:
        out[name] = z[..., off:off + w]
        off += w
    return out


def axial_rope_tables(n_tok, rot_dim, dtype):
    t = jnp.arange(n_tok, dtype=jnp.int32)
    row = (t // GRID_W).astype(jnp.float32)
    col = (t % GRID_W).astype(jnp.float32)
    n_freq = rot_dim // 4
    inv_freq = jnp.exp(-math.log(ROPE_BASE) * jnp.arange(n_freq, dtype=jnp.float32) / n_freq)
    ang = jnp.concatenate([row[:, None] * inv_freq, col[:, None] * inv_freq], axis=-1)
    return jnp.cos(ang).astype(dtype), jnp.sin(ang).astype(dtype)


def apply_rope(x, cos, sin):
    half = x.shape[-1] // 2
    x1, x2 = x[..., :half], x[..., half:]
    c = cos[None, :, None, :]
    s = sin[None, :, None, :]
    return jnp.concatenate([x1 * c - x2 * s, x1 * s + x2 * c], axis=-1)


def pool_mixer(u, w_pool, s_pool):
    B, N, W = u.shape
    uf = u.astype(jnp.float32)
    cs = jnp.concatenate([jnp.zeros((B, 1, W), jnp.float32), jnp.cumsum(uf, axis=1)], axis=1)
    t = jnp.arange(N, dtype=jnp.int32)
    groups = []
    for gi, w in enumerate(POOL_WINDOWS):
        lo = jnp.clip(t - w // 2, 0, N)
        hi = jnp.clip(t - w // 2 + w, 0, N)
        sl = slice(gi * POOL_GROUP, (gi + 1) * POOL_GROUP)
        csg = cs[..., sl]
        win_sum = jnp.take(csg, hi, axis=1) - jnp.take(csg, lo, axis=1)
        mean = win_sum / (hi - lo).astype(jnp.float32)[None, :, None]
        groups.append(mean - uf[..., sl])
    y = jnp.stack(groups, axis=2).astype(u.dtype)
    y = jnp.einsum('bngc,gcd->bngd', y, w_pool).reshape(B, N, W)
    return y * s_pool


def dense_attention(q, k, v, coef, scale):
    M, B, S, H, D = q.shape
    nb = S // BLOCK
    qb = jnp.moveaxis(q.reshape(M, B, nb, BLOCK, H, D), 2, 0)
    cf = coef.astype(jnp.float32)

    def one(q_blk):
        s = jnp.einsum('mbqhd,mbkhd->mbhqk', q_blk, k).astype(jnp.float32) * scale
        p = jnp.einsum('m,mbhqk->bhqk', cf, jax.nn.softmax(s, axis=-1))
        return jnp.einsum('bhqk,bkhd->bqhd', p.astype(v.dtype), v)

    o = lax.map(one, qb)
    return jnp.moveaxis(o, 0, 1).reshape(B, S, H, v.shape[-1])


def sink_attend(q, keys, values, masks, sink_hg, scale):
    scores = []
    for k, m in zip(keys, masks):
        s = jnp.einsum('bqhgd,bkhd->bhgqk', q, k).astype(jnp.float32) * scale
        if m is not None:
            s = jnp.where(m, s, NEG_INF)
        scores.append(s)
    s_sink = jnp.broadcast_to(sink_hg[None, :, :, None, None], scores[0].shape[:-1] + (1,))
    p = jax.nn.softmax(jnp.concatenate(scores + [s_sink], axis=-1), axis=-1)
    outs, off = [], 0
    for s, v in zip(scores, values):
        n = s.shape[-1]
        outs.append(jnp.einsum('bhgqk,bkhd->bqhgd', p[..., off:off + n].astype(v.dtype), v))
        off += n
    return sum(outs)


def windowed_attention(q, k, v, k_ctx, v_ctx, sink_hg):
    B, S, Hkv, G, D = q.shape
    nb = S // BLOCK
    scale = D ** -0.5
    qb = jnp.moveaxis(q.reshape(B, nb, BLOCK, Hkv, G, D), 1, 0)
    pad = ((0, 0), (BLOCK, BLOCK), (0, 0), (0, 0))
    kp = jnp.pad(k, pad).reshape(B, nb + 2, BLOCK, Hkv, D)
    vp = jnp.pad(v, pad).reshape(B, nb + 2, BLOCK, Hkv, D)
    kw = jnp.moveaxis(jnp.concatenate([kp[:, :-2], kp[:, 1:-1], kp[:, 2:]], axis=2), 1, 0)
    vw = jnp.moveaxis(jnp.concatenate([vp[:, :-2], vp[:, 1:-1], vp[:, 2:]], axis=2), 1, 0)
    a = jnp.arange(BLOCK)[:, None]
    j = jnp.arange(3 * BLOCK)[None, :]
    band = jnp.abs(a + BLOCK - j) <= SWA_WINDOW
    kpos = (jnp.arange(nb)[:, None] - 1) * BLOCK + jnp.arange(3 * BLOCK)[None, :]
    inside = (kpos >= 0) & (kpos < S)
    mask = band[None] & inside[:, None, :]

    def one(args):
        qi, ki, vi, mi = args
        return sink_attend(qi, [k_ctx, ki], [v_ctx, vi], [None, mi], sink_hg, scale)

    o = lax.map(one, (qb, kw, vw, mask))
    return jnp.moveaxis(o, 0, 1).reshape(B, S, Hkv * G * D)


def hybrid_layer(layer_idx, x, xc, c, c_ctx, w_mod, b_mod, g_pre, g_post, w_in, w_pool, s_pool,
                 g_cq, w_uq, g_ckv, w_uk, w_uv, lam_q1, lam_k1, lam_q2, lam_k2, g_diff,
                 sink, w_br, w_out, need_ctx):
    B, S, D = x.shape
    C = xc.shape[1]
    G = SWA_HEADS // SWA_KV_HEADS
    f32 = jnp.float32

    shift, scale, gate = jnp.split(jax.nn.silu(c) @ w_mod + b_mod, 3, axis=-1)
    shift_c, scale_c, gate_c = jnp.split(jax.nn.silu(c_ctx) @ w_mod + b_mod, 3, axis=-1)
    h = rms_norm(x, g_pre) * (1 + scale[:, None, :]) + shift[:, None, :]
    hc = rms_norm(xc, g_pre) * (1 + scale_c) + shift_c
    z = split_cols(h @ w_in)
    zc = split_cols(hc @ w_in)

    rope_tab = {d: axial_rope_tables(S, d, x.dtype) for d in (MLA_ROPE, DIFF_QK, SWA_HEAD)}

    def rot(t, d):
        return apply_rope(t, *rope_tab[d])

    def mla_q(zz, pos):
        n = zz['mla_cq'].shape[1]
        q = (rms_norm(zz['mla_cq'], g_cq) @ w_uq).reshape(B, n, MLA_HEADS, MLA_NOPE + MLA_ROPE)
        q_rope = rot(q[..., MLA_NOPE:], MLA_ROPE) if pos else q[..., MLA_NOPE:]
        return jnp.concatenate([q[..., :MLA_NOPE], q_rope], axis=-1)

    def mla_kv(zz, pos):
        n = zz['mla_ckv'].shape[1]
        ckv = rms_norm(zz['mla_ckv'], g_ckv)
        k_nope = (ckv @ w_uk).reshape(B, n, MLA_HEADS, MLA_NOPE)
        v = (ckv @ w_uv).reshape(B, n, MLA_HEADS, MLA_V)
        k_rope = zz['mla_kr'][:, :, None, :]
        if pos:
            k_rope = rot(k_rope, MLA_ROPE)
        k = jnp.concatenate([k_nope, jnp.broadcast_to(k_rope, (B, n, MLA_HEADS, MLA_ROPE))], axis=-1)
        return k, v

    one_map = jnp.ones((1,), f32)
    mla_scale = (MLA_NOPE + MLA_ROPE) ** -0.5
    k_mc, v_mc = mla_kv(zc, False)
    k_ml, v_ml = mla_kv(z, True)
    y_mla = dense_attention(mla_q(z, True)[None], jnp.concatenate([k_mc, k_ml], axis=1)[None],
                            jnp.concatenate([v_mc, v_ml], axis=1), one_map, mla_scale).reshape(B, S, MLA_HEADS * MLA_V)

    def diff_split(t, pos):
        n = t.shape[1]
        t = t.reshape(B, n, DIFF_HEADS * 2, DIFF_QK)
        if pos:
            t = rot(t, DIFF_QK)
        return jnp.moveaxis(t.reshape(B, n, DIFF_HEADS, 2, DIFF_QK), 3, 0)

    lam_init = 0.8 - 0.6 * math.exp(-0.3 * layer_idx)
    lam = (jnp.exp(jnp.sum(lam_q1.astype(f32) * lam_k1.astype(f32)))
           - jnp.exp(jnp.sum(lam_q2.astype(f32) * lam_k2.astype(f32))) + lam_init)
    diff_coef = jnp.stack([jnp.ones((), f32), -lam])

    def diff_out(o):
        return (rms_norm(o, g_diff) * (1 - lam_init)).reshape(B, o.shape[1], DIFF_HEADS * DIFF_V)

    k_dc = diff_split(zc['diff_k'], False)
    v_dc = zc['diff_v'].reshape(B, C, DIFF_HEADS, DIFF_V)
    k_dl = diff_split(z['diff_k'], True)
    v_dl = z['diff_v'].reshape(B, S, DIFF_HEADS, DIFF_V)
    y_diff = diff_out(dense_attention(diff_split(z['diff_q'], True), jnp.concatenate([k_dc, k_dl], axis=2),
                                      jnp.concatenate([v_dc, v_dl], axis=1), diff_coef, DIFF_QK ** -0.5))

    sink_hg = sink.astype(f32).reshape(SWA_KV_HEADS, G)

    def swa_q(zz, pos):
        n = zz['swa_q'].shape[1]
        q = zz['swa_q'].reshape(B, n, SWA_HEADS, SWA_HEAD)
        if pos:
            q = rot(q, SWA_HEAD)
        return q.reshape(B, n, SWA_KV_HEADS, G, SWA_HEAD)

    def swa_kv(zz, pos):
        n = zz['swa_k'].shape[1]
        k = zz['swa_k'].reshape(B, n, SWA_KV_HEADS, SWA_HEAD)
        if pos:
            k = rot(k, SWA_HEAD)
        return k, zz['swa_v'].reshape(B, n, SWA_KV_HEADS, SWA_HEAD)

    k_sc, v_sc = swa_kv(zc, False)
    k_sl, v_sl = swa_kv(z, True)
    y_swa = windowed_attention(swa_q(z, True), k_sl, v_sl, k_sc, v_sc, sink_hg)

    y_pool = pool_mixer(z['pool_in'], w_pool, s_pool)

    def merge(ys, zz):
        n = zz['gates'].shape[1]
        gates = jax.nn.silu(zz['gates']).reshape(B, n, N_BRANCH, BRANCH_W)
        mgate = jax.nn.sigmoid(zz['merge']).reshape(B, n, N_BRANCH, D)
        m = sum(mgate[:, :, r] * ((ys[r] * gates[:, :, r]) @ w_br[r]) for r in range(N_BRANCH))
        return rms_norm(m @ w_out, g_post)

    x_new = x + gate[:, None, :] * merge([y_pool, y_mla, y_diff, y_swa], z)

    if need_ctx:
        yc_pool = pool_mixer(zc['pool_in'], w_pool, s_pool)
        yc_mla = dense_attention(mla_q(zc, False)[None], k_mc[None], v_mc, one_map,
                                 mla_scale).reshape(B, C, MLA_HEADS * MLA_V)
        yc_diff = diff_out(dense_attention(diff_split(zc['diff_q'], False), k_dc, v_dc, diff_coef, DIFF_QK ** -0.5))
        yc_swa = sink_attend(swa_q(zc, False), [k_sc], [v_sc], [None], sink_hg,
                             SWA_HEAD ** -0.5).reshape(B, C, SWA_HEADS * SWA_HEAD)
        xc = xc + gate_c * merge([yc_pool, yc_mla, yc_diff, yc_swa], zc)
    return x_new, xc


def setup_inputs(seed: int = 0) -> dict:
    key = jax.random.key(seed)
    ks = jax.random.split(key, 24)
    L = DEPTH

    def nrm(k, shape, s):
        return jax.random.normal(k, shape, jnp.float32) * s

    return {
        'x': nrm(ks[0], (BATCH, SEQ, D_MODEL), 1.0),
        'c': nrm(ks[1], (BATCH, D_MODEL), 1.0),
        'ctx': nrm(ks[2], (BATCH, CTX_LEN, D_MODEL), 1.0),
        'c_ctx': nrm(ks[3], (D_MODEL,), 1.0),
        'w_mod': nrm(ks[4], (L, D_MODEL, 3 * D_MODEL), 0.5 * D_MODEL ** -0.5),
        'b_mod': nrm(ks[5], (L, 3 * D_MODEL), 0.01),
        'g_pre': 1.0 + nrm(ks[6], (L, D_MODEL), 0.05),
        'g_post': 1.0 + nrm(ks[7], (L, D_MODEL), 0.05),
        'w_in': nrm(ks[8], (L, D_MODEL, IN_COLS), D_MODEL ** -0.5),
        'w_pool': nrm(ks[9], (L, len(POOL_WINDOWS), POOL_GROUP, POOL_GROUP), POOL_GROUP ** -0.5),
        's_pool': 1.0 + nrm(ks[10], (L, BRANCH_W), 0.05),
        'g_cq': 1.0 + nrm(ks[11], (L, MLA_Q_RANK), 0.05),
        'w_uq': nrm(ks[12], (L, MLA_Q_RANK, MLA_HEADS * (MLA_NOPE + MLA_ROPE)), MLA_Q_RANK ** -0.5),
        'g_ckv': 1.0 + nrm(ks[13], (L, MLA_KV_RANK), 0.05),
        'w_uk': nrm(ks[14], (L, MLA_KV_RANK, MLA_HEADS * MLA_NOPE), MLA_KV_RANK ** -0.5),
        'w_uv': nrm(ks[15], (L, MLA_KV_RANK, MLA_HEADS * MLA_V), MLA_KV_RANK ** -0.5),
        'lam_q1': nrm(ks[16], (L, DIFF_QK), 0.1),
        'lam_k1': nrm(ks[17], (L, DIFF_QK), 0.1),
        'lam_q2': nrm(ks[18], (L, DIFF_QK), 0.1),
        'lam_k2': nrm(ks[19], (L, DIFF_QK), 0.1),
        'g_diff': 1.0 + nrm(ks[20], (L, DIFF_V), 0.05),
        'sink': nrm(ks[21], (L, SWA_HEADS), 0.5),
        'w_br': nrm(ks[22], (L, N_BRANCH, BRANCH_W, D_MODEL), BRANCH_W ** -0.5),
        'w_out': nrm(ks[23], (L, D_MODEL, D_MODEL), D_MODEL ** -0.5),
    }


def reference(x, c, ctx, c_ctx, w_mod, b_mod, g_pre, g_post, w_in, w_pool, s_pool, g_cq, w_uq, g_ckv,
              w_uk, w_uv, lam_q1, lam_k1, lam_q2, lam_k2, g_diff, sink, w_br, w_out):
    xc = ctx
    for l in range(DEPTH):
        x, xc = hybrid_layer(l, x, xc, c, c_ctx, w_mod[l], b_mod[l], g_pre[l], g_post[l], w_in[l],
                             w_pool[l], s_pool[l], g_cq[l], w_uq[l], g_ckv[l], w_uk[l], w_uv[l],
                             lam_q1[l], lam_k1[l], lam_q2[l], lam_k2[l], g_diff[l], sink[l],
                             w_br[l], w_out[l], l < DEPTH - 1)
    return x
```

```python
import math
from contextlib import ExitStack

import numpy as np
import ml_dtypes

import concourse.bass as bass
import concourse.mybir as mybir
from concourse.bass_utils import run_bass_kernel_spmd

F32 = mybir.dt.float32
BF16 = mybir.dt.bfloat16
AF = mybir.ActivationFunctionType
ALU = mybir.AluOpType

D = 1024
CTX = 256
DEPTH = 2
IN_COLS = 7008
EPS = 1e-6
C_POOL, C_CQ, C_CKV, C_KR, C_DQ, C_DK, C_DV, C_SQ, C_SK, C_SV, C_G, C_M = (
    0, 256, 448, 576, 608, 864, 1120, 1376, 1632, 1760, 1888, 2912)
POOL_WINDOWS = (2, 4, 8, 16)
PPAD0, PPAD1, PPAD2 = 8, 16, 8


class Res:
    __slots__ = ("name", "w", "r", "dsem", "dcnt")

    def __init__(self, name):
        self.name = name
        self.w = None
        self.r = {}
        self.dsem = None
        self.dcnt = 0


class Sched:
    ENGS = ("pe", "act", "dve", "pool", "sp")
    EPOCH = 30000

    def __init__(self, nc, stack):
        self.nc = nc
        self.stack = stack
        self.ops = {e: [] for e in self.ENGS}
        self.cnt = {e: 0 for e in self.ENGS}
        self.sems = {e: [] for e in self.ENGS}
        self.known = {e: {} for e in self.ENGS}
        self.dma_res = []
        self.nsem = 0

    def _sem(self, name):
        self.nsem += 1
        return self.stack.enter_context(self.nc.semaphore(name))

    def _eng_sem(self, eng, idx):
        ep = idx // self.EPOCH
        while len(self.sems[eng]) <= ep:
            self.sems[eng].append(self._sem(f"p_{eng}_{len(self.sems[eng])}"))
        return self.sems[eng][ep], idx % self.EPOCH + 1

    def _collect(self, eng, reads, writes):
        deps = []
        for r in reads:
            if r.w is not None:
                deps.append((r.w, True))
        for w in writes:
            if w.w is not None:
                deps.append((w.w, False))
            for t in w.r.values():
                deps.append((t, False))
        need = {}
        for tok, raw in deps:
            if tok[0] == "e":
                _, f, idx = tok
                if f == eng:
                    if eng in ("pe", "sp") or not raw:
                        continue
                key = ("e", f)
                val = idx
            else:
                _, res, val = tok
                key = ("d", id(res), res)
            if self.known[eng].get(key[:2], -1) >= val:
                continue
            if key not in need or need[key] < val:
                need[key] = val
        waits = []
        for key, val in need.items():
            self.known[eng][key[:2]] = val
            if key[0] == "e":
                sem, v = self._eng_sem(key[1], val)
                waits.append((sem, v))
            else:
                waits.append((key[2].dsem, val))
        return waits

    def _update(self, tok, key, reads, writes):
        for w in writes:
            w.w = tok
            w.r = {}
        for r in reads:
            r.r[key] = tok

    def op(self, eng, fn, reads=(), writes=()):
        waits = self._collect(eng, reads, writes)
        idx = self.cnt[eng]
        self.cnt[eng] += 1
        tok = ("e", eng, idx)
        self._update(tok, ("e", eng), reads, writes)
        sem, _ = self._eng_sem(eng, idx)
        self.ops[eng].append((fn, waits, sem, 1))

    def dma(self, queue, out, in_, sres, reads=(), writes=()):
        waits = self._collect(queue, reads, writes)
        if sres.dsem is None:
            sres.dsem = self._sem(f"d_{sres.name}")
            self.dma_res.append(sres)
        sres.dcnt += 16
        tok = ("d", sres, sres.dcnt)
        self._update(tok, ("d", id(sres)), reads, writes)
        self.ops[queue].append((lambda e, o=out, i=in_: e.dma_start(out=o, in_=i), waits, sres.dsem, 16))

    def barrier(self):
        for e in self.ENGS:
            waits = []
            for f in self.ENGS:
                if f == e or self.cnt[f] == 0:
                    continue
                idx = self.cnt[f] - 1
                if self.known[e].get(("e", f), -1) >= idx:
                    continue
                self.known[e][("e", f)] = idx
                waits.append(self._eng_sem(f, idx))
            for res in self.dma_res:
                if self.known[e].get(("d", id(res)), -1) >= res.dcnt:
                    continue
                self.known[e][("d", id(res))] = res.dcnt
                waits.append((res.dsem, res.dcnt))
            if waits:
                self.ops[e].append((None, waits, None, 0))

    def replay(self, eng, e):
        for fn, waits, sem, inc in self.ops[eng]:
            for s, v in waits:
                e.wait_ge(s, v)
            if fn is not None:
                ins = fn(e)
                ins.then_inc(sem, inc)


class Ring:
    def __init__(self, items):
        self.items = items
        self.i = 0

    def next(self):
        it = self.items[self.i % len(self.items)]
        self.i += 1
        return it


def _rope_tab(S, rot_dim):
    t = np.arange(S)
    row = (t // 64).astype(np.float64)
    col = (t % 64).astype(np.float64)
    n_freq = rot_dim // 4
    inv = np.exp(-math.log(10000.0) * np.arange(n_freq, dtype=np.float64) / n_freq).astype(np.float32).astype(np.float64)
    ang = np.concatenate([row[:, None] * inv, col[:, None] * inv], axis=-1).astype(np.float32)
    c = np.cos(ang.astype(np.float64)).T
    s = np.sin(ang.astype(np.float64)).T
    C = np.concatenate([c, c], 0)
    Sg = np.concatenate([-s, s], 0)
    return C.astype(np.float32), Sg.astype(np.float32)


def make_consts(S):
    TT = CTX + S
    c32, s32 = _rope_tab(S, 32)
    c64, s64 = _rope_tab(S, 64)

    def full(C, Sg, rep):
        Cf = np.ones((C.shape[0] * rep, TT), np.float32)
        Sf = np.zeros((C.shape[0] * rep, TT), np.float32)
        Cf[:, CTX:] = np.tile(C, (rep, 1))
        Sf[:, CTX:] = np.tile(Sg, (rep, 1))
        return Cf, Sf

    t32c, t32s = full(c32, s32, 4)
    t64c, t64s = full(c64, s64, 2)
    tmc = np.ones((128, TT), np.float32)
    tms = np.zeros((128, TT), np.float32)
    tmc[64:96] = t32c[0:32]
    tms[64:96] = t32s[0:32]
    tab = np.stack([tmc, tms, t32c, t32s, t64c, t64s], 0)
    PT = PPAD0 + CTX + PPAD1 + S + PPAD2
    rc = np.zeros((256, PT), np.float32)
    for gi, w in enumerate(POOL_WINDOWS):
        for (off, L) in ((PPAD0, CTX), (PPAD0 + CTX + PPAD1, S)):
            t = np.arange(L)
            lo = np.clip(t - w // 2, 0, L)
            hi = np.clip(t - w // 2 + w, 0, L)
            rc[gi * 64:(gi + 1) * 64, off:off + L] = (1.0 / (hi - lo).astype(np.float64)).astype(np.float32)[None, :]
    ident = np.eye(128, dtype=np.float32).astype(ml_dtypes.bfloat16)
    a = np.arange(128)[None, :]
    b = np.arange(128)[:, None]
    NEG = -30000.0
    m_next = np.where(b <= a, 0.0, NEG)
    m_prev = np.where(a <= b, 0.0, NEG)
    mask3 = np.concatenate([m_next, np.zeros((128, 128)), m_prev], 1).astype(np.float32).astype(ml_dtypes.bfloat16)
    return dict(tab=tab, rcnt=rc, ident=ident, mask3=mask3)


def build_nc(S, l, phases, final):
    TT = CTX + S
    NKT = TT // 128
    NL = S // 128
    PT = PPAD0 + CTX + PPAD1 + S + PPAD2
    qtiles = [(0, CTX)] + [(CTX + i * 512, 512) for i in range(S // 512)]
    need_ctx = not final
    lam_init = 0.8 - 0.6 * math.exp(-0.3 * l)
    my_qtiles = qtiles if need_ctx else qtiles[1:]
    nc = bass.Bass("TRN2", target_bir_lowering=False)

    in_names = []

    def din(name, shape, dt=F32):
        in_names.append(name)
        return nc.dram_tensor(name, list(shape), dt, kind="ExternalInput").ap()

    def scr(name, shape, dt, producer, consumers):
        if producer in phases:
            return nc.dram_tensor(name, list(shape), dt, kind="ExternalOutput").ap()
        if any(c in phases for c in consumers):
            in_names.append(name)
            return nc.dram_tensor(name, list(shape), dt, kind="ExternalInput").ap()
        return None

    x_src = din("xT", [D, S])
    xc_src = din("ctxT", [D, CTX])
    c_in = din("c", [D])
    cc_in = din("c_ctx", [D])
    w_mod = din("w_mod", [DEPTH, D, 3 * D])
    b_mod = din("b_mod", [DEPTH, 3 * D])
    g_pre = din("g_pre", [DEPTH, D])
    g_post = din("g_post", [DEPTH, D])
    w_in = din("w_in", [DEPTH, D, IN_COLS])
    w_pool = din("w_pool", [DEPTH, 4, 64, 64])
    s_pool = din("s_pool", [DEPTH, 256])
    g_cq = din("g_cq", [DEPTH, 192])
    w_uq = din("w_uq", [DEPTH, 192, 384])
    g_ckv = din("g_ckv", [DEPTH, 128])
    w_uk = din("w_uk", [DEPTH, 128, 256])
    w_uv = din("w_uv", [DEPTH, 128, 256])
    lam_in = [din(n, [DEPTH, 32]) for n in ("lam_q1", "lam_k1", "lam_q2", "lam_k2")]
    g_diff = din("g_diff", [DEPTH, 64])
    sink = din("sink", [DEPTH, 4])
    w_br = din("w_br", [DEPTH, 4, 256, D])
    w_out = din("w_out", [DEPTH, D, D])
    tab = din("tab", [6, 128, TT])
    rcnt = din("rcnt", [256, PT])
    ident_in = din("ident", [128, 128], BF16)
    mask3_in = din("mask3", [128, 384], BF16)
    hT = scr("hT_s", [D, TT], BF16, "0", ("M", "D", "F1", "F2"))
    small_s = scr("small_s", [128, 256], F32, "0", ("M", "D", "F1", "F2"))
    ygm_s = scr("ygm_s", [256, TT], BF16, "M", ("F2",))
    ygd_s = scr("ygd_s", [256, TT], BF16, "D", ("F2",))
    ygs_s = scr("ygs_s", [256, TT], BF16, "F1", ("F2",))
    ygp_s = scr("ygp_s", [256, TT], BF16, "F1", ("F2",))
    x_dst = scr("yT", [D, S], F32, "F2", ())
    xc_dst = scr("ycT", [D, CTX], F32, "F2", ()) if need_ctx else None
    pool_s = nc.dram_tensor("pool_s", [256, PT], F32).ap() if "F1" in phases else None

    stack = ExitStack()
    S_ = Sched(nc, stack)
    stack.enter_context(nc.allow_non_contiguous_dma(reason="tiny per-partition vectors and strided weight views"))

    def sb(name, shape, dt):
        return stack.enter_context(nc.sbuf_tensor("s_" + name, list(shape), dt))

    def slots(name, n, shape, dt):
        return Ring([(sb(f"{name}{i}", shape, dt), Res(f"{name}{i}")) for i in range(n)])

    def has(*ps):
        return any(p in phases for p in ps)

    need_big = 0
    if has("M"):
        need_big = max(need_big, 2 * TT + NKT * 192)
    if has("D"):
        need_big = max(need_big, TT + NKT * 192)
    if has("F1"):
        need_big = max(need_big, TT + NKT * 192)
    if has("F2"):
        need_big = max(need_big, 8 * 4096)
    big = sb("big", [128, max(need_big, 64)], BF16)
    psum = stack.enter_context(nc.psum_tensor("psum", [128, 8, 512], F32))
    bank = [Res(f"bank{i}") for i in range(8)]
    grp = [Res("grp0"), Res("grp1")]
    hT_ring = slots("hTt", 2, [128, 8, 512], BF16)
    stage_ring = slots("stg", 1, [128, 8, 512], F32)
    ones_bf = sb("ones_bf", [128, 128], BF16)
    bones_bf = sb("bones_bf", [128, 128], BF16)
    const_res = Res("consts")
    small = sb("small", [128, 256], F32)
    small_res = Res("small")
    sq_ring = slots("sq", 2, [128, 512], BF16)
    rstd_ring = slots("rstd", 2, [128, 512], F32)
    f32_ring = slots("tf", 8, [128, 512], F32)
    bfx_ring = slots("tb", 4, [128, 512], BF16)
    wA = sb("wA", [128, 8, 1024], BF16)
    wA_res = Res("wA")
    wB = sb("wB", [128, 8, 1024], BF16)
    wB_res = Res("wB")
    if has("0"):
        modv = sb("modv", [128, 24, 2], F32)
        modv_res = Res("modv")
        csb = sb("csb", [128, 8, 2], F32)
        csb_res = Res("csb")
        stab = sb("stab", [128, 160], F32)
        stab_res = Res("stab")
    if has("M", "D", "F1"):
        pT_ring = slots("pT", 3, [128, 2, 512], BF16)
        tabs_ring = slots("tabs", 2, [128, 2, 512], F32)
        qT = [(sb(f"qT{i}", [128, 512], BF16), Res(f"qT{i}")) for i in range(2)]
        sg = sb("sg", [128, 4, 512], F32)
        sg_res = [Res(f"sg{i}") for i in range(4)]
        yraw = sb("yraw", [128, 2, 512], F32)
        yraw_res = [Res("yraw0"), Res("yraw1")]
    yg = sb("yg", [128, 8, 512], BF16)
    yg_res = [Res(f"yg{i}") for i in range(8)]
    if has("F1"):
        ident = sb("ident", [128, 128], BF16)
        mask3 = sb("mask3", [128, 384], BF16)
        pu = sb("pu", [128, 2, 528], F32)
        pu_res = Res("pu")
        prc = sb("prc", [128, 2, 512], F32)
        prc_res = Res("prc")
        pw = sb("pw", [128, 2, 2, 528], F32)
        pw_res = Res("pw")
        pl = sb("pl", [128, 2, 512], BF16)
        pl_res = [Res("pl0"), Res("pl1")]
        wpd = sb("wpd", [128, 2, 128], BF16)
        wpd_res = Res("wpd")
    if has("F2"):
        macc = sb("macc", [128, 8, 512], F32)
        macc_res = [Res(f"macc{i}") for i in range(8)]
        mbf = sb("mbf", [128, 8, 512], BF16)
        mbf_res = [Res(f"mbf{i}") for i in range(8)]
        wm_res = Res("wm")

    sp, pe, act, dve, pool = "sp", "pe", "act", "dve", "pool"

    def mm(out, lhsT, rhs, start, stop, reads, writes, tp=None):
        def fn(e, out=out, lhsT=lhsT, rhs=rhs, start=start, stop=stop, tp=tp):
            if tp is None:
                return e.matmul(out, lhsT, rhs, start=start, stop=stop)
            return e.matmul(out, lhsT, rhs, start=start, stop=stop, tile_position=tp)
        S_.op(pe, fn, reads, writes)

    def actf(out, in_, func, reads, writes, scale=1.0, bias=0.0):
        S_.op(act, lambda e, o=out, i=in_, f=func, s=scale, b=bias: e.activation(out=o, in_=i, func=f, bias=b, scale=s),
              reads, writes)

    def tt(eng, out, in0, in1, op, reads, writes):
        S_.op(eng, lambda e, o=out, a=in0, b=in1, p=op: e.tensor_tensor(out=o, in0=a, in1=b, op=p), reads, writes)

    def stt(eng, out, in0, scalar, in1, op0, op1, reads, writes):
        S_.op(eng, lambda e, o=out, a=in0, s=scalar, b=in1, p0=op0, p1=op1:
              e.scalar_tensor_tensor(out=o, in0=a, scalar=s, in1=b, op0=p0, op1=p1), reads, writes)

    def ts(eng, out, in0, s1, s2, op0, op1, reads, writes):
        if s2 is None:
            S_.op(eng, lambda e, o=out, a=in0, s=s1, p0=op0: e.tensor_scalar(out=o, in0=a, scalar1=s, scalar2=None, op0=p0),
                  reads, writes)
        else:
            S_.op(eng, lambda e, o=out, a=in0, s=s1, t=s2, p0=op0, p1=op1:
                  e.tensor_scalar(out=o, in0=a, scalar1=s, scalar2=t, op0=p0, op1=p1), reads, writes)

    def cp(eng, out, in_, reads, writes):
        if eng == act:
            S_.op(eng, lambda e, o=out, i=in_: e.copy(out=o, in_=i), reads, writes)
        else:
            S_.op(eng, lambda e, o=out, i=in_: e.tensor_copy(out=o, in_=i), reads, writes)

    def recip(out, in_, reads, writes):
        S_.op(dve, lambda e, o=out, i=in_: e.reciprocal(out=o, in_=i), reads, writes)

    def memset(eng, ap, val, writes):
        S_.op(eng, lambda e, a=ap, v=val: e.memset(a, v), (), writes)

    def ld(out, in_, res, reads=(), q=sp):
        S_.dma(q, out, in_, res, reads=reads, writes=(res,))

    def st(out, in_, res, q=pool):
        S_.dma(q, out, in_, res, reads=(res,), writes=())

    def rstd_from(ssum_ap, bank_res, n, eps, N, P=128):
        r_t, r_res = rstd_ring.next()
        actf(r_t[0:P, 0:N], ssum_ap, AF.Ln, (bank_res,), (r_res,), scale=1.0 / n, bias=eps)
        actf(r_t[0:P, 0:N], r_t[0:P, 0:N], AF.Exp, (r_res,), (r_res,), scale=-0.5)
        return r_t, r_res


    def finalize_div(acc, o0, N, add_ap=None):
        s0 = 64 - o0
        rs_t, rs_r = f32_ring.next()
        if add_ap is not None:
            ts(dve, rs_t[s0:s0 + 64, 0:N], psum[s0:s0 + 64, acc, 0:N], add_ap, None, ALU.add, None,
               (bank[acc], small_res), (rs_r,))
            recip(rs_t[s0:s0 + 64, 0:N], rs_t[s0:s0 + 64, 0:N], (rs_r,), (rs_r,))
        else:
            recip(rs_t[s0:s0 + 64, 0:N], psum[s0:s0 + 64, acc, 0:N], (bank[acc],), (rs_r,))
        r2_t, r2_r = f32_ring.next()
        S_.dma(pool, r2_t[o0:o0 + 64, 0:N], rs_t[s0:s0 + 64, 0:N], r2_r, reads=(rs_r,), writes=(r2_r,))
        y_t, y_r = f32_ring.next()
        tt(dve, y_t[o0:o0 + 64, 0:N], psum[o0:o0 + 64, acc, 0:N], r2_t[o0:o0 + 64, 0:N], ALU.mult,
           (bank[acc], r2_r), (y_r,))
        return y_t, y_r

    memset(dve, ones_bf[:, :], 1.0, (const_res,))
    memset(dve, bones_bf[:, :], 0.0, (const_res,))
    memset(dve, bones_bf[0:64, 0:64], 1.0, (const_res,))
    memset(dve, bones_bf[64:128, 64:128], 1.0, (const_res,))
    idr, mkr = Res("ident"), Res("mask3")
    if has("F1"):
        ld(ident[:, :], ident_in, idr)
        ld(mask3[:, :], mask3_in, mkr)

    hT_v = hT.rearrange("(k p) t -> p k t", p=128)
    av = small[:, 64:80].rearrange("p (k v) -> p k v", v=2)
    bv = small[:, 96:112].rearrange("p (k v) -> p k v", v=2)
    gv = small[:, 128:144].rearrange("p (k v) -> p k v", v=2)

    def load_hT(t0, N):
        t, r = hT_ring.next()
        ld(t[:, :, 0:N], hT_v[:, :, t0:t0 + N], r)
        return t, r

    def load_tab(idx, t0, N):
        t, r = tabs_ring.next()
        ld(t[:, :, 0:N], tab[idx:idx + 2, :, t0:t0 + N].rearrange("a p t -> p a t"), r)
        return t, r

    def load_w(dst, dres, src_cols_ap, ncols, kdim=8):
        s_t, s_r = stage_ring.next()
        ld(s_t[:, 0:kdim, 0:ncols], src_cols_ap, s_r)
        cp(pool, dst, s_t[:, 0:kdim, 0:ncols], (s_r,), (dres,))

    def win_cols(l, c0, n):
        return w_in[l, :, c0:c0 + n].rearrange("(k p) c -> p k c", p=128)

    def rope(z_ap, zsw_ap, zres, zswres, tabt, tabr, out_ap, out_res, P0, P1, N):
        t1, r1 = f32_ring.next()
        t2, r2 = f32_ring.next()
        tt(dve, t1[P0:P1, 0:N], zsw_ap, tabt[P0:P1, 1, 0:N], ALU.mult, (zswres, tabr), (r1,))
        tt(dve, t2[P0:P1, 0:N], z_ap, tabt[P0:P1, 0, 0:N], ALU.mult, (zres, tabr), (r2,))
        tt(pool, out_ap, t1[P0:P1, 0:N], t2[P0:P1, 0:N], ALU.add, (r1, r2), (out_res,))

    def proj(bank_i, w_t, w_res, c0, M, hT_t, hT_r, N, rows=None):
        for k in range(8):
            mm(psum[0:M, bank_i, 0:N], w_t[:, k, c0:c0 + M], hT_t[:, k, 0:N], k == 0, k == 7,
               (w_res, hT_r), (bank[bank_i],))

    def swap_halves(dst, src, res, ncols, half):
        nh = ncols // (2 * half)
        dv = dst.rearrange("p k (h j i) -> p k h j i", h=nh, j=2, i=half)
        sv = src.rearrange("p k (h j i) -> p k h j i", h=nh, j=2, i=half)
        for k in range(dst.shape[1]):
            cp(pool, dv[:, k, :, 0, :], sv[:, k, :, 1, :], (res,), (res,))
            cp(pool, dv[:, k, :, 1, :], sv[:, k, :, 0, :], (res,), (res,))

    def attn_core(kt_list, N, scale, acc_i, k_ap_fn, q_ap, q_res, v_ap_fn, tp=None, final=True):
        nk = len(kt_list)
        assert nk % 2 == 0
        for pi in range(nk // 2):
            g = attn_core.gi % 2
            attn_core.gi += 1
            b0 = 2 * g
            for j in range(2):
                kt = kt_list[2 * pi + j]
                mm(psum[:, b0 + j, 0:N], k_ap_fn(kt), q_ap, True, True, (q_res,), (grp[g],), tp=tp)
            p_t, p_r = pT_ring.next()
            actf(p_t[:, :, 0:N], psum[:, b0:b0 + 2, 0:N], AF.Exp, (grp[g],), (p_r,), scale=scale)
            for j in range(2):
                kt = kt_list[2 * pi + j]
                first = (pi == 0 and j == 0)
                last = final and (pi == nk // 2 - 1 and j == 1)
                mm(psum[:, acc_i, 0:N], v_ap_fn(kt), p_t[:, j, 0:N], first, last, (p_r,), (bank[acc_i],))
    attn_core.gi = 0


    xk_src = x_src.rearrange("(k p) t -> p k t", p=128)
    xck_src = xc_src.rearrange("(k p) t -> p k t", p=128)

    def x_tile_ap(t0, N):
        if t0 < CTX:
            return xck_src[:, :, 0:N]
        return xk_src[:, :, t0 - CTX:t0 - CTX + N]

    if has("0"):
        ld(small[:, 0:8], g_pre[l].rearrange("(k p) -> p k", p=128), small_res)
        ld(small[:, 8:16], g_post[l].rearrange("(k p) -> p k", p=128), small_res)
        ld(small[:, 16:40], b_mod[l].rearrange("(k p) -> p k", p=128), small_res)
        ld(small[:, 40:41], g_cq[l, 0:128].rearrange("(k p) -> p k", p=128), small_res)
        ld(small[0:64, 41:42], g_cq[l, 128:192].rearrange("(k p) -> p k", p=64), small_res)
        ld(small[:, 42:43], g_ckv[l].rearrange("(k p) -> p k", p=128), small_res)
        ld(small[0:64, 43:44], g_diff[l].rearrange("(k p) -> p k", p=64), small_res)
        ld(small[64:128, 43:44], g_diff[l].rearrange("(k p) -> p k", p=64), small_res)
        ld(small[:, 44:46], s_pool[l].rearrange("(k p) -> p k", p=128), small_res)
        ld(small[:, 48:52], sink[l].partition_broadcast(128), small_res)
        for i, la in enumerate(lam_in):
            ld(stab[:, i * 32:(i + 1) * 32], la[l].partition_broadcast(128), stab_res)
        ld(csb[:, :, 0], c_in.rearrange("(k p) -> p k", p=128), csb_res)
        ld(csb[:, :, 1], cc_in.rearrange("(k p) -> p k", p=128), csb_res)
        S_.barrier()
        ts(dve, small[:, 43:44], small[:, 43:44], 1.0 - lam_init, None, ALU.mult, None, (small_res,), (small_res,))
        actf(small[:, 48:52], small[:, 48:52], AF.Exp, (small_res,), (small_res,))
        tt(dve, stab[:, 128:160], stab[:, 0:32], stab[:, 32:64], ALU.mult, (stab_res,), (stab_res,))
        S_.op(dve, lambda e: e.reduce_sum(out=small[:, 53:54], in_=stab[:, 128:160], axis=mybir.AxisListType.X),
              (stab_res,), (small_res,))
        tt(dve, stab[:, 128:160], stab[:, 64:96], stab[:, 96:128], ALU.mult, (stab_res, small_res), (stab_res,))
        S_.op(dve, lambda e: e.reduce_sum(out=small[:, 54:55], in_=stab[:, 128:160], axis=mybir.AxisListType.X),
              (stab_res,), (small_res,))
        actf(small[:, 53:55], small[:, 53:55], AF.Exp, (small_res,), (small_res,))
        stt(dve, small[:, 52:53], small[:, 54:55], -lam_init, small[:, 53:54], ALU.add, ALU.subtract,
            (small_res,), (small_res,))
        t_t, t_r = f32_ring.next()
        tv = t_t[:, 0:16].rearrange("p (k v) -> p k v", v=2)
        actf(tv, csb[:, :, :], AF.Tanh, (csb_res,), (t_r,), scale=0.5)
        stt(dve, tv, tv, 1.0, csb[:, :, :], ALU.add, ALU.mult, (t_r, csb_res), (t_r,))
        ts(dve, csb[:, :, :], tv, 0.5, None, ALU.mult, None, (t_r,), (csb_res,))
        for j in range(24):
            s_t, s_r = stage_ring.next()
            ld(s_t[:, :, 0:128], w_mod[l, :, j * 128:(j + 1) * 128].rearrange("(k p) c -> p k c", p=128), s_r)
            bi = 6 + (j % 2)
            for k in range(8):
                mm(psum[:, bi, 0:2], s_t[:, k, 0:128], csb[:, k, :], k == 0, k == 7, (s_r, csb_res), (bank[bi],))
            ts(dve, modv[:, j, :], psum[:, bi, 0:2], small[:, 16 + j:17 + j], None, ALU.add, None,
               (bank[bi], small_res), (modv_res,))
        for k in range(8):
            ts(dve, av[:, k, :], modv[:, 8 + k, :], 1.0, small[:, k:k + 1], ALU.add, ALU.mult,
               (modv_res, small_res), (small_res,))
            ts(dve, gv[:, k, :], modv[:, 16 + k, :], small[:, 8 + k:9 + k], None, ALU.mult, None,
               (modv_res, small_res), (small_res,))
        cp(dve, bv, modv[:, 0:8, :], (modv_res, small_res), (small_res,))
        st(small_s, small[:, :], small_res)
        S_.barrier()
        for (t0, N) in qtiles:
            v = 1 if t0 < CTX else 0
            x_t, x_r = stage_ring.next()
            ld(x_t[:, :, 0:N], x_tile_ap(t0, N), x_r)
            h_t, h_r = hT_ring.next()
            for k in range(8):
                q_t, q_r = sq_ring.next()
                actf(q_t[:, 0:N], x_t[:, k, 0:N], AF.Square, (x_r,), (q_r,))
                mm(psum[:, 6, 0:N], ones_bf[:, :], q_t[:, 0:N], k == 0, k == 7, (q_r, const_res), (bank[6],))
            r_t, r_r = rstd_from(psum[:, 6, 0:N], bank[6], D, EPS, N)
            for k in range(8):
                f_t, f_r = f32_ring.next()
                stt(dve, f_t[:, 0:N], x_t[:, k, 0:N], av[:, k, v:v + 1], r_t[:, 0:N], ALU.mult, ALU.mult,
                    (x_r, r_r, small_res), (f_r,))
                actf(h_t[:, k, 0:N], f_t[:, 0:N], AF.Identity, (f_r, small_res), (h_r,), bias=bv[:, k, v:v + 1])
            st(hT_v[:, :, t0:t0 + N], h_t[:, :, 0:N], h_r)
        S_.barrier()
    else:
        ld(small[:, :], small_s, small_res)
        S_.barrier()

    def gates_chunk(w_t, w_res, c0, sgi, bi, h_t, h_r, N):
        proj(bi, w_t, w_res, c0, 128, h_t, h_r, N)
        th_t, th_r = f32_ring.next()
        actf(th_t[:, 0:N], psum[:, bi, 0:N], AF.Tanh, (bank[bi],), (th_r,), scale=0.5)
        stt(dve, sg[:, sgi, 0:N], th_t[:, 0:N], 1.0, psum[:, bi, 0:N], ALU.add, ALU.mult,
            (th_r, bank[bi]), (sg_res[sgi],))

    if has("M"):
        ygm_v = ygm_s.rearrange("(c p) t -> p c t", p=128)
        KTm = [big[:, i * TT:(i + 1) * TT] for i in range(2)]
        VB = 2 * TT

        def vm_ap(kt, hh):
            base = VB + kt * 192 + hh * 64
            return big[:, base:base + 128]

        Vall = big[:, VB:VB + NKT * 192].rearrange("p (t s d) -> p t s d", t=NKT, s=3, d=64)
        load_w(wA[:, :, 0:128], wA_res, win_cols(l, C_CKV, 128), 128)
        memset(pool, wA[:, :, 128:320], 0.0, (wA_res,))
        load_w(wA[:, :, 192:224], wA_res, win_cols(l, C_KR, 32), 32)
        cp(pool, wA[:, :, 288:304], wA[:, :, 208:224], (wA_res,), (wA_res,))
        cp(pool, wA[:, :, 304:320], wA[:, :, 192:208], (wA_res,), (wA_res,))
        load_w(wA[:, :, 320:512], wA_res, win_cols(l, C_CQ, 192), 192)
        load_w(wA[:, :, 512:768], wA_res, win_cols(l, C_G + 256, 256), 256)
        s_t, s_r = stage_ring.next()
        memset(pool, s_t[:, 0:3, :], 0.0, (s_r,))
        ld(s_t[:, 0, 0:256], w_uk[l], s_r)
        ld(s_t[:, 0, 256:512], w_uv[l], s_r)
        ld(s_t[:, 1, 0:384], w_uq[l, 0:128, :], s_r)
        ld(s_t[0:64, 2, 0:384], w_uq[l, 128:192, :], s_r)
        cp(pool, wB[:, 0:3, 0:512], s_t[:, 0:3, 0:512], (s_r,), (wB_res,))
        cp(pool, wB[:, 3:5, 0:384], wB[:, 1:3, 0:384], (wB_res,), (wB_res,))
        for kk in range(2):
            sv_ = wB[:, 1 + kk, 0:384].rearrange("p (h c) -> p h c", c=96)
            dv_ = wB[:, 3 + kk, 0:384].rearrange("p (h c) -> p h c", c=96)
            cp(pool, dv_[:, :, 64:80], sv_[:, :, 80:96], (wB_res,), (wB_res,))
            cp(pool, dv_[:, :, 80:96], sv_[:, :, 64:80], (wB_res,), (wB_res,))
        for pr in range(2):
            memset(pool, Vall[:, :, 1, :], 1.0, ())
            for (t0, N) in qtiles:
                h_t, h_r = load_hT(t0, N)
                tb_t, tb_r = load_tab(0, t0, N)
                proj(6, wA, wA_res, 0, 128, h_t, h_r, N)
                q_t, q_r = sq_ring.next()
                actf(q_t[:, 0:N], psum[:, 6, 0:N], AF.Square, (bank[6],), (q_r,))
                mm(psum[:, 7, 0:N], ones_bf[:, :], q_t[:, 0:N], True, True, (q_r, const_res), (bank[7],))
                r_t, r_r = rstd_from(psum[:, 7, 0:N], bank[7], 128, EPS, N)
                cn_t, cn_r = bfx_ring.next()
                stt(dve, cn_t[:, 0:N], psum[:, 6, 0:N], small[:, 42:43], r_t[:, 0:N], ALU.mult, ALU.mult,
                    (bank[6], r_r, small_res), (cn_r,))
                proj(6, wA, wA_res, 128, 96, h_t, h_r, N)
                proj(7, wA, wA_res, 224, 96, h_t, h_r, N)
                kr_t, kr_r = bfx_ring.next()
                rope(psum[64:96, 6, 0:N], psum[64:96, 7, 0:N], bank[6], bank[7], tb_t, tb_r,
                     kr_t[64:96, 0:N], kr_r, 64, 96, N)
                for hh in range(2):
                    h = 2 * pr + hh
                    cp(pool, KTm[hh][64:96, t0:t0 + N], kr_t[64:96, 0:N], (kr_r,), ())
                    bi = 4 + hh
                    mm(psum[0:64, bi, 0:N], wB[:, 0, h * 64:(h + 1) * 64], cn_t[:, 0:N], True, True,
                       (wB_res, cn_r), (bank[bi],))
                    cp(act, KTm[hh][0:64, t0:t0 + N], psum[0:64, bi, 0:N], (bank[bi],), ())
                for s in range(N // 128):
                    bi = 6 + (s % 2)
                    kt = t0 // 128 + s
                    mm(psum[:, bi, 0:128], cn_t[:, s * 128:(s + 1) * 128], wB[:, 0, 256 + pr * 128:256 + (pr + 1) * 128],
                       True, True, (cn_r, wB_res), (bank[bi],))
                    cp(dve, Vall[:, kt, 0:3:2, :], psum[:, bi, 0:128].rearrange("p (s d) -> p s d", s=2),
                       (bank[bi],), ())
            S_.barrier()
            for (t0, N) in my_qtiles:
                kts = list(range(2)) if t0 < CTX else list(range(NKT))
                h_t, h_r = load_hT(t0, N)
                tb_t, tb_r = load_tab(0, t0, N)
                proj(6, wA, wA_res, 320, 128, h_t, h_r, N)
                proj(7, wA, wA_res, 448, 64, h_t, h_r, N)
                qa_t, qa_r = sq_ring.next()
                qb_t, qb_r = sq_ring.next()
                actf(qa_t[:, 0:N], psum[:, 6, 0:N], AF.Square, (bank[6],), (qa_r,))
                actf(qb_t[0:64, 0:N], psum[0:64, 7, 0:N], AF.Square, (bank[7],), (qb_r,))
                mm(psum[:, 5, 0:N], ones_bf[:, :], qa_t[:, 0:N], True, False, (qa_r, const_res), (bank[5],))
                mm(psum[:, 5, 0:N], ones_bf[0:64, :], qb_t[0:64, 0:N], False, True, (qb_r, const_res), (bank[5],))
                r_t, r_r = rstd_from(psum[:, 5, 0:N], bank[5], 192, EPS, N)
                ca_t, ca_r = bfx_ring.next()
                cb_t, cb_r = bfx_ring.next()
                stt(dve, ca_t[:, 0:N], psum[:, 6, 0:N], small[:, 40:41], r_t[:, 0:N], ALU.mult, ALU.mult,
                    (bank[6], r_r, small_res), (ca_r,))
                stt(dve, cb_t[0:64, 0:N], psum[0:64, 7, 0:N], small[0:64, 41:42], r_t[0:64, 0:N], ALU.mult, ALU.mult,
                    (bank[7], r_r, small_res), (cb_r,))
                gates_chunk(wA, wA_res, 512 + pr * 128, pr, 6, h_t, h_r, N)
                for hh in range(2):
                    h = 2 * pr + hh
                    for (bi, kb) in ((6, 1), (7, 3)):
                        mm(psum[0:96, bi, 0:N], wB[:, kb, h * 96:(h + 1) * 96], ca_t[:, 0:N], True, False,
                           (wB_res, ca_r), (bank[bi],))
                        mm(psum[0:96, bi, 0:N], wB[0:64, kb + 1, h * 96:(h + 1) * 96], cb_t[0:64, 0:N], False, True,
                           (wB_res, cb_r), (bank[bi],))
                    q_t, q_r = qT[hh]
                    rope(psum[0:96, 6, 0:N], psum[0:96, 7, 0:N], bank[6], bank[7], tb_t, tb_r, q_t[0:96, 0:N], q_r,
                         0, 96, N)
                    acc = 4 + hh
                    attn_core(kts, N, 96 ** -0.5, acc,
                              lambda kt, hh=hh: KTm[hh][0:96, kt * 128:(kt + 1) * 128],
                              q_t[0:96, 0:N], q_r, lambda kt, hh=hh: vm_ap(kt, hh))
                    o0 = hh * 64
                    y_t, y_r = finalize_div(acc, o0, N)
                    tt(pool, yg[o0:o0 + 64, 2 + pr, 0:N], y_t[o0:o0 + 64, 0:N], sg[o0:o0 + 64, pr, 0:N], ALU.mult,
                       (y_r, sg_res[pr]), (yg_res[2 + pr],))
                st(ygm_v[:, pr, t0:t0 + N], yg[:, 2 + pr, 0:N], yg_res[2 + pr])
            S_.barrier()

    if has("D"):
        ygd_v = ygd_s.rearrange("(c p) t -> p c t", p=128)
        KTd = big[:, 0:TT]
        VBd = TT

        def vd_ap(kt, hh):
            base = VBd + kt * 192 + hh * 64
            return big[:, base:base + 128]

        Vd = big[:, VBd:VBd + NKT * 192].rearrange("p (t s d) -> p t s d", t=NKT, s=3, d=64)
        load_w(wA[:, :, 0:256], wA_res, win_cols(l, C_DK, 256), 256)
        swap_halves(wA[:, :, 256:512], wA[:, :, 0:256], wA_res, 256, 16)
        load_w(wA[:, :, 512:768], wA_res, win_cols(l, C_DV, 256), 256)
        load_w(wB[:, :, 0:256], wB_res, win_cols(l, C_DQ, 256), 256)
        swap_halves(wB[:, :, 256:512], wB[:, :, 0:256], wB_res, 256, 16)
        load_w(wB[:, :, 512:768], wB_res, win_cols(l, C_G + 512, 256), 256)
        for pr in range(2):
            memset(pool, Vd[:, :, 1, :], 1.0, ())
            for (t0, N) in qtiles:
                h_t, h_r = load_hT(t0, N)
                tb_t, tb_r = load_tab(2, t0, N)
                proj(6, wA, wA_res, pr * 128, 128, h_t, h_r, N)
                proj(7, wA, wA_res, 256 + pr * 128, 128, h_t, h_r, N)
                rope(psum[:, 6, 0:N], psum[:, 7, 0:N], bank[6], bank[7], tb_t, tb_r, KTd[:, t0:t0 + N], Res("kd"),
                     0, 128, N)
                for s in range(N // 128):
                    bi = 4 + (s % 2)
                    kt = t0 // 128 + s
                    for k in range(8):
                        mm(psum[:, bi, 0:128], h_t[:, k, s * 128:(s + 1) * 128],
                           wA[:, k, 512 + pr * 128:512 + (pr + 1) * 128], k == 0, k == 7, (h_r, wA_res), (bank[bi],))
                    cp(dve, Vd[:, kt, 0:3:2, :], psum[:, bi, 0:128].rearrange("p (s d) -> p s d", s=2),
                       (bank[bi],), ())
            S_.barrier()
            for (t0, N) in my_qtiles:
                kts = list(range(2)) if t0 < CTX else list(range(NKT))
                h_t, h_r = load_hT(t0, N)
                tb_t, tb_r = load_tab(2, t0, N)
                proj(6, wB, wB_res, pr * 128, 128, h_t, h_r, N)
                proj(7, wB, wB_res, 256 + pr * 128, 128, h_t, h_r, N)
                q_t, q_r = qT[0]
                rope(psum[:, 6, 0:N], psum[:, 7, 0:N], bank[6], bank[7], tb_t, tb_r, q_t[:, 0:N], q_r, 0, 128, N)
                gates_chunk(wB, wB_res, 512 + pr * 128, pr, 6, h_t, h_r, N)
                for hh in range(2):
                    o0 = hh * 64
                    ys = []
                    for m in range(2):
                        i = 2 * hh + m
                        acc = 4 + m
                        attn_core(kts, N, 32 ** -0.5, acc,
                                  lambda kt, i=i: KTd[32 * i:32 * i + 32, kt * 128:(kt + 1) * 128],
                                  q_t[32 * i:32 * i + 32, 0:N], q_r, lambda kt, hh=hh: vd_ap(kt, hh), tp=(32 * i, 0))
                        ys.append(finalize_div(acc, o0, N))
                    (y1, r1), (y2, r2) = ys
                    stt(dve, yraw[o0:o0 + 64, 0, 0:N], y2[o0:o0 + 64, 0:N], small[o0:o0 + 64, 52:53],
                        y1[o0:o0 + 64, 0:N], ALU.mult, ALU.add, (r1, r2, small_res), (yraw_res[0],))
                s_t2, s_r2 = sq_ring.next()
                actf(s_t2[:, 0:N], yraw[:, 0, 0:N], AF.Square, (yraw_res[0],), (s_r2,))
                mm(psum[:, 6, 0:N], bones_bf[:, :], s_t2[:, 0:N], True, True, (s_r2, const_res), (bank[6],))
                r_t, r_r = rstd_from(psum[:, 6, 0:N], bank[6], 64, EPS, N)
                f_t, f_r = f32_ring.next()
                stt(dve, f_t[:, 0:N], yraw[:, 0, 0:N], small[:, 43:44], r_t[:, 0:N], ALU.mult, ALU.mult,
                    (yraw_res[0], r_r, small_res), (f_r,))
                tt(pool, yg[:, 4 + pr, 0:N], f_t[:, 0:N], sg[:, pr, 0:N], ALU.mult, (f_r, sg_res[pr]), (yg_res[4 + pr],))
                st(ygd_v[:, pr, t0:t0 + N], yg[:, 4 + pr, 0:N], yg_res[4 + pr])
            S_.barrier()

    if has("F1"):
        ygs_v = ygs_s.rearrange("(c p) t -> p c t", p=128)
        ygp_v = ygp_s.rearrange("(c p) t -> p c t", p=128)
        pool_v = pool_s.rearrange("(c p) t -> p c t", p=128)
        rcnt_v = rcnt.rearrange("(c p) t -> p c t", p=128)
        KTs = big[:, 0:TT]
        VBs = TT

        def vs_ap(kt, kv):
            base = VBs + kt * 192 + kv * 64
            return big[:, base:base + 128]

        Vs = big[:, VBs:VBs + NKT * 192].rearrange("p (t s d) -> p t s d", t=NKT, s=3, d=64)
        memset(pool, Vs[:, :, 1, :], 1.0, ())
        memset(pool, pw[:, :, 0, 0:16], 0.0, (pw_res,))
        st(pool_v[:, :, 0:PPAD0], pw[:, :, 0, 0:PPAD0], pw_res)
        st(pool_v[:, :, PPAD0 + CTX:PPAD0 + CTX + PPAD1], pw[:, :, 0, 0:PPAD1], pw_res)
        st(pool_v[:, :, PT - PPAD2:PT], pw[:, :, 0, 0:PPAD2], pw_res)
        load_w(wA[:, :, 0:128], wA_res, win_cols(l, C_SK, 128), 128)
        swap_halves(wA[:, :, 128:256], wA[:, :, 0:128], wA_res, 128, 32)
        load_w(wA[:, :, 256:384], wA_res, win_cols(l, C_SV, 128), 128)
        load_w(wA[:, :, 384:640], wA_res, win_cols(l, C_POOL, 256), 256)
        for g in range(2):
            for kv in range(2):
                load_w(wB[:, :, g * 128 + kv * 64:g * 128 + kv * 64 + 64], wB_res,
                       win_cols(l, C_SQ + (kv * 2 + g) * 64, 64), 64)
                load_w(wB[:, :, 512 + g * 128 + kv * 64:512 + g * 128 + kv * 64 + 64], wB_res,
                       win_cols(l, C_G + 768 + (kv * 2 + g) * 64, 64), 64)
        swap_halves(wB[:, :, 256:512], wB[:, :, 0:256], wB_res, 256, 32)
        load_w(wB[:, :, 768:1024], wB_res, win_cols(l, C_G, 256), 256)
        s_t, s_r = stage_ring.next()
        memset(pool, s_t[:, 0:2, 0:128], 0.0, (s_r,))
        for g4 in range(4):
            r0 = (g4 % 2) * 64
            ld(s_t[r0:r0 + 64, g4 // 2, r0:r0 + 64], w_pool[l, g4], s_r)
        cp(pool, wpd[:, :, :], s_t[:, 0:2, 0:128], (s_r,), (wpd_res,))
        for (t0, N) in qtiles:
            h_t, h_r = load_hT(t0, N)
            tb_t, tb_r = load_tab(4, t0, N)
            proj(6, wA, wA_res, 0, 128, h_t, h_r, N)
            proj(7, wA, wA_res, 128, 128, h_t, h_r, N)
            rope(psum[:, 6, 0:N], psum[:, 7, 0:N], bank[6], bank[7], tb_t, tb_r, KTs[:, t0:t0 + N], Res("ks"), 0, 128, N)
            for s in range(N // 128):
                bi = 4 + (s % 2)
                kt = t0 // 128 + s
                for k in range(8):
                    mm(psum[:, bi, 0:128], h_t[:, k, s * 128:(s + 1) * 128], wA[:, k, 256:384], k == 0, k == 7,
                       (h_r, wA_res), (bank[bi],))
                cp(dve, Vs[:, kt, 0:3:2, :], psum[:, bi, 0:128].rearrange("p (s d) -> p s d", s=2), (bank[bi],), ())
            for c_ in range(2):
                proj(6 + c_, wA, wA_res, 384 + c_ * 128, 128, h_t, h_r, N)
                cp(act, pu[:, c_, 0:N], psum[:, 6 + c_, 0:N], (bank[6 + c_],), (pu_res,))
            po = t0 + (PPAD0 if t0 < CTX else PPAD0 + PPAD1)
            st(pool_v[:, :, po:po + N], pu[:, :, 0:N], pu_res)
        S_.barrier()
        for (t0, N) in my_qtiles:
            is_ctx = t0 < CTX
            h_t, h_r = load_hT(t0, N)
            tb_t, tb_r = load_tab(4, t0, N)
            po = t0 + (PPAD0 if is_ctx else PPAD0 + PPAD1)
            ld(pu[:, :, 0:N + 16], pool_v[:, :, po - 8:po + N + 8], pu_res)
            ld(prc[:, :, 0:N], rcnt_v[:, :, po:po + N], prc_res)
            for g in range(2):
                proj(6, wB, wB_res, g * 128, 128, h_t, h_r, N)
                proj(7, wB, wB_res, 256 + g * 128, 128, h_t, h_r, N)
                rope(psum[:, 6, 0:N], psum[:, 7, 0:N], bank[6], bank[7], tb_t, tb_r, qT[g][0][:, 0:N], qT[g][1], 0, 128, N)
            for i4 in range(4):
                gates_chunk(wB, wB_res, 512 + i4 * 128, i4, 6 + (i4 % 2), h_t, h_r, N)
            for g in range(2):
                q_t, q_r = qT[g]
                for kv in range(2):
                    acc = 4 + kv
                    o0 = kv * 64
                    s0 = 64 - o0
                    hq = kv * 2 + g
                    wins = []
                    if not is_ctx:
                        qb0 = (t0 - CTX) // 128
                        for ktl in range(max(qb0 - 1, 0), min(qb0 + 4, NL - 1) + 1):
                            qlo, qhi = max(ktl - 1, qb0), min(ktl + 1, qb0 + 3)
                            wins.append((ktl, (qlo - qb0) * 128, (qhi - qb0 + 1) * 128, (qlo - ktl + 1) * 128))
                    attn_core([0, 1], N, 0.125, acc,
                              lambda kt, kv=kv: KTs[kv * 64:kv * 64 + 64, kt * 128:(kt + 1) * 128],
                              q_t[o0:o0 + 64, 0:N], q_r, lambda kt, kv=kv: vs_ap(kt, kv), final=(len(wins) == 0))
                    for wi, (ktl, c0, c1, m0) in enumerate(wins):
                        gq = attn_core.gi % 2
                        attn_core.gi += 1
                        b0 = 2 * gq
                        kt = 2 + ktl
                        mm(psum[:, b0, c0:c1], KTs[o0:o0 + 64, kt * 128:(kt + 1) * 128], q_t[o0:o0 + 64, c0:c1],
                           True, False, (q_r,), (grp[gq],))
                        mm(psum[:, b0, c0:c1], ident[:, :], mask3[:, m0:m0 + (c1 - c0)], False, True, (idr, mkr),
                           (grp[gq],))
                        p_t, p_r = pT_ring.next()
                        actf(p_t[:, 0, c0:c1], psum[:, b0, c0:c1], AF.Exp, (grp[gq],), (p_r,), scale=0.125)
                        mm(psum[:, acc, c0:c1], vs_ap(kt, kv), p_t[:, 0, c0:c1], False, wi == len(wins) - 1,
                           (p_r,), (bank[acc],))
                    y_t, y_r = finalize_div(acc, o0, N, add_ap=small[s0:s0 + 64, 48 + hq:49 + hq])
                    tt(pool, yg[o0:o0 + 64, 6 + g, 0:N], y_t[o0:o0 + 64, 0:N], sg[o0:o0 + 64, g, 0:N], ALU.mult,
                       (y_r, sg_res[g]), (yg_res[6 + g],))
            for g in range(2):
                st(ygs_v[:, g, t0:t0 + N], yg[:, 6 + g, 0:N], yg_res[6 + g])
            M_ = N + 16
            for c_ in range(2):
                for half in range(2):
                    gi4 = 2 * c_ + half
                    nlev = gi4 + 1
                    R0, R1 = half * 64, half * 64 + 64
                    tt(pool, pw[R0:R1, c_, 1, 1:M_], pu[R0:R1, c_, 1:M_], pu[R0:R1, c_, 0:M_ - 1], ALU.add,
                       (pu_res,), (pw_res,))
                    for lev in range(2, nlev + 1):
                        lo = (1 << lev) - 1
                        sh = 1 << (lev - 1)
                        src = pw[R0:R1, c_, (lev - 1) % 2, :]
                        tt(pool, pw[R0:R1, c_, lev % 2, lo:M_], src[:, lo:M_], src[:, lo - sh:M_ - sh], ALU.add,
                           (pw_res,), (pw_res,))
                    off = 8 + (1 << (nlev - 1)) - 1
                    f_t, f_r = f32_ring.next()
                    tt(pool, f_t[R0:R1, 0:N], pw[R0:R1, c_, nlev % 2, off:off + N], prc[R0:R1, c_, 0:N], ALU.mult,
                       (pw_res, prc_res), (f_r,))
                    tt(pool, pl[R0:R1, c_, 0:N], f_t[R0:R1, 0:N], pu[R0:R1, c_, 8:8 + N], ALU.subtract,
                       (f_r, pu_res), (pl_res[c_],))
                bi = 6 + c_
                mm(psum[:, bi, 0:N], wpd[:, c_, :], pl[:, c_, 0:N], True, True, (wpd_res, pl_res[c_]), (bank[bi],))
                stt(dve, yg[:, c_, 0:N], psum[:, bi, 0:N], small[:, 44 + c_:45 + c_], sg[:, 2 + c_, 0:N],
                    ALU.mult, ALU.mult, (bank[bi], small_res, sg_res[2 + c_]), (yg_res[c_],))
                st(ygp_v[:, c_, t0:t0 + N], yg[:, c_, 0:N], yg_res[c_])
        S_.barrier()

    if has("F2"):
        ysrc = {0: ygp_s, 1: ygm_s, 2: ygd_s, 3: ygs_s}
        yv = {r: ysrc[r].rearrange("(c p) t -> p c t", p=128) for r in range(4)}
        wm = big[:, 0:8 * 4096].rearrange("p (k c) -> p k c", k=8)
        for blk in range(8):
            load_w(wm[:, :, blk * 512:(blk + 1) * 512], wm_res, win_cols(l, C_M + blk * 512, 512), 512)
        for ch in range(2):
            s_t, s_r = stage_ring.next()
            for r in range(3):
                ld(s_t[:, 2 * r:2 * r + 2, :], w_br[l, r, :, ch * 512:(ch + 1) * 512].rearrange("(kc p) c -> p kc c", p=128),
                   s_r)
            for g in range(2):
                for kv in range(2):
                    r0 = (kv * 2 + g) * 64
                    ld(s_t[kv * 64:kv * 64 + 64, 6 + g, :], w_br[l, 3, r0:r0 + 64, ch * 512:(ch + 1) * 512], s_r)
            cp(pool, wA[:, :, ch * 512:(ch + 1) * 512], s_t[:, :, :], (s_r,), (wA_res,))
            s_t, s_r = stage_ring.next()
            ld(s_t[:, :, :], w_out[l, :, ch * 512:(ch + 1) * 512].rearrange("(k p) c -> p k c", p=128), s_r)
            cp(pool, wB[:, :, ch * 512:(ch + 1) * 512], s_t[:, :, :], (s_r,), (wB_res,))
        for (t0, N) in my_qtiles:
            is_ctx = t0 < CTX
            v = 1 if is_ctx else 0
            h_t, h_r = load_hT(t0, N)
            for r in range(4):
                for kc in range(2):
                    ld(yg[:, 2 * r + kc, 0:N], yv[r][:, kc, t0:t0 + N], yg_res[2 * r + kc])
            for j in range(8):
                for r in range(4):
                    bz = 4 + (r % 2)
                    bb = 6 + (r % 2)
                    proj(bz, wm, wm_res, r * 1024 + j * 128, 128, h_t, h_r, N)
                    tm_t, tm_r = bfx_ring.next()
                    actf(tm_t[:, 0:N], psum[:, bz, 0:N], AF.Tanh, (bank[bz],), (tm_r,), scale=0.5)
                    for kc in range(2):
                        mm(psum[:, bb, 0:N], wA[:, 2 * r + kc, j * 128:(j + 1) * 128], yg[:, 2 * r + kc, 0:N],
                           kc == 0, kc == 1, (wA_res, yg_res[2 * r + kc]), (bank[bb],))
                    if r == 0:
                        stt(dve, macc[:, j, 0:N], tm_t[:, 0:N], 1.0, psum[:, bb, 0:N], ALU.add, ALU.mult,
                            (tm_r, bank[bb]), (macc_res[j],))
                    else:
                        f_t, f_r = f32_ring.next()
                        stt(dve, f_t[:, 0:N], tm_t[:, 0:N], 1.0, psum[:, bb, 0:N], ALU.add, ALU.mult,
                            (tm_r, bank[bb]), (f_r,))
                        if r < 3:
                            tt(pool, macc[:, j, 0:N], macc[:, j, 0:N], f_t[:, 0:N], ALU.add, (macc_res[j], f_r),
                               (macc_res[j],))
                        else:
                            tt(pool, mbf[:, j, 0:N], macc[:, j, 0:N], f_t[:, 0:N], ALU.add, (macc_res[j], f_r),
                               (mbf_res[j],))
            for j in range(8):
                bo = 4 + (j % 2)
                for k in range(8):
                    mm(psum[:, bo, 0:N], wB[:, k, j * 128:(j + 1) * 128], mbf[:, k, 0:N], k == 0, k == 7,
                       (wB_res, mbf_res[k]), (bank[bo],))
                cp(act, macc[:, j, 0:N], psum[:, bo, 0:N], (bank[bo],), (macc_res[j],))
                q_t, q_r = sq_ring.next()
                actf(q_t[:, 0:N], psum[:, bo, 0:N], AF.Square, (bank[bo],), (q_r,))
                mm(psum[:, 6, 0:N], ones_bf[:, :], q_t[:, 0:N], j == 0, j == 7, (q_r, const_res), (bank[6],))
            r_t, r_r = rstd_from(psum[:, 6, 0:N], bank[6], D, 16 * EPS, N)
            dst_v = (xc_dst if is_ctx else x_dst).rearrange("(k p) t -> p k t", p=128)
            tq = 0 if is_ctx else t0 - CTX
            for j in range(8):
                xs_t, xs_r = f32_ring.next()
                ld(xs_t[:, 0:N], x_tile_ap(t0, N)[:, j, :], xs_r)
                f_t, f_r = f32_ring.next()
                stt(dve, f_t[:, 0:N], macc[:, j, 0:N], gv[:, j, v:v + 1], r_t[:, 0:N], ALU.mult, ALU.mult,
                    (macc_res[j], r_r, small_res), (f_r,))
                tt(pool, xs_t[:, 0:N], f_t[:, 0:N], xs_t[:, 0:N], ALU.add, (f_r, xs_r), (xs_r,))
                st(dst_v[:, j, tq:tq + N], xs_t[:, 0:N], xs_r)
        S_.barrier()

    S_.barrier()
    with nc.Block() as block:
        @block.sync
        def _(e):
            S_.replay("sp", e)

        @block.tensor
        def _(e):
            S_.replay("pe", e)

        @block.scalar
        def _(e):
            S_.replay("act", e)

        @block.vector
        def _(e):
            S_.replay("dve", e)

        @block.gpsimd
        def _(e):
            S_.replay("pool", e)
    stack.close()
    nc.in_names_ = in_names
    nc.sched_ = S_
    return nc


def build_fused(S):
    phases = {"0", "M", "D", "F1", "F2"}
    TT = CTX + S
    NKT = TT // 128
    NL = S // 128
    PT = PPAD0 + CTX + PPAD1 + S + PPAD2
    qtiles = [(0, CTX)] + [(CTX + i * 512, 512) for i in range(S // 512)]
    nc = bass.Bass("TRN2", target_bir_lowering=False)

    in_names = []

    def din(name, shape, dt=F32):
        in_names.append(name)
        return nc.dram_tensor(name, list(shape), dt, kind="ExternalInput").ap()

    def scr(name, shape, dt, producer, consumers):
        return nc.dram_tensor(name, list(shape), dt).ap()

    x_in = din("xT", [D, S])
    xc_in = din("ctxT", [D, CTX])
    c_in = din("c", [D])
    cc_in = din("c_ctx", [D])
    w_mod = din("w_mod", [DEPTH, D, 3 * D])
    b_mod = din("b_mod", [DEPTH, 3 * D])
    g_pre = din("g_pre", [DEPTH, D])
    g_post = din("g_post", [DEPTH, D])
    w_in = din("w_in", [DEPTH, D, IN_COLS])
    w_pool = din("w_pool", [DEPTH, 4, 64, 64])
    s_pool = din("s_pool", [DEPTH, 256])
    g_cq = din("g_cq", [DEPTH, 192])
    w_uq = din("w_uq", [DEPTH, 192, 384])
    g_ckv = din("g_ckv", [DEPTH, 128])
    w_uk = din("w_uk", [DEPTH, 128, 256])
    w_uv = din("w_uv", [DEPTH, 128, 256])
    lam_in = [din(n, [DEPTH, 32]) for n in ("lam_q1", "lam_k1", "lam_q2", "lam_k2")]
    g_diff = din("g_diff", [DEPTH, 64])
    sink = din("sink", [DEPTH, 4])
    w_br = din("w_br", [DEPTH, 4, 256, D])
    w_out = din("w_out", [DEPTH, D, D])
    tab = din("tab", [6, 128, TT])
    rcnt = din("rcnt", [256, PT])
    ident_in = din("ident", [128, 128], BF16)
    mask3_in = din("mask3", [128, 384], BF16)
    hT = scr("hT_s", [D, TT], BF16, "0", ("M", "D", "F1", "F2"))
    small_s = scr("small_s", [128, 256], F32, "0", ("M", "D", "F1", "F2"))
    ygm_s = scr("ygm_s", [256, TT], BF16, "M", ("F2",))
    ygd_s = scr("ygd_s", [256, TT], BF16, "D", ("F2",))
    ygs_s = scr("ygs_s", [256, TT], BF16, "F1", ("F2",))
    ygp_s = scr("ygp_s", [256, TT], BF16, "F1", ("F2",))
    y_out = nc.dram_tensor("yT", [D, S], F32, kind="ExternalOutput").ap()
    x1T = nc.dram_tensor("x1T_s", [D, S], F32).ap()
    xc1T = nc.dram_tensor("xc1T_s", [D, CTX], F32).ap()
    pool_s = nc.dram_tensor("pool_s", [256, PT], F32).ap()

    stack = ExitStack()
    S_ = Sched(nc, stack)
    stack.enter_context(nc.allow_non_contiguous_dma(reason="tiny per-partition vectors and strided weight views"))

    def sb(name, shape, dt):
        return stack.enter_context(nc.sbuf_tensor("s_" + name, list(shape), dt))

    def slots(name, n, shape, dt):
        return Ring([(sb(f"{name}{i}", shape, dt), Res(f"{name}{i}")) for i in range(n)])

    def has(*ps):
        return any(p in phases for p in ps)

    need_big = 0
    if has("M"):
        need_big = max(need_big, 2 * TT + NKT * 192)
    if has("D"):
        need_big = max(need_big, TT + NKT * 192)
    if has("F1"):
        need_big = max(need_big, TT + NKT * 192)
    if has("F2"):
        need_big = max(need_big, 8 * 4096)
    big = sb("big", [128, max(need_big, 64)], BF16)
    psum = stack.enter_context(nc.psum_tensor("psum", [128, 8, 512], F32))
    bank = [Res(f"bank{i}") for i in range(8)]
    grp = [Res("grp0"), Res("grp1")]
    hT_ring = slots("hTt", 2, [128, 8, 512], BF16)
    stage_ring = slots("stg", 1, [128, 8, 512], F32)
    ones_bf = sb("ones_bf", [128, 128], BF16)
    bones_bf = sb("bones_bf", [128, 128], BF16)
    const_res = Res("consts")
    small = sb("small", [128, 256], F32)
    small_res = Res("small")
    sq_ring = slots("sq", 2, [128, 512], BF16)
    rstd_ring = slots("rstd", 2, [128, 512], F32)
    f32_ring = slots("tf", 6, [128, 512], F32)
    bfx_ring = slots("tb", 4, [128, 512], BF16)
    wA = sb("wA", [128, 8, 1024], BF16)
    wA_res = Res("wA")
    wB = sb("wB", [128, 8, 1024], BF16)
    wB_res = Res("wB")
    if has("0"):
        modv = sb("modv", [128, 24, 2], F32)
        modv_res = Res("modv")
        csb = sb("csb", [128, 8, 2], F32)
        csb_res = Res("csb")
        stab = sb("stab", [128, 160], F32)
        stab_res = Res("stab")
    if has("M", "D", "F1"):
        arF = sb("arF", [128, 8, 512], F32)
        arB = sb("arB", [128, 8, 512], BF16)
        pT_ring = Ring([(arB[:, 2 * i:2 * i + 2, :], Res(f"pT{i}")) for i in range(3)])
        tabs_ring = Ring([(arF[:, 4 + 2 * i:6 + 2 * i, :], Res(f"tabs{i}")) for i in range(2)])
        qT = [(arB[:, 6 + i, :], Res(f"qT{i}")) for i in range(2)]
        sg = arF[:, 0:4, :]
        sg_res = [Res(f"sg{i}") for i in range(4)]
        yraw = sb("yraw", [128, 1, 512], F32)
        yraw_res = [Res("yraw0"), Res("yraw1")]
    yg = sb("yg", [128, 8, 512], BF16)
    yg_res = [Res(f"yg{i}") for i in range(8)]
    if has("F1"):
        ident = sb("ident", [128, 128], BF16)
        mask3 = sb("mask3", [128, 384], BF16)
        pu = sb("pu", [128, 2, 528], F32)
        pu_res = Res("pu")
        prc = sb("prc", [128, 2, 512], F32)
        prc_res = Res("prc")
        pw = sb("pw", [128, 2, 2, 528], F32)
        pw_res = Res("pw")
        pl = sb("pl", [128, 2, 512], BF16)
        pl_res = [Res("pl0"), Res("pl1")]
        wpd = sb("wpd", [128, 2, 128], BF16)
        wpd_res = Res("wpd")
    if has("F2"):
        macc = arF
        macc_res = [Res(f"macc{i}") for i in range(8)]
        mbf = arB
        mbf_res = [Res(f"mbf{i}") for i in range(8)]
        wm_res = Res("wm")

    sp, pe, act, dve, pool = "sp", "pe", "act", "dve", "pool"

    def mm(out, lhsT, rhs, start, stop, reads, writes, tp=None):
        def fn(e, out=out, lhsT=lhsT, rhs=rhs, start=start, stop=stop, tp=tp):
            if tp is None:
                return e.matmul(out, lhsT, rhs, start=start, stop=stop)
            return e.matmul(out, lhsT, rhs, start=start, stop=stop, tile_position=tp)
        S_.op(pe, fn, reads, writes)

    def actf(out, in_, func, reads, writes, scale=1.0, bias=0.0):
        S_.op(act, lambda e, o=out, i=in_, f=func, s=scale, b=bias: e.activation(out=o, in_=i, func=f, bias=b, scale=s),
              reads, writes)

    def tt(eng, out, in0, in1, op, reads, writes):
        S_.op(eng, lambda e, o=out, a=in0, b=in1, p=op: e.tensor_tensor(out=o, in0=a, in1=b, op=p), reads, writes)

    def stt(eng, out, in0, scalar, in1, op0, op1, reads, writes):
        S_.op(eng, lambda e, o=out, a=in0, s=scalar, b=in1, p0=op0, p1=op1:
              e.scalar_tensor_tensor(out=o, in0=a, scalar=s, in1=b, op0=p0, op1=p1), reads, writes)

    def ts(eng, out, in0, s1, s2, op0, op1, reads, writes):
        if s2 is None:
            S_.op(eng, lambda e, o=out, a=in0, s=s1, p0=op0: e.tensor_scalar(out=o, in0=a, scalar1=s, scalar2=None, op0=p0),
                  reads, writes)
        else:
            S_.op(eng, lambda e, o=out, a=in0, s=s1, t=s2, p0=op0, p1=op1:
                  e.tensor_scalar(out=o, in0=a, scalar1=s, scalar2=t, op0=p0, op1=p1), reads, writes)

    def cp(eng, out, in_, reads, writes):
        if eng == act:
            S_.op(eng, lambda e, o=out, i=in_: e.copy(out=o, in_=i), reads, writes)
        else:
            S_.op(eng, lambda e, o=out, i=in_: e.tensor_copy(out=o, in_=i), reads, writes)

    def recip(out, in_, reads, writes):
        S_.op(dve, lambda e, o=out, i=in_: e.reciprocal(out=o, in_=i), reads, writes)

    def memset(eng, ap, val, writes):
        S_.op(eng, lambda e, a=ap, v=val: e.memset(a, v), (), writes)

    def ld(out, in_, res, reads=(), q=sp):
        S_.dma(q, out, in_, res, reads=reads, writes=(res,))

    def st(out, in_, res, q=pool):
        S_.dma(q, out, in_, res, reads=(res,), writes=())

    def rstd_from(ssum_ap, bank_res, n, eps, N, P=128):
        r_t, r_res = rstd_ring.next()
        actf(r_t[0:P, 0:N], ssum_ap, AF.Ln, (bank_res,), (r_res,), scale=1.0 / n, bias=eps)
        actf(r_t[0:P, 0:N], r_t[0:P, 0:N], AF.Exp, (r_res,), (r_res,), scale=-0.5)
        return r_t, r_res


    def finalize_div(acc, o0, N, add_ap=None):
        s0 = 64 - o0
        rs_t, rs_r = f32_ring.next()
        if add_ap is not None:
            ts(dve, rs_t[s0:s0 + 64, 0:N], psum[s0:s0 + 64, acc, 0:N], add_ap, None, ALU.add, None,
               (bank[acc], small_res), (rs_r,))
            recip(rs_t[s0:s0 + 64, 0:N], rs_t[s0:s0 + 64, 0:N], (rs_r,), (rs_r,))
        else:
            recip(rs_t[s0:s0 + 64, 0:N], psum[s0:s0 + 64, acc, 0:N], (bank[acc],), (rs_r,))
        r2_t, r2_r = f32_ring.next()
        S_.dma(pool, r2_t[o0:o0 + 64, 0:N], rs_t[s0:s0 + 64, 0:N], r2_r, reads=(rs_r,), writes=(r2_r,))
        y_t, y_r = f32_ring.next()
        tt(dve, y_t[o0:o0 + 64, 0:N], psum[o0:o0 + 64, acc, 0:N], r2_t[o0:o0 + 64, 0:N], ALU.mult,
           (bank[acc], r2_r), (y_r,))
        return y_t, y_r

    memset(dve, ones_bf[:, :], 1.0, (const_res,))
    memset(dve, bones_bf[:, :], 0.0, (const_res,))
    memset(dve, bones_bf[0:64, 0:64], 1.0, (const_res,))
    memset(dve, bones_bf[64:128, 64:128], 1.0, (const_res,))
    idr, mkr = Res("ident"), Res("mask3")
    if has("F1"):
        ld(ident[:, :], ident_in, idr)
        ld(mask3[:, :], mask3_in, mkr)

    hT_v = hT.rearrange("(k p) t -> p k t", p=128)
    av = small[:, 64:80].rearrange("p (k v) -> p k v", v=2)
    bv = small[:, 96:112].rearrange("p (k v) -> p k v", v=2)
    gv = small[:, 128:144].rearrange("p (k v) -> p k v", v=2)

    def load_hT(t0, N):
        t, r = hT_ring.next()
        ld(t[:, :, 0:N], hT_v[:, :, t0:t0 + N], r)
        return t, r

    def load_tab(idx, t0, N):
        t, r = tabs_ring.next()
        ld(t[:, :, 0:N], tab[idx:idx + 2, :, t0:t0 + N].rearrange("a p t -> p a t"), r)
        return t, r

    def load_w(dst, dres, src_cols_ap, ncols, kdim=8):
        s_t, s_r = stage_ring.next()
        ld(s_t[:, 0:kdim, 0:ncols], src_cols_ap, s_r)
        cp(pool, dst, s_t[:, 0:kdim, 0:ncols], (s_r,), (dres,))

    def win_cols(l, c0, n):
        return w_in[l, :, c0:c0 + n].rearrange("(k p) c -> p k c", p=128)

    def rope(z_ap, zsw_ap, zres, zswres, tabt, tabr, out_ap, out_res, P0, P1, N):
        t1, r1 = f32_ring.next()
        t2, r2 = f32_ring.next()
        tt(dve, t1[P0:P1, 0:N], zsw_ap, tabt[P0:P1, 1, 0:N], ALU.mult, (zswres, tabr), (r1,))
        tt(dve, t2[P0:P1, 0:N], z_ap, tabt[P0:P1, 0, 0:N], ALU.mult, (zres, tabr), (r2,))
        tt(pool, out_ap, t1[P0:P1, 0:N], t2[P0:P1, 0:N], ALU.add, (r1, r2), (out_res,))

    def proj(bank_i, w_t, w_res, c0, M, hT_t, hT_r, N, rows=None):
        for k in range(8):
            mm(psum[0:M, bank_i, 0:N], w_t[:, k, c0:c0 + M], hT_t[:, k, 0:N], k == 0, k == 7,
               (w_res, hT_r), (bank[bank_i],))

    def swap_halves(dst, src, res, ncols, half):
        nh = ncols // (2 * half)
        dv = dst.rearrange("p k (h j i) -> p k h j i", h=nh, j=2, i=half)
        sv = src.rearrange("p k (h j i) -> p k h j i", h=nh, j=2, i=half)
        for k in range(dst.shape[1]):
            cp(pool, dv[:, k, :, 0, :], sv[:, k, :, 1, :], (res,), (res,))
            cp(pool, dv[:, k, :, 1, :], sv[:, k, :, 0, :], (res,), (res,))

    def attn_core(kt_list, N, scale, acc_i, k_ap_fn, q_ap, q_res, v_ap_fn, tp=None, final=True):
        nk = len(kt_list)
        assert nk % 2 == 0
        for pi in range(nk // 2):
            g = attn_core.gi % 2
            attn_core.gi += 1
            b0 = 2 * g
            for j in range(2):
                kt = kt_list[2 * pi + j]
                mm(psum[:, b0 + j, 0:N], k_ap_fn(kt), q_ap, True, True, (q_res,), (grp[g],), tp=tp)
            p_t, p_r = pT_ring.next()
            actf(p_t[:, :, 0:N], psum[:, b0:b0 + 2, 0:N], AF.Exp, (grp[g],), (p_r,), scale=scale)
            for j in range(2):
                kt = kt_list[2 * pi + j]
                first = (pi == 0 and j == 0)
                last = final and (pi == nk // 2 - 1 and j == 1)
                mm(psum[:, acc_i, 0:N], v_ap_fn(kt), p_t[:, j, 0:N], first, last, (p_r,), (bank[acc_i],))
    attn_core.gi = 0


    for l in range(DEPTH):
        final = (l == DEPTH - 1)
        need_ctx = not final
        lam_init = 0.8 - 0.6 * math.exp(-0.3 * l)
        my_qtiles = qtiles if need_ctx else qtiles[1:]
        x_src = x_in if l == 0 else x1T
        xc_src = xc_in if l == 0 else xc1T
        x_dst = y_out if final else x1T
        xc_dst = xc1T
        xk_src = x_src.rearrange("(k p) t -> p k t", p=128)
        xck_src = xc_src.rearrange("(k p) t -> p k t", p=128)

        def x_tile_ap(t0, N):
            if t0 < CTX:
                return xck_src[:, :, 0:N]
            return xk_src[:, :, t0 - CTX:t0 - CTX + N]

        if has("0"):
            ld(small[:, 0:8], g_pre[l].rearrange("(k p) -> p k", p=128), small_res)
            ld(small[:, 8:16], g_post[l].rearrange("(k p) -> p k", p=128), small_res)
            ld(small[:, 16:40], b_mod[l].rearrange("(k p) -> p k", p=128), small_res)
            ld(small[:, 40:41], g_cq[l, 0:128].rearrange("(k p) -> p k", p=128), small_res)
            ld(small[0:64, 41:42], g_cq[l, 128:192].rearrange("(k p) -> p k", p=64), small_res)
            ld(small[:, 42:43], g_ckv[l].rearrange("(k p) -> p k", p=128), small_res)
            ld(small[0:64, 43:44], g_diff[l].rearrange("(k p) -> p k", p=64), small_res)
            ld(small[64:128, 43:44], g_diff[l].rearrange("(k p) -> p k", p=64), small_res)
            ld(small[:, 44:46], s_pool[l].rearrange("(k p) -> p k", p=128), small_res)
            ld(small[:, 48:52], sink[l].partition_broadcast(128), small_res)
            for i, la in enumerate(lam_in):
                ld(stab[:, i * 32:(i + 1) * 32], la[l].partition_broadcast(128), stab_res)
            ld(csb[:, :, 0], c_in.rearrange("(k p) -> p k", p=128), csb_res)
            ld(csb[:, :, 1], cc_in.rearrange("(k p) -> p k", p=128), csb_res)
            S_.barrier()
            ts(dve, small[:, 43:44], small[:, 43:44], 1.0 - lam_init, None, ALU.mult, None, (small_res,), (small_res,))
            actf(small[:, 48:52], small[:, 48:52], AF.Exp, (small_res,), (small_res,))
            tt(dve, stab[:, 128:160], stab[:, 0:32], stab[:, 32:64], ALU.mult, (stab_res,), (stab_res,))
            S_.op(dve, lambda e: e.reduce_sum(out=small[:, 53:54], in_=stab[:, 128:160], axis=mybir.AxisListType.X),
                  (stab_res,), (small_res,))
            tt(dve, stab[:, 128:160], stab[:, 64:96], stab[:, 96:128], ALU.mult, (stab_res, small_res), (stab_res,))
            S_.op(dve, lambda e: e.reduce_sum(out=small[:, 54:55], in_=stab[:, 128:160], axis=mybir.AxisListType.X),
                  (stab_res,), (small_res,))
            actf(small[:, 53:55], small[:, 53:55], AF.Exp, (small_res,), (small_res,))
            stt(dve, small[:, 52:53], small[:, 54:55], -lam_init, small[:, 53:54], ALU.add, ALU.subtract,
                (small_res,), (small_res,))
            t_t, t_r = f32_ring.next()
            tv = t_t[:, 0:16].rearrange("p (k v) -> p k v", v=2)
            actf(tv, csb[:, :, :], AF.Tanh, (csb_res,), (t_r,), scale=0.5)
            stt(dve, tv, tv, 1.0, csb[:, :, :], ALU.add, ALU.mult, (t_r, csb_res), (t_r,))
            ts(dve, csb[:, :, :], tv, 0.5, None, ALU.mult, None, (t_r,), (csb_res,))
            for j in range(24):
                s_t, s_r = stage_ring.next()
                ld(s_t[:, :, 0:128], w_mod[l, :, j * 128:(j + 1) * 128].rearrange("(k p) c -> p k c", p=128), s_r)
                bi = 6 + (j % 2)
                for k in range(8):
                    mm(psum[:, bi, 0:2], s_t[:, k, 0:128], csb[:, k, :], k == 0, k == 7, (s_r, csb_res), (bank[bi],))
                ts(dve, modv[:, j, :], psum[:, bi, 0:2], small[:, 16 + j:17 + j], None, ALU.add, None,
                   (bank[bi], small_res), (modv_res,))
            for k in range(8):
                ts(dve, av[:, k, :], modv[:, 8 + k, :], 1.0, small[:, k:k + 1], ALU.add, ALU.mult,
                   (modv_res, small_res), (small_res,))
                ts(dve, gv[:, k, :], modv[:, 16 + k, :], small[:, 8 + k:9 + k], None, ALU.mult, None,
                   (modv_res, small_res), (small_res,))
            cp(dve, bv, modv[:, 0:8, :], (modv_res, small_res), (small_res,))
            st(small_s, small[:, :], small_res)
            S_.barrier()
            for (t0, N) in qtiles:
                v = 1 if t0 < CTX else 0
                x_t, x_r = stage_ring.next()
                ld(x_t[:, :, 0:N], x_tile_ap(t0, N), x_r)
                h_t, h_r = hT_ring.next()
                for k in range(8):
                    q_t, q_r = sq_ring.next()
                    actf(q_t[:, 0:N], x_t[:, k, 0:N], AF.Square, (x_r,), (q_r,))
                    mm(psum[:, 6, 0:N], ones_bf[:, :], q_t[:, 0:N], k == 0, k == 7, (q_r, const_res), (bank[6],))
                r_t, r_r = rstd_from(psum[:, 6, 0:N], bank[6], D, EPS, N)
                for k in range(8):
                    f_t, f_r = f32_ring.next()
                    stt(dve, f_t[:, 0:N], x_t[:, k, 0:N], av[:, k, v:v + 1], r_t[:, 0:N], ALU.mult, ALU.mult,
                        (x_r, r_r, small_res), (f_r,))
                    actf(h_t[:, k, 0:N], f_t[:, 0:N], AF.Identity, (f_r, small_res), (h_r,), bias=bv[:, k, v:v + 1])
                st(hT_v[:, :, t0:t0 + N], h_t[:, :, 0:N], h_r)
            S_.barrier()
        else:
            ld(small[:, :], small_s, small_res)
            S_.barrier()

        def gates_chunk(w_t, w_res, c0, sgi, bi, h_t, h_r, N):
            proj(bi, w_t, w_res, c0, 128, h_t, h_r, N)
            th_t, th_r = f32_ring.next()
            actf(th_t[:, 0:N], psum[:, bi, 0:N], AF.Tanh, (bank[bi],), (th_r,), scale=0.5)
            stt(dve, sg[:, sgi, 0:N], th_t[:, 0:N], 1.0, psum[:, bi, 0:N], ALU.add, ALU.mult,
                (th_r, bank[bi]), (sg_res[sgi],))

        if has("M"):
            ygm_v = ygm_s.rearrange("(c p) t -> p c t", p=128)
            KTm = [big[:, i * TT:(i + 1) * TT] for i in range(2)]
            VB = 2 * TT

            def vm_ap(kt, hh):
                base = VB + kt * 192 + hh * 64
                return big[:, base:base + 128]

            Vall = big[:, VB:VB + NKT * 192].rearrange("p (t s d) -> p t s d", t=NKT, s=3, d=64)
            load_w(wA[:, :, 0:128], wA_res, win_cols(l, C_CKV, 128), 128)
            memset(pool, wA[:, :, 128:320], 0.0, (wA_res,))
            load_w(wA[:, :, 192:224], wA_res, win_cols(l, C_KR, 32), 32)
            cp(pool, wA[:, :, 288:304], wA[:, :, 208:224], (wA_res,), (wA_res,))
            cp(pool, wA[:, :, 304:320], wA[:, :, 192:208], (wA_res,), (wA_res,))
            load_w(wA[:, :, 320:512], wA_res, win_cols(l, C_CQ, 192), 192)
            load_w(wA[:, :, 512:768], wA_res, win_cols(l, C_G + 256, 256), 256)
            s_t, s_r = stage_ring.next()
            memset(pool, s_t[:, 0:3, :], 0.0, (s_r,))
            ld(s_t[:, 0, 0:256], w_uk[l], s_r)
            ld(s_t[:, 0, 256:512], w_uv[l], s_r)
            ld(s_t[:, 1, 0:384], w_uq[l, 0:128, :], s_r)
            ld(s_t[0:64, 2, 0:384], w_uq[l, 128:192, :], s_r)
            cp(pool, wB[:, 0:3, 0:512], s_t[:, 0:3, 0:512], (s_r,), (wB_res,))
            cp(pool, wB[:, 3:5, 0:384], wB[:, 1:3, 0:384], (wB_res,), (wB_res,))
            for kk in range(2):
                sv_ = wB[:, 1 + kk, 0:384].rearrange("p (h c) -> p h c", c=96)
                dv_ = wB[:, 3 + kk, 0:384].rearrange("p (h c) -> p h c", c=96)
                cp(pool, dv_[:, :, 64:80], sv_[:, :, 80:96], (wB_res,), (wB_res,))
                cp(pool, dv_[:, :, 80:96], sv_[:, :, 64:80], (wB_res,), (wB_res,))
            for pr in range(2):
                memset(pool, Vall[:, :, 1, :], 1.0, ())
                for (t0, N) in qtiles:
                    h_t, h_r = load_hT(t0, N)
                    tb_t, tb_r = load_tab(0, t0, N)
                    proj(6, wA, wA_res, 0, 128, h_t, h_r, N)
                    q_t, q_r = sq_ring.next()
                    actf(q_t[:, 0:N], psum[:, 6, 0:N], AF.Square, (bank[6],), (q_r,))
                    mm(psum[:, 7, 0:N], ones_bf[:, :], q_t[:, 0:N], True, True, (q_r, const_res), (bank[7],))
                    r_t, r_r = rstd_from(psum[:, 7, 0:N], bank[7], 128, EPS, N)
                    cn_t, cn_r = bfx_ring.next()
                    stt(dve, cn_t[:, 0:N], psum[:, 6, 0:N], small[:, 42:43], r_t[:, 0:N], ALU.mult, ALU.mult,
                        (bank[6], r_r, small_res), (cn_r,))
                    proj(6, wA, wA_res, 128, 96, h_t, h_r, N)
                    proj(7, wA, wA_res, 224, 96, h_t, h_r, N)
                    kr_t, kr_r = bfx_ring.next()
                    rope(psum[64:96, 6, 0:N], psum[64:96, 7, 0:N], bank[6], bank[7], tb_t, tb_r,
                         kr_t[64:96, 0:N], kr_r, 64, 96, N)
                    for hh in range(2):
                        h = 2 * pr + hh
                        cp(pool, KTm[hh][64:96, t0:t0 + N], kr_t[64:96, 0:N], (kr_r,), ())
                        bi = 4 + hh
                        mm(psum[0:64, bi, 0:N], wB[:, 0, h * 64:(h + 1) * 64], cn_t[:, 0:N], True, True,
                           (wB_res, cn_r), (bank[bi],))
                        cp(act, KTm[hh][0:64, t0:t0 + N], psum[0:64, bi, 0:N], (bank[bi],), ())
                    for s in range(N // 128):
                        bi = 6 + (s % 2)
                        kt = t0 // 128 + s
                        mm(psum[:, bi, 0:128], cn_t[:, s * 128:(s + 1) * 128], wB[:, 0, 256 + pr * 128:256 + (pr + 1) * 128],
                           True, True, (cn_r, wB_res), (bank[bi],))
                        cp(dve, Vall[:, kt, 0:3:2, :], psum[:, bi, 0:128].rearrange("p (s d) -> p s d", s=2),
                           (bank[bi],), ())
                S_.barrier()
                for (t0, N) in my_qtiles:
                    kts = list(range(2)) if t0 < CTX else list(range(NKT))
                    h_t, h_r = load_hT(t0, N)
                    tb_t, tb_r = load_tab(0, t0, N)
                    proj(6, wA, wA_res, 320, 128, h_t, h_r, N)
                    proj(7, wA, wA_res, 448, 64, h_t, h_r, N)
                    qa_t, qa_r = sq_ring.next()
                    qb_t, qb_r = sq_ring.next()
                    actf(qa_t[:, 0:N], psum[:, 6, 0:N], AF.Square, (bank[6],), (qa_r,))
                    actf(qb_t[0:64, 0:N], psum[0:64, 7, 0:N], AF.Square, (bank[7],), (qb_r,))
                    mm(psum[:, 5, 0:N], ones_bf[:, :], qa_t[:, 0:N], True, False, (qa_r, const_res), (bank[5],))
                    mm(psum[:, 5, 0:N], ones_bf[0:64, :], qb_t[0:64, 0:N], False, True, (qb_r, const_res), (bank[5],))
                    r_t, r_r = rstd_from(psum[:, 5, 0:N], bank[5], 192, EPS, N)
                    ca_t, ca_r = bfx_ring.next()
                    cb_t, cb_r = bfx_ring.next()
                    stt(dve, ca_t[:, 0:N], psum[:, 6, 0:N], small[:, 40:41], r_t[:, 0:N], ALU.mult, ALU.mult,
                        (bank[6], r_r, small_res), (ca_r,))
                    stt(dve, cb_t[0:64, 0:N], psum[0:64, 7, 0:N], small[0:64, 41:42], r_t[0:64, 0:N], ALU.mult, ALU.mult,
                        (bank[7], r_r, small_res), (cb_r,))
                    gates_chunk(wA, wA_res, 512 + pr * 128, pr, 6, h_t, h_r, N)
                    for hh in range(2):
                        h = 2 * pr + hh
                        for (bi, kb) in ((6, 1), (7, 3)):
                            mm(psum[0:96, bi, 0:N], wB[:, kb, h * 96:(h + 1) * 96], ca_t[:, 0:N], True, False,
                               (wB_res, ca_r), (bank[bi],))
                            mm(psum[0:96, bi, 0:N], wB[0:64, kb + 1, h * 96:(h + 1) * 96], cb_t[0:64, 0:N], False, True,
                               (wB_res, cb_r), (bank[bi],))
                        q_t, q_r = qT[hh]
                        rope(psum[0:96, 6, 0:N], psum[0:96, 7, 0:N], bank[6], bank[7], tb_t, tb_r, q_t[0:96, 0:N], q_r,
                             0, 96, N)
                        acc = 4 + hh
                        attn_core(kts, N, 96 ** -0.5, acc,
                                  lambda kt, hh=hh: KTm[hh][0:96, kt * 128:(kt + 1) * 128],
                                  q_t[0:96, 0:N], q_r, lambda kt, hh=hh: vm_ap(kt, hh))
                        o0 = hh * 64
                        y_t, y_r = finalize_div(acc, o0, N)
                        tt(pool, yg[o0:o0 + 64, 2 + pr, 0:N], y_t[o0:o0 + 64, 0:N], sg[o0:o0 + 64, pr, 0:N], ALU.mult,
                           (y_r, sg_res[pr]), (yg_res[2 + pr],))
                    st(ygm_v[:, pr, t0:t0 + N], yg[:, 2 + pr, 0:N], yg_res[2 + pr])
                S_.barrier()

        if has("D"):
            ygd_v = ygd_s.rearrange("(c p) t -> p c t", p=128)
            KTd = big[:, 0:TT]
            VBd = TT

            def vd_ap(kt, hh):
                base = VBd + kt * 192 + hh * 64
                return big[:, base:base + 128]

            Vd = big[:, VBd:VBd + NKT * 192].rearrange("p (t s d) -> p t s d", t=NKT, s=3, d=64)
            load_w(wA[:, :, 0:256], wA_res, win_cols(l, C_DK, 256), 256)
            swap_halves(wA[:, :, 256:512], wA[:, :, 0:256], wA_res, 256, 16)
            load_w(wA[:, :, 512:768], wA_res, win_cols(l, C_DV, 256), 256)
            load_w(wB[:, :, 0:256], wB_res, win_cols(l, C_DQ, 256), 256)
            swap_halves(wB[:, :, 256:512], wB[:, :, 0:256], wB_res, 256, 16)
            load_w(wB[:, :, 512:768], wB_res, win_cols(l, C_G + 512, 256), 256)
            for pr in range(2):
                memset(pool, Vd[:, :, 1, :], 1.0, ())
                for (t0, N) in qtiles:
                    h_t, h_r = load_hT(t0, N)
                    tb_t, tb_r = load_tab(2, t0, N)
                    proj(6, wA, wA_res, pr * 128, 128, h_t, h_r, N)
                    proj(7, wA, wA_res, 256 + pr * 128, 128, h_t, h_r, N)
                    rope(psum[:, 6, 0:N], psum[:, 7, 0:N], bank[6], bank[7], tb_t, tb_r, KTd[:, t0:t0 + N], Res("kd"),
                         0, 128, N)
                    for s in range(N // 128):
                        bi = 4 + (s % 2)
                        kt = t0 // 128 + s
                        for k in range(8):
                            mm(psum[:, bi, 0:128], h_t[:, k, s * 128:(s + 1) * 128],
                               wA[:, k, 512 + pr * 128:512 + (pr + 1) * 128], k == 0, k == 7, (h_r, wA_res), (bank[bi],))
                        cp(dve, Vd[:, kt, 0:3:2, :], psum[:, bi, 0:128].rearrange("p (s d) -> p s d", s=2),
                           (bank[bi],), ())
                S_.barrier()
                for (t0, N) in my_qtiles:
                    kts = list(range(2)) if t0 < CTX else list(range(NKT))
                    h_t, h_r = load_hT(t0, N)
                    tb_t, tb_r = load_tab(2, t0, N)
                    proj(6, wB, wB_res, pr * 128, 128, h_t, h_r, N)
                    proj(7, wB, wB_res, 256 + pr * 128, 128, h_t, h_r, N)
                    q_t, q_r = qT[0]
                    rope(psum[:, 6, 0:N], psum[:, 7, 0:N], bank[6], bank[7], tb_t, tb_r, q_t[:, 0:N], q_r, 0, 128, N)
                    gates_chunk(wB, wB_res, 512 + pr * 128, pr, 6, h_t, h_r, N)
                    for hh in range(2):
                        o0 = hh * 64
                        ys = []
                        for m in range(2):
                            i = 2 * hh + m
                            acc = 4 + m
                            attn_core(kts, N, 32 ** -0.5, acc,
                                      lambda kt, i=i: KTd[32 * i:32 * i + 32, kt * 128:(kt + 1) * 128],
                                      q_t[32 * i:32 * i + 32, 0:N], q_r, lambda kt, hh=hh: vd_ap(kt, hh), tp=(32 * i, 0))
                            ys.append(finalize_div(acc, o0, N))
                        (y1, r1), (y2, r2) = ys
                        stt(dve, yraw[o0:o0 + 64, 0, 0:N], y2[o0:o0 + 64, 0:N], small[o0:o0 + 64, 52:53],
                            y1[o0:o0 + 64, 0:N], ALU.mult, ALU.add, (r1, r2, small_res), (yraw_res[0],))
                    s_t2, s_r2 = sq_ring.next()
                    actf(s_t2[:, 0:N], yraw[:, 0, 0:N], AF.Square, (yraw_res[0],), (s_r2,))
                    mm(psum[:, 6, 0:N], bones_bf[:, :], s_t2[:, 0:N], True, True, (s_r2, const_res), (bank[6],))
                    r_t, r_r = rstd_from(psum[:, 6, 0:N], bank[6], 64, EPS, N)
                    f_t, f_r = f32_ring.next()
                    stt(dve, f_t[:, 0:N], yraw[:, 0, 0:N], small[:, 43:44], r_t[:, 0:N], ALU.mult, ALU.mult,
                        (yraw_res[0], r_r, small_res), (f_r,))
                    tt(pool, yg[:, 4 + pr, 0:N], f_t[:, 0:N], sg[:, pr, 0:N], ALU.mult, (f_r, sg_res[pr]), (yg_res[4 + pr],))
                    st(ygd_v[:, pr, t0:t0 + N], yg[:, 4 + pr, 0:N], yg_res[4 + pr])
                S_.barrier()

        if has("F1"):
            ygs_v = ygs_s.rearrange("(c p) t -> p c t", p=128)
            ygp_v = ygp_s.rearrange("(c p) t -> p c t", p=128)
            pool_v = pool_s.rearrange("(c p) t -> p c t", p=128)
            rcnt_v = rcnt.rearrange("(c p) t -> p c t", p=128)
            KTs = big[:, 0:TT]
            VBs = TT

            def vs_ap(kt, kv):
                base = VBs + kt * 192 + kv * 64
                return big[:, base:base + 128]

            Vs = big[:, VBs:VBs + NKT * 192].rearrange("p (t s d) -> p t s d", t=NKT, s=3, d=64)
            memset(pool, Vs[:, :, 1, :], 1.0, ())
            memset(pool, pw[:, :, 0, 0:16], 0.0, (pw_res,))
            st(pool_v[:, :, 0:PPAD0], pw[:, :, 0, 0:PPAD0], pw_res)
            st(pool_v[:, :, PPAD0 + CTX:PPAD0 + CTX + PPAD1], pw[:, :, 0, 0:PPAD1], pw_res)
            st(pool_v[:, :, PT - PPAD2:PT], pw[:, :, 0, 0:PPAD2], pw_res)
            load_w(wA[:, :, 0:128], wA_res, win_cols(l, C_SK, 128), 128)
            swap_halves(wA[:, :, 128:256], wA[:, :, 0:128], wA_res, 128, 32)
            load_w(wA[:, :, 256:384], wA_res, win_cols(l, C_SV, 128), 128)
            load_w(wA[:, :, 384:640], wA_res, win_cols(l, C_POOL, 256), 256)
            for g in range(2):
                for kv in range(2):
                    load_w(wB[:, :, g * 128 + kv * 64:g * 128 + kv * 64 + 64], wB_res,
                           win_cols(l, C_SQ + (kv * 2 + g) * 64, 64), 64)
                    load_w(wB[:, :, 512 + g * 128 + kv * 64:512 + g * 128 + kv * 64 + 64], wB_res,
                           win_cols(l, C_G + 768 + (kv * 2 + g) * 64, 64), 64)
            swap_halves(wB[:, :, 256:512], wB[:, :, 0:256], wB_res, 256, 32)
            load_w(wB[:, :, 768:1024], wB_res, win_cols(l, C_G, 256), 256)
            s_t, s_r = stage_ring.next()
            memset(pool, s_t[:, 0:2, 0:128], 0.0, (s_r,))
            for g4 in range(4):
                r0 = (g4 % 2) * 64
                ld(s_t[r0:r0 + 64, g4 // 2, r0:r0 + 64], w_pool[l, g4], s_r)
            cp(pool, wpd[:, :, :], s_t[:, 0:2, 0:128], (s_r,), (wpd_res,))
            for (t0, N) in qtiles:
                h_t, h_r = load_hT(t0, N)
                tb_t, tb_r = load_tab(4, t0, N)
                proj(6, wA, wA_res, 0, 128, h_t, h_r, N)
                proj(7, wA, wA_res, 128, 128, h_t, h_r, N)
                rope(psum[:, 6, 0:N], psum[:, 7, 0:N], bank[6], bank[7], tb_t, tb_r, KTs[:, t0:t0 + N], Res("ks"), 0, 128, N)
                for s in range(N // 128):
                    bi = 4 + (s % 2)
                    kt = t0 // 128 + s
                    for k in range(8):
                        mm(psum[:, bi, 0:128], h_t[:, k, s * 128:(s + 1) * 128], wA[:, k, 256:384], k == 0, k == 7,
                           (h_r, wA_res), (bank[bi],))
                    cp(dve, Vs[:, kt, 0:3:2, :], psum[:, bi, 0:128].rearrange("p (s d) -> p s d", s=2), (bank[bi],), ())
                for c_ in range(2):
                    proj(6 + c_, wA, wA_res, 384 + c_ * 128, 128, h_t, h_r, N)
                    cp(act, pu[:, c_, 0:N], psum[:, 6 + c_, 0:N], (bank[6 + c_],), (pu_res,))
                po = t0 + (PPAD0 if t0 < CTX else PPAD0 + PPAD1)
                st(pool_v[:, :, po:po + N], pu[:, :, 0:N], pu_res)
            S_.barrier()
            for (t0, N) in my_qtiles:
                is_ctx = t0 < CTX
                h_t, h_r = load_hT(t0, N)
                tb_t, tb_r = load_tab(4, t0, N)
                po = t0 + (PPAD0 if is_ctx else PPAD0 + PPAD1)
                ld(pu[:, :, 0:N + 16], pool_v[:, :, po - 8:po + N + 8], pu_res)
                ld(prc[:, :, 0:N], rcnt_v[:, :, po:po + N], prc_res)
                for g in range(2):
                    proj(6, wB, wB_res, g * 128, 128, h_t, h_r, N)
                    proj(7, wB, wB_res, 256 + g * 128, 128, h_t, h_r, N)
                    rope(psum[:, 6, 0:N], psum[:, 7, 0:N], bank[6], bank[7], tb_t, tb_r, qT[g][0][:, 0:N], qT[g][1], 0, 128, N)
                for i4 in range(4):
                    gates_chunk(wB, wB_res, 512 + i4 * 128, i4, 6 + (i4 % 2), h_t, h_r, N)
                for g in range(2):
                    q_t, q_r = qT[g]
                    for kv in range(2):
                        acc = 4 + kv
                        o0 = kv * 64
                        s0 = 64 - o0
                        hq = kv * 2 + g
                        wins = []
                        if not is_ctx:
                            qb0 = (t0 - CTX) // 128
                            for ktl in range(max(qb0 - 1, 0), min(qb0 + 4, NL - 1) + 1):
                                qlo, qhi = max(ktl - 1, qb0), min(ktl + 1, qb0 + 3)
                                wins.append((ktl, (qlo - qb0) * 128, (qhi - qb0 + 1) * 128, (qlo - ktl + 1) * 128))
                        attn_core([0, 1], N, 0.125, acc,
                                  lambda kt, kv=kv: KTs[kv * 64:kv * 64 + 64, kt * 128:(kt + 1) * 128],
                                  q_t[o0:o0 + 64, 0:N], q_r, lambda kt, kv=kv: vs_ap(kt, kv), final=(len(wins) == 0))
                        for wi, (ktl, c0, c1, m0) in enumerate(wins):
                            gq = attn_core.gi % 2
                            attn_core.gi += 1
                            b0 = 2 * gq
                            kt = 2 + ktl
                            mm(psum[:, b0, c0:c1], KTs[o0:o0 + 64, kt * 128:(kt + 1) * 128], q_t[o0:o0 + 64, c0:c1],
                               True, False, (q_r,), (grp[gq],))
                            mm(psum[:, b0, c0:c1], ident[:, :], mask3[:, m0:m0 + (c1 - c0)], False, True, (idr, mkr),
                               (grp[gq],))
                            p_t, p_r = pT_ring.next()
                            actf(p_t[:, 0, c0:c1], psum[:, b0, c0:c1], AF.Exp, (grp[gq],), (p_r,), scale=0.125)
                            mm(psum[:, acc, c0:c1], vs_ap(kt, kv), p_t[:, 0, c0:c1], False, wi == len(wins) - 1,
                               (p_r,), (bank[acc],))
                        y_t, y_r = finalize_div(acc, o0, N, add_ap=small[s0:s0 + 64, 48 + hq:49 + hq])
                        tt(pool, yg[o0:o0 + 64, 6 + g, 0:N], y_t[o0:o0 + 64, 0:N], sg[o0:o0 + 64, g, 0:N], ALU.mult,
                           (y_r, sg_res[g]), (yg_res[6 + g],))
                for g in range(2):
                    st(ygs_v[:, g, t0:t0 + N], yg[:, 6 + g, 0:N], yg_res[6 + g])
                M_ = N + 16
                for c_ in range(2):
                    for half in range(2):
                        gi4 = 2 * c_ + half
                        nlev = gi4 + 1
                        R0, R1 = half * 64, half * 64 + 64
                        tt(pool, pw[R0:R1, c_, 1, 1:M_], pu[R0:R1, c_, 1:M_], pu[R0:R1, c_, 0:M_ - 1], ALU.add,
                           (pu_res,), (pw_res,))
                        for lev in range(2, nlev + 1):
                            lo = (1 << lev) - 1
                            sh = 1 << (lev - 1)
                            src = pw[R0:R1, c_, (lev - 1) % 2, :]
                            tt(pool, pw[R0:R1, c_, lev % 2, lo:M_], src[:, lo:M_], src[:, lo - sh:M_ - sh], ALU.add,
                               (pw_res,), (pw_res,))
                        off = 8 + (1 << (nlev - 1)) - 1
                        f_t, f_r = f32_ring.next()
                        tt(pool, f_t[R0:R1, 0:N], pw[R0:R1, c_, nlev % 2, off:off + N], prc[R0:R1, c_, 0:N], ALU.mult,
                           (pw_res, prc_res), (f_r,))
                        tt(pool, pl[R0:R1, c_, 0:N], f_t[R0:R1, 0:N], pu[R0:R1, c_, 8:8 + N], ALU.subtract,
                           (f_r, pu_res), (pl_res[c_],))
                    bi = 6 + c_
                    mm(psum[:, bi, 0:N], wpd[:, c_, :], pl[:, c_, 0:N], True, True, (wpd_res, pl_res[c_]), (bank[bi],))
                    stt(dve, yg[:, c_, 0:N], psum[:, bi, 0:N], small[:, 44 + c_:45 + c_], sg[:, 2 + c_, 0:N],
                        ALU.mult, ALU.mult, (bank[bi], small_res, sg_res[2 + c_]), (yg_res[c_],))
                    st(ygp_v[:, c_, t0:t0 + N], yg[:, c_, 0:N], yg_res[c_])
            S_.barrier()

        if has("F2"):
            ysrc = {0: ygp_s, 1: ygm_s, 2: ygd_s, 3: ygs_s}
            yv = {r: ysrc[r].rearrange("(c p) t -> p c t", p=128) for r in range(4)}
            wm = big[:, 0:8 * 4096].rearrange("p (k c) -> p k c", k=8)
            for blk in range(8):
                load_w(wm[:, :, blk * 512:(blk + 1) * 512], wm_res, win_cols(l, C_M + blk * 512, 512), 512)
            for ch in range(2):
                s_t, s_r = stage_ring.next()
                for r in range(3):
                    ld(s_t[:, 2 * r:2 * r + 2, :], w_br[l, r, :, ch * 512:(ch + 1) * 512].rearrange("(kc p) c -> p kc c", p=128),
                       s_r)
                for g in range(2):
                    for kv in range(2):
                        r0 = (kv * 2 + g) * 64
                        ld(s_t[kv * 64:kv * 64 + 64, 6 + g, :], w_br[l, 3, r0:r0 + 64, ch * 512:(ch + 1) * 512], s_r)
                cp(pool, wA[:, :, ch * 512:(ch + 1) * 512], s_t[:, :, :], (s_r,), (wA_res,))
                s_t, s_r = stage_ring.next()
                ld(s_t[:, :, :], w_out[l, :, ch * 512:(ch + 1) * 512].rearrange("(k p) c -> p k c", p=128), s_r)
                cp(pool, wB[:, :, ch * 512:(ch + 1) * 512], s_t[:, :, :], (s_r,), (wB_res,))
            for (t0, N) in my_qtiles:
                is_ctx = t0 < CTX
                v = 1 if is_ctx else 0
                h_t, h_r = load_hT(t0, N)
                for r in range(4):
                    for kc in range(2):
                        ld(yg[:, 2 * r + kc, 0:N], yv[r][:, kc, t0:t0 + N], yg_res[2 * r + kc])
                for j in range(8):
                    for r in range(4):
                        bz = 4 + (r % 2)
                        bb = 6 + (r % 2)
                        proj(bz, wm, wm_res, r * 1024 + j * 128, 128, h_t, h_r, N)
                        tm_t, tm_r = bfx_ring.next()
                        actf(tm_t[:, 0:N], psum[:, bz, 0:N], AF.Tanh, (bank[bz],), (tm_r,), scale=0.5)
                        for kc in range(2):
                            mm(psum[:, bb, 0:N], wA[:, 2 * r + kc, j * 128:(j + 1) * 128], yg[:, 2 * r + kc, 0:N],
                               kc == 0, kc == 1, (wA_res, yg_res[2 * r + kc]), (bank[bb],))
                        if r == 0:
                            stt(dve, macc[:, j, 0:N], tm_t[:, 0:N], 1.0, psum[:, bb, 0:N], ALU.add, ALU.mult,
                                (tm_r, bank[bb]), (macc_res[j],))
                        else:
                            f_t, f_r = f32_ring.next()
                            stt(dve, f_t[:, 0:N], tm_t[:, 0:N], 1.0, psum[:, bb, 0:N], ALU.add, ALU.mult,
                                (tm_r, bank[bb]), (f_r,))
                            if r < 3:
                                tt(pool, macc[:, j, 0:N], macc[:, j, 0:N], f_t[:, 0:N], ALU.add, (macc_res[j], f_r),
                                   (macc_res[j],))
                            else:
                                tt(pool, mbf[:, j, 0:N], macc[:, j, 0:N], f_t[:, 0:N], ALU.add, (macc_res[j], f_r),
                                   (mbf_res[j],))
                for j in range(8):
                    bo = 4 + (j % 2)
                    for k in range(8):
                        mm(psum[:, bo, 0:N], wB[:, k, j * 128:(j + 1) * 128], mbf[:, k, 0:N], k == 0, k == 7,
                           (wB_res, mbf_res[k]), (bank[bo],))
                    cp(act, macc[:, j, 0:N], psum[:, bo, 0:N], (bank[bo],), (macc_res[j],))
                    q_t, q_r = sq_ring.next()
                    actf(q_t[:, 0:N], psum[:, bo, 0:N], AF.Square, (bank[bo],), (q_r,))
                    mm(psum[:, 6, 0:N], ones_bf[:, :], q_t[:, 0:N], j == 0, j == 7, (q_r, const_res), (bank[6],))
                r_t, r_r = rstd_from(psum[:, 6, 0:N], bank[6], D, 16 * EPS, N)
                dst_v = (xc_dst if is_ctx else x_dst).rearrange("(k p) t -> p k t", p=128)
                tq = 0 if is_ctx else t0 - CTX
                for j in range(8):
                    xs_t, xs_r = f32_ring.next()
                    ld(xs_t[:, 0:N], x_tile_ap(t0, N)[:, j, :], xs_r)
                    f_t, f_r = f32_ring.next()
                    stt(dve, f_t[:, 0:N], macc[:, j, 0:N], gv[:, j, v:v + 1], r_t[:, 0:N], ALU.mult, ALU.mult,
                        (macc_res[j], r_r, small_res), (f_r,))
                    tt(pool, xs_t[:, 0:N], f_t[:, 0:N], xs_t[:, 0:N], ALU.add, (f_r, xs_r), (xs_r,))
                    st(dst_v[:, j, tq:tq + N], xs_t[:, 0:N], xs_r)
            S_.barrier()

    S_.barrier()
    with nc.Block() as block:
        @block.sync
        def _(e):
            S_.replay("sp", e)

        @block.tensor
        def _(e):
            S_.replay("pe", e)

        @block.scalar
        def _(e):
            S_.replay("act", e)

        @block.vector
        def _(e):
            S_.replay("dve", e)

        @block.gpsimd
        def _(e):
            S_.replay("pool", e)
    stack.close()
    nc.in_names_ = in_names
    nc.sched_ = S_
    return nc


def make_inputs_core(inputs, b, S, consts=None):
    if consts is None:
        consts = make_consts(S)
    m = {
        "xT": np.ascontiguousarray(inputs["x"][b, :S].T),
        "ctxT": np.ascontiguousarray(inputs["ctx"][b].T),
        "c": np.ascontiguousarray(inputs["c"][b]),
        "c_ctx": np.ascontiguousarray(inputs["c_ctx"]),
    }
    for k in ("w_mod", "b_mod", "g_pre", "g_post", "w_in", "w_pool", "s_pool", "g_cq", "w_uq", "g_ckv", "w_uk", "w_uv",
              "lam_q1", "lam_k1", "lam_q2", "lam_k2", "g_diff", "sink", "w_br", "w_out"):
        m[k] = np.ascontiguousarray(inputs[k])
    m.update(consts)
    return m


def kernel(**inputs):
    inputs = {k: np.asarray(v) for k, v in inputs.items()}
    B, S = inputs["x"].shape[0], inputs["x"].shape[1]
    consts = make_consts(S)
    nc = build_fused(S)
    names = nc.in_names_
    in_maps = []
    for b in range(B):
        m = make_inputs_core(inputs, b, S, consts)
        in_maps.append({k: m[k] for k in names})
    res = run_bass_kernel_spmd(nc, in_maps, core_ids=list(range(B)))
    out = np.stack([np.ascontiguousarray(np.asarray(res.results[b]["yT"]).T) for b in range(B)], 0)
    return out.astype(np.float32)
```

```python
import math
from contextlib import ExitStack

import numpy as np
import ml_dtypes

import concourse.bass as bass
import concourse.mybir as mybir
from concourse.bass_utils import run_bass_kernel_spmd

F32 = mybir.dt.float32
BF16 = mybir.dt.bfloat16
AF = mybir.ActivationFunctionType
ALU = mybir.AluOpType

D = 1024
CTX = 256
DEPTH = 2
IN_COLS = 7008
EPS = 1e-6
C_POOL, C_CQ, C_CKV, C_KR, C_DQ, C_DK, C_DV, C_SQ, C_SK, C_SV, C_G, C_M = (
    0, 256, 448, 576, 608, 864, 1120, 1376, 1632, 1760, 1888, 2912)
POOL_WINDOWS = (2, 4, 8, 16)
PPAD0, PPAD1, PPAD2 = 8, 16, 8


class Res:
    __slots__ = ("name", "w", "r", "dsem", "dcnt")

    def __init__(self, name):
        self.name = name
        self.w = None
        self.r = {}
        self.dsem = None
        self.dcnt = 0


class Sched:
    ENGS = ("pe", "act", "dve", "pool", "sp")
    EPOCH = 30000

    def __init__(self, nc, stack):
        self.nc = nc
        self.stack = stack
        self.ops = {e: [] for e in self.ENGS}
        self.cnt = {e: 0 for e in self.ENGS}
        self.sems = {e: [] for e in self.ENGS}
        self.known = {e: {} for e in self.ENGS}
        self.dma_res = []
        self.nsem = 0

    def _sem(self, name):
        self.nsem += 1
        return self.stack.enter_context(self.nc.semaphore(name))

    def _eng_sem(self, eng, idx):
        ep = idx // self.EPOCH
        while len(self.sems[eng]) <= ep:
            self.sems[eng].append(self._sem(f"p_{eng}_{len(self.sems[eng])}"))
        return self.sems[eng][ep], idx % self.EPOCH + 1

    def _collect(self, eng, reads, writes):
        deps = []
        for r in reads:
            if r.w is not None:
                deps.append((r.w, True))
        for w in writes:
            if w.w is not None:
                deps.append((w.w, False))
            for t in w.r.values():
                deps.append((t, False))
        need = {}
        for tok, raw in deps:
            if tok[0] == "e":
                _, f, idx = tok
                if f == eng:
                    if eng in ("pe", "sp") or not raw:
                        continue
                key = ("e", f)
                val = idx
            else:
                _, res, val = tok
                key = ("d", id(res), res)
            if self.known[eng].get(key[:2], -1) >= val:
                continue
            if key not in need or need[key] < val:
                need[key] = val
        waits = []
        for key, val in need.items():
            self.known[eng][key[:2]] = val
            if key[0] == "e":
                sem, v = self._eng_sem(key[1], val)
                waits.append((sem, v))
            else:
                waits.append((key[2].dsem, val))
        return waits

    def _update(self, tok, key, reads, writes):
        for w in writes:
            w.w = tok
            w.r = {}
        for r in reads:
            r.r[key] = tok

    def op(self, eng, fn, reads=(), writes=()):
        waits = self._collect(eng, reads, writes)
        idx = self.cnt[eng]
        self.cnt[eng] += 1
        tok = ("e", eng, idx)
        self._update(tok, ("e", eng), reads, writes)
        sem, _ = self._eng_sem(eng, idx)
        self.ops[eng].append((fn, waits, sem, 1))

    def dma(self, queue, out, in_, sres, reads=(), writes=()):
        waits = self._collect(queue, reads, writes)
        if sres.dsem is None:
            sres.dsem = self._sem(f"d_{sres.name}")
            self.dma_res.append(sres)
        sres.dcnt += 16
        tok = ("d", sres, sres.dcnt)
        self._update(tok, ("d", id(sres)), reads, writes)
        self.ops[queue].append((lambda e, o=out, i=in_: e.dma_start(out=o, in_=i), waits, sres.dsem, 16))

    def barrier(self):
        for e in self.ENGS:
            waits = []
            for f in self.ENGS:
                if f == e or self.cnt[f] == 0:
                    continue
                idx = self.cnt[f] - 1
                if self.known[e].get(("e", f), -1) >= idx:
                    continue
                self.known[e][("e", f)] = idx
                waits.append(self._eng_sem(f, idx))
            for res in self.dma_res:
                if self.known[e].get(("d", id(res)), -1) >= res.dcnt:
                    continue
                self.known[e][("d", id(res))] = res.dcnt
                waits.append((res.dsem, res.dcnt))
            if waits:
                self.ops[e].append((None, waits, None, 0))

    def replay(self, eng, e):
        for fn, waits, sem, inc in self.ops[eng]:
            for s, v in waits:
                e.wait_ge(s, v)
            if fn is not None:
                ins = fn(e)
                ins.then_inc(sem, inc)


class Ring:
    def __init__(self, items):
        self.items = items
        self.i = 0

    def next(self):
        it = self.items[self.i % len(self.items)]
        self.i += 1
        return it


def _rope_tab(S, rot_dim):
    t = np.arange(S)
    row = (t // 64).astype(np.float64)
    col = (t % 64).astype(np.float64)
    n_freq = rot_dim // 4
    inv = np.exp(-math.log(10000.0) * np.arange(n_freq, dtype=np.float64) / n_freq).astype(np.float32).astype(np.float64)
    ang = np.concatenate([row[:, None] * inv, col[:, None] * inv], axis=-1).astype(np.float32)
    c = np.cos(ang.astype(np.float64)).T
    s = np.sin(ang.astype(np.float64)).T
    C = np.concatenate([c, c], 0)
    Sg = np.concatenate([-s, s], 0)
    return C.astype(np.float32), Sg.astype(np.float32)


def make_consts(S):
    TT = CTX + S
    c32, s32 = _rope_tab(S, 32)
    c64, s64 = _rope_tab(S, 64)

    def full(C, Sg, rep):
        Cf = np.ones((C.shape[0] * rep, TT), np.float32)
        Sf = np.zeros((C.shape[0] * rep, TT), np.float32)
        Cf[:, CTX:] = np.tile(C, (rep, 1))
        Sf[:, CTX:] = np.tile(Sg, (rep, 1))
        return Cf, Sf

    t32c, t32s = full(c32, s32, 4)
    t64c, t64s = full(c64, s64, 2)
    tmc = np.ones((128, TT), np.float32)
    tms = np.zeros((128, TT), np.float32)
    tmc[64:96] = t32c[0:32]
    tms[64:96] = t32s[0:32]
    tab = np.stack([tmc, tms, t32c, t32s, t64c, t64s], 0)
    PT = PPAD0 + CTX + PPAD1 + S + PPAD2
    rc = np.zeros((256, PT), np.float32)
    for gi, w in enumerate(POOL_WINDOWS):
        for (off, L) in ((PPAD0, CTX), (PPAD0 + CTX + PPAD1, S)):
            t = np.arange(L)
            lo = np.clip(t - w // 2, 0, L)
            hi = np.clip(t - w // 2 + w, 0, L)
            rc[gi * 64:(gi + 1) * 64, off:off + L] = (1.0 / (hi - lo).astype(np.float64)).astype(np.float32)[None, :]
    ident = np.eye(128, dtype=np.float32).astype(ml_dtypes.bfloat16)
    a = np.arange(128)[None, :]
    b = np.arange(128)[:, None]
    NEG = -30000.0
    m_next = np.where(b <= a, 0.0, NEG)
    m_prev = np.where(a <= b, 0.0, NEG)
    mask3 = np.concatenate([m_next, np.zeros((128, 128)), m_prev], 1).astype(np.float32).astype(ml_dtypes.bfloat16)
    return dict(tab=tab, rcnt=rc, ident=ident, mask3=mask3)


def build_fused(S):
    phases = {"0", "M", "D", "F1", "F2"}
    TT = CTX + S
    NKT = TT // 128
    NL = S // 128
    PT = PPAD0 + CTX + PPAD1 + S + PPAD2
    qtiles = [(0, CTX)] + [(CTX + i * 512, 512) for i in range(S // 512)]
    nc = bass.Bass("TRN2", target_bir_lowering=False)

    in_names = []

    def din(name, shape, dt=F32):
        in_names.append(name)
        return nc.dram_tensor(name, list(shape), dt, kind="ExternalInput").ap()

    def scr(name, shape, dt, producer, consumers):
        return nc.dram_tensor(name, list(shape), dt).ap()

    x_in = din("xT", [D, S])
    xc_in = din("ctxT", [D, CTX])
    c_in = din("c", [D])
    cc_in = din("c_ctx", [D])
    w_mod = din("w_mod", [DEPTH, D, 3 * D])
    b_mod = din("b_mod", [DEPTH, 3 * D])
    g_pre = din("g_pre", [DEPTH, D])
    g_post = din("g_post", [DEPTH, D])
    w_in = din("w_in", [DEPTH, D, IN_COLS])
    w_pool = din("w_pool", [DEPTH, 4, 64, 64])
    s_pool = din("s_pool", [DEPTH, 256])
    g_cq = din("g_cq", [DEPTH, 192])
    w_uq = din("w_uq", [DEPTH, 192, 384])
    g_ckv = din("g_ckv", [DEPTH, 128])
    w_uk = din("w_uk", [DEPTH, 128, 256])
    w_uv = din("w_uv", [DEPTH, 128, 256])
    lam_in = [din(n, [DEPTH, 32]) for n in ("lam_q1", "lam_k1", "lam_q2", "lam_k2")]
    g_diff = din("g_diff", [DEPTH, 64])
    sink = din("sink", [DEPTH, 4])
    w_br = din("w_br", [DEPTH, 4, 256, D])
    w_out = din("w_out", [DEPTH, D, D])
    tab = din("tab", [6, 128, TT])
    rcnt = din("rcnt", [256, PT])
    ident_in = din("ident", [128, 128], BF16)
    mask3_in = din("mask3", [128, 384], BF16)
    hT = scr("hT_s", [D, TT], BF16, "0", ("M", "D", "F1", "F2"))
    small_s = scr("small_s", [128, 256], F32, "0", ("M", "D", "F1", "F2"))
    ygm_s = scr("ygm_s", [256, TT], BF16, "M", ("F2",))
    ygd_s = scr("ygd_s", [256, TT], BF16, "D", ("F2",))
    ygs_s = scr("ygs_s", [256, TT], BF16, "F1", ("F2",))
    ygp_s = scr("ygp_s", [256, TT], BF16, "F1", ("F2",))
    y_out = nc.dram_tensor("yT", [D, S], F32, kind="ExternalOutput").ap()
    x1T = nc.dram_tensor("x1T_s", [D, S], F32).ap()
    xc1T = nc.dram_tensor("xc1T_s", [D, CTX], F32).ap()
    pool_s = nc.dram_tensor("pool_s", [256, PT], F32).ap()

    stack = ExitStack()
    S_ = Sched(nc, stack)
    stack.enter_context(nc.allow_non_contiguous_dma(reason="tiny per-partition vectors and strided weight views"))

    def sb(name, shape, dt):
        return stack.enter_context(nc.sbuf_tensor("s_" + name, list(shape), dt))

    def slots(name, n, shape, dt):
        return Ring([(sb(f"{name}{i}", shape, dt), Res(f"{name}{i}")) for i in range(n)])

    def has(*ps):
        return any(p in phases for p in ps)

    need_big = 0
    if has("M"):
        need_big = max(need_big, 2 * TT + NKT * 192)
    if has("D"):
        need_big = max(need_big, TT + NKT * 192)
    if has("F1"):
        need_big = max(need_big, TT + NKT * 192)
    if has("F2"):
        need_big = max(need_big, 8 * 4096)
    big = sb("big", [128, max(need_big, 64)], BF16)
    psum = stack.enter_context(nc.psum_tensor("psum", [128, 8, 512], F32))
    bank = [Res(f"bank{i}") for i in range(8)]
    grp = [Res("grp0"), Res("grp1")]
    hT_ring = slots("hTt", 2, [128, 8, 512], BF16)
    stage_ring = slots("stg", 1, [128, 8, 512], F32)
    ones_bf = sb("ones_bf", [128, 128], BF16)
    bones_bf = sb("bones_bf", [128, 128], BF16)
    const_res = Res("consts")
    small = sb("small", [128, 256], F32)
    small_res = Res("small")
    sq_ring = slots("sq", 2, [128, 512], BF16)
    rstd_ring = slots("rstd", 2, [128, 512], F32)
    f32_ring = slots("tf", 6, [128, 512], F32)
    bfx_ring = slots("tb", 4, [128, 512], BF16)
    wA = sb("wA", [128, 8, 1024], BF16)
    wA_res = Res("wA")
    wB = sb("wB", [128, 8, 1024], BF16)
    wB_res = Res("wB")
    if has("0"):
        modv = sb("modv", [128, 24, 2], F32)
        modv_res = Res("modv")
        csb = sb("csb", [128, 8, 2], F32)
        csb_res = Res("csb")
        stab = sb("stab", [128, 160], F32)
        stab_res = Res("stab")
    if has("M", "D", "F1"):
        arF = sb("arF", [128, 8, 512], F32)
        arB = sb("arB", [128, 8, 512], BF16)
        pT_ring = Ring([(arB[:, 2 * i:2 * i + 2, :], Res(f"pT{i}")) for i in range(3)])
        tabs_ring = Ring([(arF[:, 4 + 2 * i:6 + 2 * i, :], Res(f"tabs{i}")) for i in range(2)])
        qT = [(arB[:, 6 + i, :], Res(f"qT{i}")) for i in range(2)]
        sg = arF[:, 0:4, :]
        sg_res = [Res(f"sg{i}") for i in range(4)]
        yraw = sb("yraw", [128, 1, 512], F32)
        yraw_res = [Res("yraw0"), Res("yraw1")]
    yg = sb("yg", [128, 8, 512], BF16)
    yg_res = [Res(f"yg{i}") for i in range(8)]
    if has("F1"):
        ident = sb("ident", [128, 128], BF16)
        mask3 = sb("mask3", [128, 384], BF16)
        pu = sb("pu", [128, 2, 528], F32)
        pu_res = Res("pu")
        prc = sb("prc", [128, 2, 512], F32)
        prc_res = Res("prc")
        pw = sb("pw", [128, 2, 2, 528], F32)
        pw_res = Res("pw")
        pl = sb("pl", [128, 2, 512], BF16)
        pl_res = [Res("pl0"), Res("pl1")]
        wpd = sb("wpd", [128, 2, 128], BF16)
        wpd_res = Res("wpd")
    if has("F2"):
        macc = arF
        macc_res = [Res(f"macc{i}") for i in range(8)]
        mbf = arB
        mbf_res = [Res(f"mbf{i}") for i in range(8)]
        wm_res = Res("wm")

    sp, pe, act, dve, pool = "sp", "pe", "act", "dve", "pool"

    def mm(out, lhsT, rhs, start, stop, reads, writes, tp=None):
        def fn(e, out=out, lhsT=lhsT, rhs=rhs, start=start, stop=stop, tp=tp):
            if tp is None:
                return e.matmul(out, lhsT, rhs, start=start, stop=stop)
            return e.matmul(out, lhsT, rhs, start=start, stop=stop, tile_position=tp)
        S_.op(pe, fn, reads, writes)

    def actf(out, in_, func, reads, writes, scale=1.0, bias=0.0):
        S_.op(act, lambda e, o=out, i=in_, f=func, s=scale, b=bias: e.activation(out=o, in_=i, func=f, bias=b, scale=s),
              reads, writes)

    def tt(eng, out, in0, in1, op, reads, writes):
        S_.op(eng, lambda e, o=out, a=in0, b=in1, p=op: e.tensor_tensor(out=o, in0=a, in1=b, op=p), reads, writes)

    def stt(eng, out, in0, scalar, in1, op0, op1, reads, writes):
        S_.op(eng, lambda e, o=out, a=in0, s=scalar, b=in1, p0=op0, p1=op1:
              e.scalar_tensor_tensor(out=o, in0=a, scalar=s, in1=b, op0=p0, op1=p1), reads, writes)

    def ts(eng, out, in0, s1, s2, op0, op1, reads, writes):
        if s2 is None:
            S_.op(eng, lambda e, o=out, a=in0, s=s1, p0=op0: e.tensor_scalar(out=o, in0=a, scalar1=s, scalar2=None, op0=p0),
                  reads, writes)
        else:
            S_.op(eng, lambda e, o=out, a=in0, s=s1, t=s2, p0=op0, p1=op1:
                  e.tensor_scalar(out=o, in0=a, scalar1=s, scalar2=t, op0=p0, op1=p1), reads, writes)

    def cp(eng, out, in_, reads, writes):
        if eng == act:
            S_.op(eng, lambda e, o=out, i=in_: e.copy(out=o, in_=i), reads, writes)
        else:
            S_.op(eng, lambda e, o=out, i=in_: e.tensor_copy(out=o, in_=i), reads, writes)

    def recip(out, in_, reads, writes):
        S_.op(dve, lambda e, o=out, i=in_: e.reciprocal(out=o, in_=i), reads, writes)

    def memset(eng, ap, val, writes):
        S_.op(eng, lambda e, a=ap, v=val: e.memset(a, v), (), writes)

    def ld(out, in_, res, reads=(), q=sp):
        S_.dma(q, out, in_, res, reads=reads, writes=(res,))

    def st(out, in_, res, q=pool):
        S_.dma(q, out, in_, res, reads=(res,), writes=())

    def rstd_from(ssum_ap, bank_res, n, eps, N, P=128):
        r_t, r_res = rstd_ring.next()
        actf(r_t[0:P, 0:N], ssum_ap, AF.Ln, (bank_res,), (r_res,), scale=1.0 / n, bias=eps)
        actf(r_t[0:P, 0:N], r_t[0:P, 0:N], AF.Exp, (r_res,), (r_res,), scale=-0.5)
        return r_t, r_res


    def finalize_div(acc, o0, N, add_ap=None):
        s0 = 64 - o0
        rs_t, rs_r = f32_ring.next()
        if add_ap is not None:
            ts(dve, rs_t[s0:s0 + 64, 0:N], psum[s0:s0 + 64, acc, 0:N], add_ap, None, ALU.add, None,
               (bank[acc], small_res), (rs_r,))
            recip(rs_t[s0:s0 + 64, 0:N], rs_t[s0:s0 + 64, 0:N], (rs_r,), (rs_r,))
        else:
            recip(rs_t[s0:s0 + 64, 0:N], psum[s0:s0 + 64, acc, 0:N], (bank[acc],), (rs_r,))
        r2_t, r2_r = f32_ring.next()
        S_.dma(pool, r2_t[o0:o0 + 64, 0:N], rs_t[s0:s0 + 64, 0:N], r2_r, reads=(rs_r,), writes=(r2_r,))
        y_t, y_r = f32_ring.next()
        tt(dve, y_t[o0:o0 + 64, 0:N], psum[o0:o0 + 64, acc, 0:N], r2_t[o0:o0 + 64, 0:N], ALU.mult,
           (bank[acc], r2_r), (y_r,))
        return y_t, y_r

    memset(dve, ones_bf[:, :], 1.0, (const_res,))
    memset(dve, bones_bf[:, :], 0.0, (const_res,))
    memset(dve, bones_bf[0:64, 0:64], 1.0, (const_res,))
    memset(dve, bones_bf[64:128, 64:128], 1.0, (const_res,))
    idr, mkr = Res("ident"), Res("mask3")
    if has("F1"):
        ld(ident[:, :], ident_in, idr)
        ld(mask3[:, :], mask3_in, mkr)

    hT_v = hT.rearrange("(k p) t -> p k t", p=128)
    av = small[:, 64:80].rearrange("p (k v) -> p k v", v=2)
    bv = small[:, 96:112].rearrange("p (k v) -> p k v", v=2)
    gv = small[:, 128:144].rearrange("p (k v) -> p k v", v=2)

    def load_hT(t0, N):
        t, r = hT_ring.next()
        ld(t[:, :, 0:N], hT_v[:, :, t0:t0 + N], r)
        return t, r

    def load_tab(idx, t0, N):
        t, r = tabs_ring.next()
        ld(t[:, :, 0:N], tab[idx:idx + 2, :, t0:t0 + N].rearrange("a p t -> p a t"), r)
        return t, r

    def load_w(dst, dres, src_cols_ap, ncols, kdim=8):
        s_t, s_r = stage_ring.next()
        ld(s_t[:, 0:kdim, 0:ncols], src_cols_ap, s_r)
        cp(pool, dst, s_t[:, 0:kdim, 0:ncols], (s_r,), (dres,))

    def win_cols(l, c0, n):
        return w_in[l, :, c0:c0 + n].rearrange("(k p) c -> p k c", p=128)

    def rope(z_ap, zsw_ap, zres, zswres, tabt, tabr, out_ap, out_res, P0, P1, N):
        t1, r1 = f32_ring.next()
        t2, r2 = f32_ring.next()
        tt(dve, t1[P0:P1, 0:N], zsw_ap, tabt[P0:P1, 1, 0:N], ALU.mult, (zswres, tabr), (r1,))
        tt(dve, t2[P0:P1, 0:N], z_ap, tabt[P0:P1, 0, 0:N], ALU.mult, (zres, tabr), (r2,))
        tt(pool, out_ap, t1[P0:P1, 0:N], t2[P0:P1, 0:N], ALU.add, (r1, r2), (out_res,))

    def proj(bank_i, w_t, w_res, c0, M, hT_t, hT_r, N, rows=None):
        for k in range(8):
            mm(psum[0:M, bank_i, 0:N], w_t[:, k, c0:c0 + M], hT_t[:, k, 0:N], k == 0, k == 7,
               (w_res, hT_r), (bank[bank_i],))

    def swap_halves(dst, src, res, ncols, half):
        nh = ncols // (2 * half)
        dv = dst.rearrange("p k (h j i) -> p k h j i", h=nh, j=2, i=half)
        sv = src.rearrange("p k (h j i) -> p k h j i", h=nh, j=2, i=half)
        for k in range(dst.shape[1]):
            cp(pool, dv[:, k, :, 0, :], sv[:, k, :, 1, :], (res,), (res,))
            cp(pool, dv[:, k, :, 1, :], sv[:, k, :, 0, :], (res,), (res,))

    def attn_core(kt_list, N, scale, acc_i, k_ap_fn, q_ap, q_res, v_ap_fn, tp=None, final=True):
        nk = len(kt_list)
        assert nk % 2 == 0
        npair = nk // 2
        base = attn_core.gi
        attn_core.gi += npair

        def qk(pi):
            g = (base + pi) % 2
            for j in range(2):
                kt = kt_list[2 * pi + j]
                mm(psum[:, 2 * g + j, 0:N], k_ap_fn(kt), q_ap, True, True, (q_res,), (grp[g],), tp=tp)

        qk(0)
        for pi in range(npair):
            g = (base + pi) % 2
            if pi + 1 < npair:
                qk(pi + 1)
            p_t, p_r = pT_ring.next()
            actf(p_t[:, :, 0:N], psum[:, 2 * g:2 * g + 2, 0:N], AF.Exp, (grp[g],), (p_r,), scale=scale)
            for j in range(2):
                kt = kt_list[2 * pi + j]
                first = (pi == 0 and j == 0)
                last = final and (pi == npair - 1 and j == 1)
                mm(psum[:, acc_i, 0:N], v_ap_fn(kt), p_t[:, j, 0:N], first, last, (p_r,), (bank[acc_i],))
    attn_core.gi = 0


    for l in range(DEPTH):
        final = (l == DEPTH - 1)
        need_ctx = not final
        lam_init = 0.8 - 0.6 * math.exp(-0.3 * l)
        my_qtiles = qtiles if need_ctx else qtiles[1:]
        x_src = x_in if l == 0 else x1T
        xc_src = xc_in if l == 0 else xc1T
        x_dst = y_out if final else x1T
        xc_dst = xc1T
        xk_src = x_src.rearrange("(k p) t -> p k t", p=128)
        xck_src = xc_src.rearrange("(k p) t -> p k t", p=128)

        def x_tile_ap(t0, N):
            if t0 < CTX:
                return xck_src[:, :, 0:N]
            return xk_src[:, :, t0 - CTX:t0 - CTX + N]

        if has("0"):
            ld(small[:, 0:8], g_pre[l].rearrange("(k p) -> p k", p=128), small_res)
            ld(small[:, 8:16], g_post[l].rearrange("(k p) -> p k", p=128), small_res)
            ld(small[:, 16:40], b_mod[l].rearrange("(k p) -> p k", p=128), small_res)
            ld(small[:, 40:41], g_cq[l, 0:128].rearrange("(k p) -> p k", p=128), small_res)
            ld(small[0:64, 41:42], g_cq[l, 128:192].rearrange("(k p) -> p k", p=64), small_res)
            ld(small[:, 42:43], g_ckv[l].rearrange("(k p) -> p k", p=128), small_res)
            ld(small[0:64, 43:44], g_diff[l].rearrange("(k p) -> p k", p=64), small_res)
            ld(small[64:128, 43:44], g_diff[l].rearrange("(k p) -> p k", p=64), small_res)
            ld(small[:, 44:46], s_pool[l].rearrange("(k p) -> p k", p=128), small_res)
            ld(small[:, 48:52], sink[l].partition_broadcast(128), small_res)
            for i, la in enumerate(lam_in):
                ld(stab[:, i * 32:(i + 1) * 32], la[l].partition_broadcast(128), stab_res)
            ld(csb[:, :, 0], c_in.rearrange("(k p) -> p k", p=128), csb_res)
            ld(csb[:, :, 1], cc_in.rearrange("(k p) -> p k", p=128), csb_res)
            S_.barrier()
            ts(dve, small[:, 43:44], small[:, 43:44], 1.0 - lam_init, None, ALU.mult, None, (small_res,), (small_res,))
            actf(small[:, 48:52], small[:, 48:52], AF.Exp, (small_res,), (small_res,))
            tt(dve, stab[:, 128:160], stab[:, 0:32], stab[:, 32:64], ALU.mult, (stab_res,), (stab_res,))
            S_.op(dve, lambda e: e.reduce_sum(out=small[:, 53:54], in_=stab[:, 128:160], axis=mybir.AxisListType.X),
                  (stab_res,), (small_res,))
            tt(dve, stab[:, 128:160], stab[:, 64:96], stab[:, 96:128], ALU.mult, (stab_res, small_res), (stab_res,))
            S_.op(dve, lambda e: e.reduce_sum(out=small[:, 54:55], in_=stab[:, 128:160], axis=mybir.AxisListType.X),
                  (stab_res,), (small_res,))
            actf(small[:, 53:55], small[:, 53:55], AF.Exp, (small_res,), (small_res,))
            stt(dve, small[:, 52:53], small[:, 54:55], -lam_init, small[:, 53:54], ALU.add, ALU.subtract,
                (small_res,), (small_res,))
            t_t, t_r = f32_ring.next()
            tv = t_t[:, 0:16].rearrange("p (k v) -> p k v", v=2)
            actf(tv, csb[:, :, :], AF.Tanh, (csb_res,), (t_r,), scale=0.5)
            stt(dve, tv, tv, 1.0, csb[:, :, :], ALU.add, ALU.mult, (t_r, csb_res), (t_r,))
            ts(dve, csb[:, :, :], tv, 0.5, None, ALU.mult, None, (t_r,), (csb_res,))
            for j in range(24):
                s_t, s_r = stage_ring.next()
                ld(s_t[:, :, 0:128], w_mod[l, :, j * 128:(j + 1) * 128].rearrange("(k p) c -> p k c", p=128), s_r)
                bi = 6 + (j % 2)
                for k in range(8):
                    mm(psum[:, bi, 0:2], s_t[:, k, 0:128], csb[:, k, :], k == 0, k == 7, (s_r, csb_res), (bank[bi],))
                ts(dve, modv[:, j, :], psum[:, bi, 0:2], small[:, 16 + j:17 + j], None, ALU.add, None,
                   (bank[bi], small_res), (modv_res,))
            for k in range(8):
                ts(dve, av[:, k, :], modv[:, 8 + k, :], 1.0, small[:, k:k + 1], ALU.add, ALU.mult,
                   (modv_res, small_res), (small_res,))
                ts(dve, gv[:, k, :], modv[:, 16 + k, :], small[:, 8 + k:9 + k], None, ALU.mult, None,
                   (modv_res, small_res), (small_res,))
            cp(dve, bv, modv[:, 0:8, :], (modv_res, small_res), (small_res,))
            st(small_s, small[:, :], small_res)
            S_.barrier()
            for (t0, N) in qtiles:
                v = 1 if t0 < CTX else 0
                x_t, x_r = stage_ring.next()
                ld(x_t[:, :, 0:N], x_tile_ap(t0, N), x_r)
                h_t, h_r = hT_ring.next()
                for k in range(8):
                    q_t, q_r = sq_ring.next()
                    actf(q_t[:, 0:N], x_t[:, k, 0:N], AF.Square, (x_r,), (q_r,))
                    mm(psum[:, 6, 0:N], ones_bf[:, :], q_t[:, 0:N], k == 0, k == 7, (q_r, const_res), (bank[6],))
                r_t, r_r = rstd_from(psum[:, 6, 0:N], bank[6], D, EPS, N)
                for k in range(8):
                    f_t, f_r = f32_ring.next()
                    stt(dve, f_t[:, 0:N], x_t[:, k, 0:N], av[:, k, v:v + 1], r_t[:, 0:N], ALU.mult, ALU.mult,
                        (x_r, r_r, small_res), (f_r,))
                    actf(h_t[:, k, 0:N], f_t[:, 0:N], AF.Identity, (f_r, small_res), (h_r,), bias=bv[:, k, v:v + 1])
                st(hT_v[:, :, t0:t0 + N], h_t[:, :, 0:N], h_r)
            S_.barrier()
        else:
            ld(small[:, :], small_s, small_res)
            S_.barrier()

        def gates_chunk(w_t, w_res, c0, sgi, bi, h_t, h_r, N):
            proj(bi, w_t, w_res, c0, 128, h_t, h_r, N)
            th_t, th_r = f32_ring.next()
            actf(th_t[:, 0:N], psum[:, bi, 0:N], AF.Tanh, (bank[bi],), (th_r,), scale=0.5)
            stt(dve, sg[:, sgi, 0:N], th_t[:, 0:N], 1.0, psum[:, bi, 0:N], ALU.add, ALU.mult,
                (th_r, bank[bi]), (sg_res[sgi],))

        if has("M"):
            ygm_v = ygm_s.rearrange("(c p) t -> p c t", p=128)
            KTm = [big[:, i * TT:(i + 1) * TT] for i in range(2)]
            VB = 2 * TT

            def vm_ap(kt, hh):
                base = VB + kt * 192 + hh * 64
                return big[:, base:base + 128]

            Vall = big[:, VB:VB + NKT * 192].rearrange("p (t s d) -> p t s d", t=NKT, s=3, d=64)
            load_w(wA[:, :, 0:128], wA_res, win_cols(l, C_CKV, 128), 128)
            memset(pool, wA[:, :, 128:320], 0.0, (wA_res,))
            load_w(wA[:, :, 192:224], wA_res, win_cols(l, C_KR, 32), 32)
            cp(pool, wA[:, :, 288:304], wA[:, :, 208:224], (wA_res,), (wA_res,))
            cp(pool, wA[:, :, 304:320], wA[:, :, 192:208], (wA_res,), (wA_res,))
            load_w(wA[:, :, 320:512], wA_res, win_cols(l, C_CQ, 192), 192)
            load_w(wA[:, :, 512:768], wA_res, win_cols(l, C_G + 256, 256), 256)
            s_t, s_r = stage_ring.next()
            memset(pool, s_t[:, 0:3, :], 0.0, (s_r,))
            ld(s_t[:, 0, 0:256], w_uk[l], s_r)
            ld(s_t[:, 0, 256:512], w_uv[l], s_r)
            ld(s_t[:, 1, 0:384], w_uq[l, 0:128, :], s_r)
            ld(s_t[0:64, 2, 0:384], w_uq[l, 128:192, :], s_r)
            cp(pool, wB[:, 0:3, 0:512], s_t[:, 0:3, 0:512], (s_r,), (wB_res,))
            cp(pool, wB[:, 3:5, 0:384], wB[:, 1:3, 0:384], (wB_res,), (wB_res,))
            for kk in range(2):
                sv_ = wB[:, 1 + kk, 0:384].rearrange("p (h c) -> p h c", c=96)
                dv_ = wB[:, 3 + kk, 0:384].rearrange("p (h c) -> p h c", c=96)
                cp(pool, dv_[:, :, 64:80], sv_[:, :, 80:96], (wB_res,), (wB_res,))
                cp(pool, dv_[:, :, 80:96], sv_[:, :, 64:80], (wB_res,), (wB_res,))
            for pr in range(2):
                memset(pool, Vall[:, :, 1, :], 1.0, ())
                for (t0, N) in qtiles:
                    h_t, h_r = load_hT(t0, N)
                    tb_t, tb_r = load_tab(0, t0, N)
                    proj(6, wA, wA_res, 0, 128, h_t, h_r, N)
                    q_t, q_r = sq_ring.next()
                    actf(q_t[:, 0:N], psum[:, 6, 0:N], AF.Square, (bank[6],), (q_r,))
                    mm(psum[:, 7, 0:N], ones_bf[:, :], q_t[:, 0:N], True, True, (q_r, const_res), (bank[7],))
                    r_t, r_r = rstd_from(psum[:, 7, 0:N], bank[7], 128, EPS, N)
                    cn_t, cn_r = bfx_ring.next()
                    stt(dve, cn_t[:, 0:N], psum[:, 6, 0:N], small[:, 42:43], r_t[:, 0:N], ALU.mult, ALU.mult,
                        (bank[6], r_r, small_res), (cn_r,))
                    proj(6, wA, wA_res, 128, 96, h_t, h_r, N)
                    proj(7, wA, wA_res, 224, 96, h_t, h_r, N)
                    kr_t, kr_r = bfx_ring.next()
                    rope(psum[64:96, 6, 0:N], psum[64:96, 7, 0:N], bank[6], bank[7], tb_t, tb_r,
                         kr_t[64:96, 0:N], kr_r, 64, 96, N)
                    for hh in range(2):
                        h = 2 * pr + hh
                        cp(pool, KTm[hh][64:96, t0:t0 + N], kr_t[64:96, 0:N], (kr_r,), ())
                        bi = 4 + hh
                        mm(psum[0:64, bi, 0:N], wB[:, 0, h * 64:(h + 1) * 64], cn_t[:, 0:N], True, True,
                           (wB_res, cn_r), (bank[bi],))
                        cp(act, KTm[hh][0:64, t0:t0 + N], psum[0:64, bi, 0:N], (bank[bi],), ())
                    for s in range(N // 128):
                        bi = 6 + (s % 2)
                        kt = t0 // 128 + s
                        mm(psum[:, bi, 0:128], cn_t[:, s * 128:(s + 1) * 128], wB[:, 0, 256 + pr * 128:256 + (pr + 1) * 128],
                           True, True, (cn_r, wB_res), (bank[bi],))
                        cp(dve, Vall[:, kt, 0:3:2, :], psum[:, bi, 0:128].rearrange("p (s d) -> p s d", s=2),
                           (bank[bi],), ())
                S_.barrier()
                for (t0, N) in my_qtiles:
                    kts = list(range(2)) if t0 < CTX else list(range(NKT))
                    h_t, h_r = load_hT(t0, N)
                    tb_t, tb_r = load_tab(0, t0, N)
                    proj(6, wA, wA_res, 320, 128, h_t, h_r, N)
                    proj(7, wA, wA_res, 448, 64, h_t, h_r, N)
                    qa_t, qa_r = sq_ring.next()
                    qb_t, qb_r = sq_ring.next()
                    actf(qa_t[:, 0:N], psum[:, 6, 0:N], AF.Square, (bank[6],), (qa_r,))
                    actf(qb_t[0:64, 0:N], psum[0:64, 7, 0:N], AF.Square, (bank[7],), (qb_r,))
                    mm(psum[:, 5, 0:N], ones_bf[:, :], qa_t[:, 0:N], True, False, (qa_r, const_res), (bank[5],))
                    mm(psum[:, 5, 0:N], ones_bf[0:64, :], qb_t[0:64, 0:N], False, True, (qb_r, const_res), (bank[5],))
                    r_t, r_r = rstd_from(psum[:, 5, 0:N], bank[5], 192, EPS, N)
                    ca_t, ca_r = bfx_ring.next()
                    cb_t, cb_r = bfx_ring.next()
                    stt(dve, ca_t[:, 0:N], psum[:, 6, 0:N], small[:, 40:41], r_t[:, 0:N], ALU.mult, ALU.mult,
                        (bank[6], r_r, small_res), (ca_r,))
                    stt(dve, cb_t[0:64, 0:N], psum[0:64, 7, 0:N], small[0:64, 41:42], r_t[0:64, 0:N], ALU.mult, ALU.mult,
                        (bank[7], r_r, small_res), (cb_r,))
                    gates_chunk(wA, wA_res, 512 + pr * 128, pr, 6, h_t, h_r, N)
                    for hh in range(2):
                        h = 2 * pr + hh
                        for (bi, kb) in ((6, 1), (7, 3)):
                            mm(psum[0:96, bi, 0:N], wB[:, kb, h * 96:(h + 1) * 96], ca_t[:, 0:N], True, False,
                               (wB_res, ca_r), (bank[bi],))
                            mm(psum[0:96, bi, 0:N], wB[0:64, kb + 1, h * 96:(h + 1) * 96], cb_t[0:64, 0:N], False, True,
                               (wB_res, cb_r), (bank[bi],))
                        q_t, q_r = qT[hh]
                        rope(psum[0:96, 6, 0:N], psum[0:96, 7, 0:N], bank[6], bank[7], tb_t, tb_r, q_t[0:96, 0:N], q_r,
                             0, 96, N)
                        acc = 4 + hh
                        attn_core(kts, N, 96 ** -0.5, acc,
                                  lambda kt, hh=hh: KTm[hh][0:96, kt * 128:(kt + 1) * 128],
                                  q_t[0:96, 0:N], q_r, lambda kt, hh=hh: vm_ap(kt, hh))
                        o0 = hh * 64
                        y_t, y_r = finalize_div(acc, o0, N)
                        tt(pool, yg[o0:o0 + 64, 2 + pr, 0:N], y_t[o0:o0 + 64, 0:N], sg[o0:o0 + 64, pr, 0:N], ALU.mult,
                           (y_r, sg_res[pr]), (yg_res[2 + pr],))
                    st(ygm_v[:, pr, t0:t0 + N], yg[:, 2 + pr, 0:N], yg_res[2 + pr])
                S_.barrier()

        if has("D"):
            ygd_v = ygd_s.rearrange("(c p) t -> p c t", p=128)
            KTd = big[:, 0:TT]
            VBd = TT

            def vd_ap(kt, hh):
                base = VBd + kt * 192 + hh * 64
                return big[:, base:base + 128]

            Vd = big[:, VBd:VBd + NKT * 192].rearrange("p (t s d) -> p t s d", t=NKT, s=3, d=64)
            load_w(wA[:, :, 0:256], wA_res, win_cols(l, C_DK, 256), 256)
            swap_halves(wA[:, :, 256:512], wA[:, :, 0:256], wA_res, 256, 16)
            load_w(wA[:, :, 512:768], wA_res, win_cols(l, C_DV, 256), 256)
            load_w(wB[:, :, 0:256], wB_res, win_cols(l, C_DQ, 256), 256)
            swap_halves(wB[:, :, 256:512], wB[:, :, 0:256], wB_res, 256, 16)
            load_w(wB[:, :, 512:768], wB_res, win_cols(l, C_G + 512, 256), 256)
            for pr in range(2):
                memset(pool, Vd[:, :, 1, :], 1.0, ())
                for (t0, N) in qtiles:
                    h_t, h_r = load_hT(t0, N)
                    tb_t, tb_r = load_tab(2, t0, N)
                    proj(6, wA, wA_res, pr * 128, 128, h_t, h_r, N)
                    proj(7, wA, wA_res, 256 + pr * 128, 128, h_t, h_r, N)
                    rope(psum[:, 6, 0:N], psum[:, 7, 0:N], bank[6], bank[7], tb_t, tb_r, KTd[:, t0:t0 + N], Res("kd"),
                         0, 128, N)
                    for s in range(N // 128):
                        bi = 4 + (s % 2)
                        kt = t0 // 128 + s
                        for k in range(8):
                            mm(psum[:, bi, 0:128], h_t[:, k, s * 128:(s + 1) * 128],
                               wA[:, k, 512 + pr * 128:512 + (pr + 1) * 128], k == 0, k == 7, (h_r, wA_res), (bank[bi],))
                        cp(dve, Vd[:, kt, 0:3:2, :], psum[:, bi, 0:128].rearrange("p (s d) -> p s d", s=2),
                           (bank[bi],), ())
                S_.barrier()
                for (t0, N) in my_qtiles:
                    kts = list(range(2)) if t0 < CTX else list(range(NKT))
                    h_t, h_r = load_hT(t0, N)
                    tb_t, tb_r = load_tab(2, t0, N)
                    proj(6, wB, wB_res, pr * 128, 128, h_t, h_r, N)
                    proj(7, wB, wB_res, 256 + pr * 128, 128, h_t, h_r, N)
                    q_t, q_r = qT[0]
                    rope(psum[:, 6, 0:N], psum[:, 7, 0:N], bank[6], bank[7], tb_t, tb_r, q_t[:, 0:N], q_r, 0, 128, N)
                    gates_chunk(wB, wB_res, 512 + pr * 128, pr, 6, h_t, h_r, N)
                    for hh in range(2):
                        o0 = hh * 64
                        ys = []
                        for m in range(2):
                            i = 2 * hh + m
                            acc = 4 + m
                            attn_core(kts, N, 32 ** -0.5, acc,
                                      lambda kt, i=i: KTd[32 * i:32 * i + 32, kt * 128:(kt + 1) * 128],
                                      q_t[32 * i:32 * i + 32, 0:N], q_r, lambda kt, hh=hh: vd_ap(kt, hh), tp=(32 * i, 0))
                            ys.append(finalize_div(acc, o0, N))
                        (y1, r1), (y2, r2) = ys
                        stt(dve, yraw[o0:o0 + 64, 0, 0:N], y2[o0:o0 + 64, 0:N], small[o0:o0 + 64, 52:53],
                            y1[o0:o0 + 64, 0:N], ALU.mult, ALU.add, (r1, r2, small_res), (yraw_res[0],))
                    s_t2, s_r2 = sq_ring.next()
                    actf(s_t2[:, 0:N], yraw[:, 0, 0:N], AF.Square, (yraw_res[0],), (s_r2,))
                    mm(psum[:, 6, 0:N], bones_bf[:, :], s_t2[:, 0:N], True, True, (s_r2, const_res), (bank[6],))
                    r_t, r_r = rstd_from(psum[:, 6, 0:N], bank[6], 64, EPS, N)
                    f_t, f_r = f32_ring.next()
                    stt(dve, f_t[:, 0:N], yraw[:, 0, 0:N], small[:, 43:44], r_t[:, 0:N], ALU.mult, ALU.mult,
                        (yraw_res[0], r_r, small_res), (f_r,))
                    tt(pool, yg[:, 4 + pr, 0:N], f_t[:, 0:N], sg[:, pr, 0:N], ALU.mult, (f_r, sg_res[pr]), (yg_res[4 + pr],))
                    st(ygd_v[:, pr, t0:t0 + N], yg[:, 4 + pr, 0:N], yg_res[4 + pr])
                S_.barrier()

        if has("F1"):
            ygs_v = ygs_s.rearrange("(c p) t -> p c t", p=128)
            ygp_v = ygp_s.rearrange("(c p) t -> p c t", p=128)
            pool_v = pool_s.rearrange("(c p) t -> p c t", p=128)
            rcnt_v = rcnt.rearrange("(c p) t -> p c t", p=128)
            KTs = big[:, 0:TT]
            VBs = TT

            def vs_ap(kt, kv):
                base = VBs + kt * 192 + kv * 64
                return big[:, base:base + 128]

            Vs = big[:, VBs:VBs + NKT * 192].rearrange("p (t s d) -> p t s d", t=NKT, s=3, d=64)
            memset(pool, Vs[:, :, 1, :], 1.0, ())
            memset(pool, pw[:, :, 0, 0:16], 0.0, (pw_res,))
            st(pool_v[:, :, 0:PPAD0], pw[:, :, 0, 0:PPAD0], pw_res)
            st(pool_v[:, :, PPAD0 + CTX:PPAD0 + CTX + PPAD1], pw[:, :, 0, 0:PPAD1], pw_res)
            st(pool_v[:, :, PT - PPAD2:PT], pw[:, :, 0, 0:PPAD2], pw_res)
            load_w(wA[:, :, 0:128], wA_res, win_cols(l, C_SK, 128), 128)
            swap_halves(wA[:, :, 128:256], wA[:, :, 0:128], wA_res, 128, 32)
            load_w(wA[:, :, 256:384], wA_res, win_cols(l, C_SV, 128), 128)
            load_w(wA[:, :, 384:640], wA_res, win_cols(l, C_POOL, 256), 256)
            for g in range(2):
                for kv in range(2):
                    load_w(wB[:, :, g * 128 + kv * 64:g * 128 + kv * 64 + 64], wB_res,
                           win_cols(l, C_SQ + (kv * 2 + g) * 64, 64), 64)
                    load_w(wB[:, :, 512 + g * 128 + kv * 64:512 + g * 128 + kv * 64 + 64], wB_res,
                           win_cols(l, C_G + 768 + (kv * 2 + g) * 64, 64), 64)
            swap_halves(wB[:, :, 256:512], wB[:, :, 0:256], wB_res, 256, 32)
            load_w(wB[:, :, 768:1024], wB_res, win_cols(l, C_G, 256), 256)
            s_t, s_r = stage_ring.next()
            memset(pool, s_t[:, 0:2, 0:128], 0.0, (s_r,))
            for g4 in range(4):
                r0 = (g4 % 2) * 64
                ld(s_t[r0:r0 + 64, g4 // 2, r0:r0 + 64], w_pool[l, g4], s_r)
            cp(pool, wpd[:, :, :], s_t[:, 0:2, 0:128], (s_r,), (wpd_res,))
            for (t0, N) in qtiles:
                h_t, h_r = load_hT(t0, N)
                tb_t, tb_r = load_tab(4, t0, N)
                proj(6, wA, wA_res, 0, 128, h_t, h_r, N)
                proj(7, wA, wA_res, 128, 128, h_t, h_r, N)
                rope(psum[:, 6, 0:N], psum[:, 7, 0:N], bank[6], bank[7], tb_t, tb_r, KTs[:, t0:t0 + N], Res("ks"), 0, 128, N)
                for s in range(N // 128):
                    bi = 4 + (s % 2)
                    kt = t0 // 128 + s
                    for k in range(8):
                        mm(psum[:, bi, 0:128], h_t[:, k, s * 128:(s + 1) * 128], wA[:, k, 256:384], k == 0, k == 7,
                           (h_r, wA_res), (bank[bi],))
                    cp(dve, Vs[:, kt, 0:3:2, :], psum[:, bi, 0:128].rearrange("p (s d) -> p s d", s=2), (bank[bi],), ())
                for c_ in range(2):
                    proj(6 + c_, wA, wA_res, 384 + c_ * 128, 128, h_t, h_r, N)
                    cp(act, pu[:, c_, 0:N], psum[:, 6 + c_, 0:N], (bank[6 + c_],), (pu_res,))
                po = t0 + (PPAD0 if t0 < CTX else PPAD0 + PPAD1)
                st(pool_v[:, :, po:po + N], pu[:, :, 0:N], pu_res)
            S_.barrier()
            for (t0, N) in my_qtiles:
                is_ctx = t0 < CTX
                h_t, h_r = load_hT(t0, N)
                tb_t, tb_r = load_tab(4, t0, N)
                po = t0 + (PPAD0 if is_ctx else PPAD0 + PPAD1)
                ld(pu[:, :, 0:N + 16], pool_v[:, :, po - 8:po + N + 8], pu_res)
                ld(prc[:, :, 0:N], rcnt_v[:, :, po:po + N], prc_res)
                for g in range(2):
                    proj(6, wB, wB_res, g * 128, 128, h_t, h_r, N)
                    proj(7, wB, wB_res, 256 + g * 128, 128, h_t, h_r, N)
                    rope(psum[:, 6, 0:N], psum[:, 7, 0:N], bank[6], bank[7], tb_t, tb_r, qT[g][0][:, 0:N], qT[g][1], 0, 128, N)
                for i4 in range(4):
                    gates_chunk(wB, wB_res, 512 + i4 * 128, i4, 6 + (i4 % 2), h_t, h_r, N)
                for g in range(2):
                    q_t, q_r = qT[g]
                    for kv in range(2):
                        acc = 4 + kv
                        o0 = kv * 64
                        s0 = 64 - o0
                        hq = kv * 2 + g
                        wins = []
                        if not is_ctx:
                            qb0 = (t0 - CTX) // 128
                            for ktl in range(max(qb0 - 1, 0), min(qb0 + 4, NL - 1) + 1):
                                qlo, qhi = max(ktl - 1, qb0), min(ktl + 1, qb0 + 3)
                                wins.append((ktl, (qlo - qb0) * 128, (qhi - qb0 + 1) * 128, (qlo - ktl + 1) * 128))
                        attn_core([0, 1], N, 0.125, acc,
                                  lambda kt, kv=kv: KTs[kv * 64:kv * 64 + 64, kt * 128:(kt + 1) * 128],
                                  q_t[o0:o0 + 64, 0:N], q_r, lambda kt, kv=kv: vs_ap(kt, kv), final=(len(wins) == 0))
                        for wi, (ktl, c0, c1, m0) in enumerate(wins):
                            gq = attn_core.gi % 2
                            attn_core.gi += 1
                            b0 = 2 * gq
                            kt = 2 + ktl
                            mm(psum[:, b0, c0:c1], KTs[o0:o0 + 64, kt * 128:(kt + 1) * 128], q_t[o0:o0 + 64, c0:c1],
                               True, False, (q_r,), (grp[gq],))
                            mm(psum[:, b0, c0:c1], ident[:, :], mask3[:, m0:m0 + (c1 - c0)], False, True, (idr, mkr),
                               (grp[gq],))
                            p_t, p_r = pT_ring.next()
                            actf(p_t[:, 0, c0:c1], psum[:, b0, c0:c1], AF.Exp, (grp[gq],), (p_r,), scale=0.125)
                            mm(psum[:, acc, c0:c1], vs_ap(kt, kv), p_t[:, 0, c0:c1], False, wi == len(wins) - 1,
                               (p_r,), (bank[acc],))
                        y_t, y_r = finalize_div(acc, o0, N, add_ap=small[s0:s0 + 64, 48 + hq:49 + hq])
                        tt(pool, yg[o0:o0 + 64, 6 + g, 0:N], y_t[o0:o0 + 64, 0:N], sg[o0:o0 + 64, g, 0:N], ALU.mult,
                           (y_r, sg_res[g]), (yg_res[6 + g],))
                for g in range(2):
                    st(ygs_v[:, g, t0:t0 + N], yg[:, 6 + g, 0:N], yg_res[6 + g])
                M_ = N + 16
                for c_ in range(2):
                    for half in range(2):
                        gi4 = 2 * c_ + half
                        nlev = gi4 + 1
                        R0, R1 = half * 64, half * 64 + 64
                        tt(pool, pw[R0:R1, c_, 1, 1:M_], pu[R0:R1, c_, 1:M_], pu[R0:R1, c_, 0:M_ - 1], ALU.add,
                           (pu_res,), (pw_res,))
                        for lev in range(2, nlev + 1):
                            lo = (1 << lev) - 1
                            sh = 1 << (lev - 1)
                            src = pw[R0:R1, c_, (lev - 1) % 2, :]
                            tt(pool, pw[R0:R1, c_, lev % 2, lo:M_], src[:, lo:M_], src[:, lo - sh:M_ - sh], ALU.add,
                               (pw_res,), (pw_res,))
                        off = 8 + (1 << (nlev - 1)) - 1
                        f_t, f_r = f32_ring.next()
                        tt(pool, f_t[R0:R1, 0:N], pw[R0:R1, c_, nlev % 2, off:off + N], prc[R0:R1, c_, 0:N], ALU.mult,
                           (pw_res, prc_res), (f_r,))
                        tt(pool, pl[R0:R1, c_, 0:N], f_t[R0:R1, 0:N], pu[R0:R1, c_, 8:8 + N], ALU.subtract,
                           (f_r, pu_res), (pl_res[c_],))
                    bi = 6 + c_
                    mm(psum[:, bi, 0:N], wpd[:, c_, :], pl[:, c_, 0:N], True, True, (wpd_res, pl_res[c_]), (bank[bi],))
                    stt(dve, yg[:, c_, 0:N], psum[:, bi, 0:N], small[:, 44 + c_:45 + c_], sg[:, 2 + c_, 0:N],
                        ALU.mult, ALU.mult, (bank[bi], small_res, sg_res[2 + c_]), (yg_res[c_],))
                    st(ygp_v[:, c_, t0:t0 + N], yg[:, c_, 0:N], yg_res[c_])
            S_.barrier()

        if has("F2"):
            ysrc = {0: ygp_s, 1: ygm_s, 2: ygd_s, 3: ygs_s}
            yv = {r: ysrc[r].rearrange("(c p) t -> p c t", p=128) for r in range(4)}
            wm = big[:, 0:8 * 4096].rearrange("p (k c) -> p k c", k=8)
            for blk in range(8):
                load_w(wm[:, :, blk * 512:(blk + 1) * 512], wm_res, win_cols(l, C_M + blk * 512, 512), 512)
            for ch in range(2):
                s_t, s_r = stage_ring.next()
                for r in range(3):
                    ld(s_t[:, 2 * r:2 * r + 2, :], w_br[l, r, :, ch * 512:(ch + 1) * 512].rearrange("(kc p) c -> p kc c", p=128),
                       s_r)
                for g in range(2):
                    for kv in range(2):
                        r0 = (kv * 2 + g) * 64
                        ld(s_t[kv * 64:kv * 64 + 64, 6 + g, :], w_br[l, 3, r0:r0 + 64, ch * 512:(ch + 1) * 512], s_r)
                cp(pool, wA[:, :, ch * 512:(ch + 1) * 512], s_t[:, :, :], (s_r,), (wA_res,))
                s_t, s_r = stage_ring.next()
                ld(s_t[:, :, :], w_out[l, :, ch * 512:(ch + 1) * 512].rearrange("(k p) c -> p k c", p=128), s_r)
                cp(pool, wB[:, :, ch * 512:(ch + 1) * 512], s_t[:, :, :], (s_r,), (wB_res,))
            for (t0, N) in my_qtiles:
                is_ctx = t0 < CTX
                v = 1 if is_ctx else 0
                h_t, h_r = load_hT(t0, N)
                for r in range(4):
                    for kc in range(2):
                        ld(yg[:, 2 * r + kc, 0:N], yv[r][:, kc, t0:t0 + N], yg_res[2 * r + kc])
                for j in range(8):
                    for r in range(4):
                        bz = 4 + (r % 2)
                        bb = 6 + (r % 2)
                        proj(bz, wm, wm_res, r * 1024 + j * 128, 128, h_t, h_r, N)
                        tm_t, tm_r = bfx_ring.next()
                        actf(tm_t[:, 0:N], psum[:, bz, 0:N], AF.Tanh, (bank[bz],), (tm_r,), scale=0.5)
                        for kc in range(2):
                            mm(psum[:, bb, 0:N], wA[:, 2 * r + kc, j * 128:(j + 1) * 128], yg[:, 2 * r + kc, 0:N],
                               kc == 0, kc == 1, (wA_res, yg_res[2 * r + kc]), (bank[bb],))
                        if r == 0:
                            stt(dve, macc[:, j, 0:N], tm_t[:, 0:N], 1.0, psum[:, bb, 0:N], ALU.add, ALU.mult,
                                (tm_r, bank[bb]), (macc_res[j],))
                        else:
                            f_t, f_r = f32_ring.next()
                            stt(dve, f_t[:, 0:N], tm_t[:, 0:N], 1.0, psum[:, bb, 0:N], ALU.add, ALU.mult,
                                (tm_r, bank[bb]), (f_r,))
                            if r < 3:
                                tt(pool, macc[:, j, 0:N], macc[:, j, 0:N], f_t[:, 0:N], ALU.add, (macc_res[j], f_r),
                                   (macc_res[j],))
                            else:
                                tt(pool, mbf[:, j, 0:N], macc[:, j, 0:N], f_t[:, 0:N], ALU.add, (macc_res[j], f_r),
                                   (mbf_res[j],))
                for j in range(8):
                    bo = 4 + (j % 2)
                    for k in range(8):
                        mm(psum[:, bo, 0:N], wB[:, k, j * 128:(j + 1) * 128], mbf[:, k, 0:N], k == 0, k == 7,
                           (wB_res, mbf_res[k]), (bank[bo],))
                    cp(act, macc[:, j, 0:N], psum[:, bo, 0:N], (bank[bo],), (macc_res[j],))
                    q_t, q_r = sq_ring.next()
                    actf(q_t[:, 0:N], psum[:, bo, 0:N], AF.Square, (bank[bo],), (q_r,))
                    mm(psum[:, 6, 0:N], ones_bf[:, :], q_t[:, 0:N], j == 0, j == 7, (q_r, const_res), (bank[6],))
                r_t, r_r = rstd_from(psum[:, 6, 0:N], bank[6], D, 16 * EPS, N)
                dst_v = (xc_dst if is_ctx else x_dst).rearrange("(k p) t -> p k t", p=128)
                tq = 0 if is_ctx else t0 - CTX
                for j in range(8):
                    xs_t, xs_r = f32_ring.next()
                    ld(xs_t[:, 0:N], x_tile_ap(t0, N)[:, j, :], xs_r)
                    f_t, f_r = f32_ring.next()
                    stt(dve, f_t[:, 0:N], macc[:, j, 0:N], gv[:, j, v:v + 1], r_t[:, 0:N], ALU.mult, ALU.mult,
                        (macc_res[j], r_r, small_res), (f_r,))
                    tt(pool, xs_t[:, 0:N], f_t[:, 0:N], xs_t[:, 0:N], ALU.add, (f_r, xs_r), (xs_r,))
                    st(dst_v[:, j, tq:tq + N], xs_t[:, 0:N], xs_r)
            S_.barrier()

    S_.barrier()
    with nc.Block() as block:
        @block.sync
        def _(e):
            S_.replay("sp", e)

        @block.tensor
        def _(e):
            S_.replay("pe", e)

        @block.scalar
        def _(e):
            S_.replay("act", e)

        @block.vector
        def _(e):
            S_.replay("dve", e)

        @block.gpsimd
        def _(e):
            S_.replay("pool", e)
    stack.close()
    nc.in_names_ = in_names
    nc.sched_ = S_
    return nc


def make_inputs_core(inputs, b, S, consts=None):
    if consts is None:
        consts = make_consts(S)
    m = {
        "xT": np.ascontiguousarray(inputs["x"][b, :S].T),
        "ctxT": np.ascontiguousarray(inputs["ctx"][b].T),
        "c": np.ascontiguousarray(inputs["c"][b]),
        "c_ctx": np.ascontiguousarray(inputs["c_ctx"]),
    }
    for k in ("w_mod", "b_mod", "g_pre", "g_post", "w_in", "w_pool", "s_pool", "g_cq", "w_uq", "g_ckv", "w_uk", "w_uv",
              "lam_q1", "lam_k1", "lam_q2", "lam_k2", "g_diff", "sink", "w_br", "w_out"):
        m[k] = np.ascontiguousarray(inputs[k])
    m.update(consts)
    return m


def kernel(**inputs):
    inputs = {k: np.asarray(v) for k, v in inputs.items()}
    B, S = inputs["x"].shape[0], inputs["x"].shape[1]
    consts = make_consts(S)
    nc = build_fused(S)
    names = nc.in_names_
    in_maps = []
    for b in range(B):
        m = make_inputs_core(inputs, b, S, consts)
        in_maps.append({k: m[k] for k in names})
    res = run_bass_kernel_spmd(nc, in_maps, core_ids=list(range(B)))
    out = np.stack([np.ascontiguousarray(np.asarray(res.results[b]["yT"]).T) for b in range(B)], 0)
    return out.astype(np.float32)
```

```python
import math
from contextlib import ExitStack

import numpy as np
import ml_dtypes

import concourse.bass as bass
import concourse.mybir as mybir
from concourse.bass_utils import run_bass_kernel_spmd

F32 = mybir.dt.float32
BF16 = mybir.dt.bfloat16
AF = mybir.ActivationFunctionType
ALU = mybir.AluOpType

D = 1024
CTX = 256
DEPTH = 2
IN_COLS = 7008
EPS = 1e-6
C_POOL, C_CQ, C_CKV, C_KR, C_DQ, C_DK, C_DV, C_SQ, C_SK, C_SV, C_G, C_M = (
    0, 256, 448, 576, 608, 864, 1120, 1376, 1632, 1760, 1888, 2912)
POOL_WINDOWS = (2, 4, 8, 16)
PPAD0, PPAD1, PPAD2 = 8, 16, 8


class Res:
    __slots__ = ("name", "w", "r", "dsem", "dcnt")

    def __init__(self, name):
        self.name = name
        self.w = None
        self.r = {}
        self.dsem = None
        self.dcnt = 0


class Sched:
    ENGS = ("pe", "act", "dve", "pool", "sp")
    EPOCH = 30000

    def __init__(self, nc, stack):
        self.nc = nc
        self.stack = stack
        self.ops = {e: [] for e in self.ENGS}
        self.cnt = {e: 0 for e in self.ENGS}
        self.sems = {e: [] for e in self.ENGS}
        self.known = {e: {} for e in self.ENGS}
        self.dma_res = []
        self.nsem = 0

    def _sem(self, name):
        self.nsem += 1
        return self.stack.enter_context(self.nc.semaphore(name))

    def _eng_sem(self, eng, idx):
        ep = idx // self.EPOCH
        while len(self.sems[eng]) <= ep:
            self.sems[eng].append(self._sem(f"p_{eng}_{len(self.sems[eng])}"))
        return self.sems[eng][ep], idx % self.EPOCH + 1

    def _collect(self, eng, reads, writes):
        deps = []
        for r in reads:
            if r.w is not None:
                deps.append((r.w, True))
        for w in writes:
            if w.w is not None:
                deps.append((w.w, False))
            for t in w.r.values():
                deps.append((t, False))
        need = {}
        for tok, raw in deps:
            if tok[0] == "e":
                _, f, idx = tok
                if f == eng:
                    if eng in ("pe", "sp") or not raw:
                        continue
                key = ("e", f)
                val = idx
            else:
                _, res, val = tok
                key = ("d", id(res), res)
            if self.known[eng].get(key[:2], -1) >= val:
                continue
            if key not in need or need[key] < val:
                need[key] = val
        waits = []
        for key, val in need.items():
            self.known[eng][key[:2]] = val
            if key[0] == "e":
                sem, v = self._eng_sem(key[1], val)
                waits.append((sem, v))
            else:
                waits.append((key[2].dsem, val))
        return waits

    def _update(self, tok, key, reads, writes):
        for w in writes:
            w.w = tok
            w.r = {}
        for r in reads:
            r.r[key] = tok

    def op(self, eng, fn, reads=(), writes=()):
        waits = self._collect(eng, reads, writes)
        idx = self.cnt[eng]
        self.cnt[eng] += 1
        tok = ("e", eng, idx)
        self._update(tok, ("e", eng), reads, writes)
        sem, _ = self._eng_sem(eng, idx)
        self.ops[eng].append((fn, waits, sem, 1))

    def dma(self, queue, out, in_, sres, reads=(), writes=()):
        waits = self._collect(queue, reads, writes)
        if sres.dsem is None:
            sres.dsem = self._sem(f"d_{sres.name}")
            self.dma_res.append(sres)
        sres.dcnt += 16
        tok = ("d", sres, sres.dcnt)
        self._update(tok, ("d", id(sres)), reads, writes)
        self.ops[queue].append((lambda e, o=out, i=in_: e.dma_start(out=o, in_=i), waits, sres.dsem, 16))

    def barrier(self):
        for e in self.ENGS:
            waits = []
            for f in self.ENGS:
                if f == e or self.cnt[f] == 0:
                    continue
                idx = self.cnt[f] - 1
                if self.known[e].get(("e", f), -1) >= idx:
                    continue
                self.known[e][("e", f)] = idx
                waits.append(self._eng_sem(f, idx))
            for res in self.dma_res:
                if self.known[e].get(("d", id(res)), -1) >= res.dcnt:
                    continue
                self.known[e][("d", id(res))] = res.dcnt
                waits.append((res.dsem, res.dcnt))
            if waits:
                self.ops[e].append((None, waits, None, 0))

    def replay(self, eng, e):
        for fn, waits, sem, inc in self.ops[eng]:
            for s, v in waits:
                e.wait_ge(s, v)
            if fn is not None:
                ins = fn(e)
                ins.then_inc(sem, inc)


class Ring:
    def __init__(self, items):
        self.items = items
        self.i = 0

    def next(self):
        it = self.items[self.i % len(self.items)]
        self.i += 1
        return it


def _rope_tab(S, rot_dim):
    t = np.arange(S)
    row = (t // 64).astype(np.float64)
    col = (t % 64).astype(np.float64)
    n_freq = rot_dim // 4
    inv = np.exp(-math.log(10000.0) * np.arange(n_freq, dtype=np.float64) / n_freq).astype(np.float32).astype(np.float64)
    ang = np.concatenate([row[:, None] * inv, col[:, None] * inv], axis=-1).astype(np.float32)
    c = np.cos(ang.astype(np.float64)).T
    s = np.sin(ang.astype(np.float64)).T
    C = np.concatenate([c, c], 0)
    Sg = np.concatenate([-s, s], 0)
    return C.astype(np.float32), Sg.astype(np.float32)


def make_consts(S):
    TT = CTX + S
    c32, s32 = _rope_tab(S, 32)
    c64, s64 = _rope_tab(S, 64)

    def full(C, Sg, rep):
        Cf = np.ones((C.shape[0] * rep, TT), np.float32)
        Sf = np.zeros((C.shape[0] * rep, TT), np.float32)
        Cf[:, CTX:] = np.tile(C, (rep, 1))
        Sf[:, CTX:] = np.tile(Sg, (rep, 1))
        return Cf, Sf

    t32c, t32s = full(c32, s32, 4)
    t64c, t64s = full(c64, s64, 2)
    tmc = np.ones((128, TT), np.float32)
    tms = np.zeros((128, TT), np.float32)
    tmc[64:96] = t32c[0:32]
    tms[64:96] = t32s[0:32]
    tab = np.stack([tmc, tms, t32c, t32s, t64c, t64s], 0)
    PT = PPAD0 + CTX + PPAD1 + S + PPAD2
    rc = np.zeros((256, PT), np.float32)
    for gi, w in enumerate(POOL_WINDOWS):
        for (off, L) in ((PPAD0, CTX), (PPAD0 + CTX + PPAD1, S)):
            t = np.arange(L)
            lo = np.clip(t - w // 2, 0, L)
            hi = np.clip(t - w // 2 + w, 0, L)
            rc[gi * 64:(gi + 1) * 64, off:off + L] = (1.0 / (hi - lo).astype(np.float64)).astype(np.float32)[None, :]
    ident = np.eye(128, dtype=np.float32).astype(ml_dtypes.bfloat16)
    a = np.arange(128)[None, :]
    b = np.arange(128)[:, None]
    NEG = -30000.0
    m_next = np.where(b <= a, 0.0, NEG)
    m_prev = np.where(a <= b, 0.0, NEG)
    mask3 = np.concatenate([m_next, np.zeros((128, 128)), m_prev], 1).astype(np.float32).astype(ml_dtypes.bfloat16)
    return dict(tab=tab, rcnt=rc, ident=ident, mask3=mask3)


def build_fused(S):
    phases = {"0", "M", "D", "F1", "F2"}
    TT = CTX + S
    NKT = TT // 128
    NL = S // 128
    PT = PPAD0 + CTX + PPAD1 + S + PPAD2
    qtiles = [(0, CTX)] + [(CTX + i * 512, 512) for i in range(S // 512)]
    nc = bass.Bass("TRN2", target_bir_lowering=False)

    in_names = []

    def din(name, shape, dt=F32):
        in_names.append(name)
        return nc.dram_tensor(name, list(shape), dt, kind="ExternalInput").ap()

    def scr(name, shape, dt, producer, consumers):
        return nc.dram_tensor(name, list(shape), dt).ap()

    x_in = din("xT", [D, S])
    xc_in = din("ctxT", [D, CTX])
    c_in = din("c", [D])
    cc_in = din("c_ctx", [D])
    w_mod = din("w_mod", [DEPTH, D, 3 * D])
    b_mod = din("b_mod", [DEPTH, 3 * D])
    g_pre = din("g_pre", [DEPTH, D])
    g_post = din("g_post", [DEPTH, D])
    w_in = din("w_in", [DEPTH, D, IN_COLS])
    w_pool = din("w_pool", [DEPTH, 4, 64, 64])
    s_pool = din("s_pool", [DEPTH, 256])
    g_cq = din("g_cq", [DEPTH, 192])
    w_uq = din("w_uq", [DEPTH, 192, 384])
    g_ckv = din("g_ckv", [DEPTH, 128])
    w_uk = din("w_uk", [DEPTH, 128, 256])
    w_uv = din("w_uv", [DEPTH, 128, 256])
    lam_in = [din(n, [DEPTH, 32]) for n in ("lam_q1", "lam_k1", "lam_q2", "lam_k2")]
    g_diff = din("g_diff", [DEPTH, 64])
    sink = din("sink", [DEPTH, 4])
    w_br = din("w_br", [DEPTH, 4, 256, D])
    w_out = din("w_out", [DEPTH, D, D])
    tab = din("tab", [6, 128, TT])
    rcnt = din("rcnt", [256, PT])
    ident_in = din("ident", [128, 128], BF16)
    mask3_in = din("mask3", [128, 384], BF16)
    hT = scr("hT_s", [D, TT], BF16, "0", ("M", "D", "F1", "F2"))
    small_s = scr("small_s", [128, 256], F32, "0", ("M", "D", "F1", "F2"))
    ygm_s = scr("ygm_s", [256, TT], BF16, "M", ("F2",))
    ygd_s = scr("ygd_s", [256, TT], BF16, "D", ("F2",))
    ygs_s = scr("ygs_s", [256, TT], BF16, "F1", ("F2",))
    ygp_s = scr("ygp_s", [256, TT], BF16, "F1", ("F2",))
    y_out = nc.dram_tensor("yT", [D, S], F32, kind="ExternalOutput").ap()
    x1T = nc.dram_tensor("x1T_s", [D, S], F32).ap()
    xc1T = nc.dram_tensor("xc1T_s", [D, CTX], F32).ap()
    pool_s = nc.dram_tensor("pool_s", [256, PT], F32).ap()

    stack = ExitStack()
    S_ = Sched(nc, stack)
    stack.enter_context(nc.allow_non_contiguous_dma(reason="tiny per-partition vectors and strided weight views"))

    def sb(name, shape, dt):
        return stack.enter_context(nc.sbuf_tensor("s_" + name, list(shape), dt))

    def slots(name, n, shape, dt):
        return Ring([(sb(f"{name}{i}", shape, dt), Res(f"{name}{i}")) for i in range(n)])

    def has(*ps):
        return any(p in phases for p in ps)

    need_big = 0
    if has("M"):
        need_big = max(need_big, 2 * TT + NKT * 192)
    if has("D"):
        need_big = max(need_big, TT + NKT * 192)
    if has("F1"):
        need_big = max(need_big, TT + NKT * 192)
    if has("F2"):
        need_big = max(need_big, 8 * 4096)
    big = sb("big", [128, max(need_big, 64)], BF16)
    psum = stack.enter_context(nc.psum_tensor("psum", [128, 8, 512], F32))
    bank = [Res(f"bank{i}") for i in range(8)]
    grp = [Res("grp0"), Res("grp1")]
    hT_ring = slots("hTt", 2, [128, 8, 512], BF16)
    stage_ring = slots("stg", 1, [128, 8, 512], F32)
    ones_bf = sb("ones_bf", [128, 128], BF16)
    bones_bf = sb("bones_bf", [128, 128], BF16)
    const_res = Res("consts")
    small = sb("small", [128, 256], F32)
    small_res = Res("small")
    sq_ring = slots("sq", 2, [128, 512], BF16)
    rstd_ring = slots("rstd", 1, [128, 512], F32)
    f32_ring = slots("tf", 6, [128, 512], F32)
    bfx_ring = slots("tb", 4, [128, 512], BF16)
    wA = sb("wA", [128, 8, 1024], BF16)
    wA_res = Res("wA")
    wB = sb("wB", [128, 8, 1024], BF16)
    wB_res = Res("wB")
    if has("0"):
        modv = sb("modv", [128, 24, 2], F32)
        modv_res = Res("modv")
        csb = sb("csb", [128, 8, 2], F32)
        csb_res = Res("csb")
        stab = sb("stab", [128, 160], F32)
        stab_res = Res("stab")
    if has("M", "D", "F1"):
        arF = sb("arF", [128, 8, 512], F32)
        arB = sb("arB", [128, 10, 512], BF16)
        pT_ring = Ring([(arB[:, 2 * i:2 * i + 2, :], Res(f"pT{i}")) for i in range(3)])
        tabs_ring = Ring([(arF[:, 4 + 2 * i:6 + 2 * i, :], Res(f"tabs{i}")) for i in range(2)])
        qT = [(arB[:, 6 + i, :], Res(f"qT{i}")) for i in range(2)]
        qm = [arB[:, 6 + i, :] for i in range(4)]
        qm_res = [Res(f"qm{i}") for i in range(4)]
        sg = arF[:, 0:4, :]
        sg_res = [Res(f"sg{i}") for i in range(4)]
        yraw = sb("yraw", [128, 1, 512], F32)
        yraw_res = [Res("yraw0"), Res("yraw1")]
    yg = sb("yg", [128, 8, 512], BF16)
    yg_res = [Res(f"yg{i}") for i in range(8)]
    if has("F1"):
        ident = sb("ident", [128, 128], BF16)
        mask3 = sb("mask3", [128, 384], BF16)
        pu = sb("pu", [128, 2, 528], F32)
        pu_res = Res("pu")
        prc = sb("prc", [128, 2, 512], F32)
        prc_res = Res("prc")
        pw = sb("pw", [128, 2, 2, 528], F32)
        pw_res = Res("pw")
        pl = sb("pl", [128, 2, 512], BF16)
        pl_res = [Res("pl0"), Res("pl1")]
        wpd = sb("wpd", [128, 2, 128], BF16)
        wpd_res = Res("wpd")
    if has("F2"):
        macc = arF
        macc_res = [Res(f"macc{i}") for i in range(8)]
        mbf = arB
        mbf_res = [Res(f"mbf{i}") for i in range(8)]
        wm_res = Res("wm")

    sp, pe, act, dve, pool = "sp", "pe", "act", "dve", "pool"

    def mm(out, lhsT, rhs, start, stop, reads, writes, tp=None):
        def fn(e, out=out, lhsT=lhsT, rhs=rhs, start=start, stop=stop, tp=tp):
            if tp is None:
                return e.matmul(out, lhsT, rhs, start=start, stop=stop)
            return e.matmul(out, lhsT, rhs, start=start, stop=stop, tile_position=tp)
        S_.op(pe, fn, reads, writes)

    def actf(out, in_, func, reads, writes, scale=1.0, bias=0.0):
        S_.op(act, lambda e, o=out, i=in_, f=func, s=scale, b=bias: e.activation(out=o, in_=i, func=f, bias=b, scale=s),
              reads, writes)

    def tt(eng, out, in0, in1, op, reads, writes):
        S_.op(eng, lambda e, o=out, a=in0, b=in1, p=op: e.tensor_tensor(out=o, in0=a, in1=b, op=p), reads, writes)

    def stt(eng, out, in0, scalar, in1, op0, op1, reads, writes):
        S_.op(eng, lambda e, o=out, a=in0, s=scalar, b=in1, p0=op0, p1=op1:
              e.scalar_tensor_tensor(out=o, in0=a, scalar=s, in1=b, op0=p0, op1=p1), reads, writes)

    def ts(eng, out, in0, s1, s2, op0, op1, reads, writes):
        if s2 is None:
            S_.op(eng, lambda e, o=out, a=in0, s=s1, p0=op0: e.tensor_scalar(out=o, in0=a, scalar1=s, scalar2=None, op0=p0),
                  reads, writes)
        else:
            S_.op(eng, lambda e, o=out, a=in0, s=s1, t=s2, p0=op0, p1=op1:
                  e.tensor_scalar(out=o, in0=a, scalar1=s, scalar2=t, op0=p0, op1=p1), reads, writes)

    def cp(eng, out, in_, reads, writes):
        if eng == act:
            S_.op(eng, lambda e, o=out, i=in_: e.copy(out=o, in_=i), reads, writes)
        else:
            S_.op(eng, lambda e, o=out, i=in_: e.tensor_copy(out=o, in_=i), reads, writes)

    def recip(out, in_, reads, writes):
        S_.op(dve, lambda e, o=out, i=in_: e.reciprocal(out=o, in_=i), reads, writes)

    def memset(eng, ap, val, writes):
        S_.op(eng, lambda e, a=ap, v=val: e.memset(a, v), (), writes)

    def ld(out, in_, res, reads=(), q=sp):
        S_.dma(q, out, in_, res, reads=reads, writes=(res,))

    def st(out, in_, res, q=pool):
        S_.dma(q, out, in_, res, reads=(res,), writes=())

    def rstd_from(ssum_ap, bank_res, n, eps, N, P=128):
        r_t, r_res = rstd_ring.next()
        actf(r_t[0:P, 0:N], ssum_ap, AF.Ln, (bank_res,), (r_res,), scale=1.0 / n, bias=eps)
        actf(r_t[0:P, 0:N], r_t[0:P, 0:N], AF.Exp, (r_res,), (r_res,), scale=-0.5)
        return r_t, r_res


    def finalize_div(acc, o0, N, add_ap=None):
        s0 = 64 - o0
        rs_t, rs_r = f32_ring.next()
        if add_ap is not None:
            ts(dve, rs_t[s0:s0 + 64, 0:N], psum[s0:s0 + 64, acc, 0:N], add_ap, None, ALU.add, None,
               (bank[acc], small_res), (rs_r,))
            recip(rs_t[s0:s0 + 64, 0:N], rs_t[s0:s0 + 64, 0:N], (rs_r,), (rs_r,))
        else:
            recip(rs_t[s0:s0 + 64, 0:N], psum[s0:s0 + 64, acc, 0:N], (bank[acc],), (rs_r,))
        r2_t, r2_r = f32_ring.next()
        S_.dma(pool, r2_t[o0:o0 + 64, 0:N], rs_t[s0:s0 + 64, 0:N], r2_r, reads=(rs_r,), writes=(r2_r,))
        y_t, y_r = f32_ring.next()
        tt(dve, y_t[o0:o0 + 64, 0:N], psum[o0:o0 + 64, acc, 0:N], r2_t[o0:o0 + 64, 0:N], ALU.mult,
           (bank[acc], r2_r), (y_r,))
        return y_t, y_r

    memset(dve, ones_bf[:, :], 1.0, (const_res,))
    memset(dve, bones_bf[:, :], 0.0, (const_res,))
    memset(dve, bones_bf[0:64, 0:64], 1.0, (const_res,))
    memset(dve, bones_bf[64:128, 64:128], 1.0, (const_res,))
    idr, mkr = Res("ident"), Res("mask3")
    if has("F1"):
        ld(ident[:, :], ident_in, idr)
        ld(mask3[:, :], mask3_in, mkr)

    hT_v = hT.rearrange("(k p) t -> p k t", p=128)
    av = small[:, 64:80].rearrange("p (k v) -> p k v", v=2)
    bv = small[:, 96:112].rearrange("p (k v) -> p k v", v=2)
    gv = small[:, 128:144].rearrange("p (k v) -> p k v", v=2)

    def load_hT(t0, N):
        t, r = hT_ring.next()
        ld(t[:, :, 0:N], hT_v[:, :, t0:t0 + N], r)
        return t, r

    def load_tab(idx, t0, N):
        t, r = tabs_ring.next()
        ld(t[:, :, 0:N], tab[idx:idx + 2, :, t0:t0 + N].rearrange("a p t -> p a t"), r)
        return t, r

    def tiles_prefetch(tiles, tab_idx):
        def loads(t0, N):
            return (load_hT(t0, N), load_tab(tab_idx, t0, N) if tab_idx is not None else (None, None))
        nxt = None
        for idx, (t0, N) in enumerate(tiles):
            cur = nxt if nxt is not None else loads(t0, N)
            nxt = loads(*tiles[idx + 1]) if idx + 1 < len(tiles) else None
            (h_t, h_r), (tb_t, tb_r) = cur
            yield t0, N, h_t, h_r, tb_t, tb_r

    def load_w(dst, dres, src_cols_ap, ncols, kdim=8):
        s_t, s_r = stage_ring.next()
        ld(s_t[:, 0:kdim, 0:ncols], src_cols_ap, s_r)
        cp(pool, dst, s_t[:, 0:kdim, 0:ncols], (s_r,), (dres,))

    def win_cols(l, c0, n):
        return w_in[l, :, c0:c0 + n].rearrange("(k p) c -> p k c", p=128)

    def rope(z_ap, zsw_ap, zres, zswres, tabt, tabr, out_ap, out_res, P0, P1, N):
        t1, r1 = f32_ring.next()
        t2, r2 = f32_ring.next()
        tt(dve, t1[P0:P1, 0:N], zsw_ap, tabt[P0:P1, 1, 0:N], ALU.mult, (zswres, tabr), (r1,))
        tt(dve, t2[P0:P1, 0:N], z_ap, tabt[P0:P1, 0, 0:N], ALU.mult, (zres, tabr), (r2,))
        tt(pool, out_ap, t1[P0:P1, 0:N], t2[P0:P1, 0:N], ALU.add, (r1, r2), (out_res,))

    def proj(bank_i, w_t, w_res, c0, M, hT_t, hT_r, N, rows=None):
        for k in range(8):
            mm(psum[0:M, bank_i, 0:N], w_t[:, k, c0:c0 + M], hT_t[:, k, 0:N], k == 0, k == 7,
               (w_res, hT_r), (bank[bank_i],))

    def swap_halves(dst, src, res, ncols, half):
        nh = ncols // (2 * half)
        dv = dst.rearrange("p k (h j i) -> p k h j i", h=nh, j=2, i=half)
        sv = src.rearrange("p k (h j i) -> p k h j i", h=nh, j=2, i=half)
        for k in range(dst.shape[1]):
            cp(pool, dv[:, k, :, 0, :], sv[:, k, :, 1, :], (res,), (res,))
            cp(pool, dv[:, k, :, 1, :], sv[:, k, :, 0, :], (res,), (res,))

    def attn_core(kt_list, N, scale, acc_i, k_ap_fn, q_ap, q_res, v_ap_fn, tp=None, final=True):
        nk = len(kt_list)
        assert nk % 2 == 0
        npair = nk // 2
        base = attn_core.gi
        attn_core.gi += npair

        def qk(pi):
            g = (base + pi) % 2
            for j in range(2):
                kt = kt_list[2 * pi + j]
                mm(psum[:, 2 * g + j, 0:N], k_ap_fn(kt), q_ap, True, True, (q_res,), (grp[g],), tp=tp)

        qk(0)
        for pi in range(npair):
            g = (base + pi) % 2
            if pi + 1 < npair:
                qk(pi + 1)
            p_t, p_r = pT_ring.next()
            actf(p_t[:, :, 0:N], psum[:, 2 * g:2 * g + 2, 0:N], AF.Exp, (grp[g],), (p_r,), scale=scale)
            for j in range(2):
                kt = kt_list[2 * pi + j]
                first = (pi == 0 and j == 0)
                last = final and (pi == npair - 1 and j == 1)
                mm(psum[:, acc_i, 0:N], v_ap_fn(kt), p_t[:, j, 0:N], first, last, (p_r,), (bank[acc_i],))
    attn_core.gi = 0


    for l in range(DEPTH):
        final = (l == DEPTH - 1)
        need_ctx = not final
        lam_init = 0.8 - 0.6 * math.exp(-0.3 * l)
        my_qtiles = qtiles if need_ctx else qtiles[1:]
        x_src = x_in if l == 0 else x1T
        xc_src = xc_in if l == 0 else xc1T
        x_dst = y_out if final else x1T
        xc_dst = xc1T
        xk_src = x_src.rearrange("(k p) t -> p k t", p=128)
        xck_src = xc_src.rearrange("(k p) t -> p k t", p=128)

        def x_tile_ap(t0, N):
            if t0 < CTX:
                return xck_src[:, :, 0:N]
            return xk_src[:, :, t0 - CTX:t0 - CTX + N]

        if has("0"):
            ld(small[:, 0:8], g_pre[l].rearrange("(k p) -> p k", p=128), small_res)
            ld(small[:, 8:16], g_post[l].rearrange("(k p) -> p k", p=128), small_res)
            ld(small[:, 16:40], b_mod[l].rearrange("(k p) -> p k", p=128), small_res)
            ld(small[:, 40:41], g_cq[l, 0:128].rearrange("(k p) -> p k", p=128), small_res)
            ld(small[0:64, 41:42], g_cq[l, 128:192].rearrange("(k p) -> p k", p=64), small_res)
            ld(small[:, 42:43], g_ckv[l].rearrange("(k p) -> p k", p=128), small_res)
            ld(small[0:64, 43:44], g_diff[l].rearrange("(k p) -> p k", p=64), small_res)
            ld(small[64:128, 43:44], g_diff[l].rearrange("(k p) -> p k", p=64), small_res)
            ld(small[:, 44:46], s_pool[l].rearrange("(k p) -> p k", p=128), small_res)
            ld(small[:, 48:52], sink[l].partition_broadcast(128), small_res)
            for i, la in enumerate(lam_in):
                ld(stab[:, i * 32:(i + 1) * 32], la[l].partition_broadcast(128), stab_res)
            ld(csb[:, :, 0], c_in.rearrange("(k p) -> p k", p=128), csb_res)
            ld(csb[:, :, 1], cc_in.rearrange("(k p) -> p k", p=128), csb_res)
            S_.barrier()
            ts(dve, small[:, 43:44], small[:, 43:44], 1.0 - lam_init, None, ALU.mult, None, (small_res,), (small_res,))
            actf(small[:, 48:52], small[:, 48:52], AF.Exp, (small_res,), (small_res,))
            tt(dve, stab[:, 128:160], stab[:, 0:32], stab[:, 32:64], ALU.mult, (stab_res,), (stab_res,))
            S_.op(dve, lambda e: e.reduce_sum(out=small[:, 53:54], in_=stab[:, 128:160], axis=mybir.AxisListType.X),
                  (stab_res,), (small_res,))
            tt(dve, stab[:, 128:160], stab[:, 64:96], stab[:, 96:128], ALU.mult, (stab_res, small_res), (stab_res,))
            S_.op(dve, lambda e: e.reduce_sum(out=small[:, 54:55], in_=stab[:, 128:160], axis=mybir.AxisListType.X),
                  (stab_res,), (small_res,))
            actf(small[:, 53:55], small[:, 53:55], AF.Exp, (small_res,), (small_res,))
            stt(dve, small[:, 52:53], small[:, 54:55], -lam_init, small[:, 53:54], ALU.add, ALU.subtract,
                (small_res,), (small_res,))
            t_t, t_r = f32_ring.next()
            tv = t_t[:, 0:16].rearrange("p (k v) -> p k v", v=2)
            actf(tv, csb[:, :, :], AF.Tanh, (csb_res,), (t_r,), scale=0.5)
            stt(dve, tv, tv, 1.0, csb[:, :, :], ALU.add, ALU.mult, (t_r, csb_res), (t_r,))
            ts(dve, csb[:, :, :], tv, 0.5, None, ALU.mult, None, (t_r,), (csb_res,))
            for j in range(24):
                s_t, s_r = stage_ring.next()
                ld(s_t[:, :, 0:128], w_mod[l, :, j * 128:(j + 1) * 128].rearrange("(k p) c -> p k c", p=128), s_r)
                bi = 6 + (j % 2)
                for k in range(8):
                    mm(psum[:, bi, 0:2], s_t[:, k, 0:128], csb[:, k, :], k == 0, k == 7, (s_r, csb_res), (bank[bi],))
                ts(dve, modv[:, j, :], psum[:, bi, 0:2], small[:, 16 + j:17 + j], None, ALU.add, None,
                   (bank[bi], small_res), (modv_res,))
            for k in range(8):
                ts(dve, av[:, k, :], modv[:, 8 + k, :], 1.0, small[:, k:k + 1], ALU.add, ALU.mult,
                   (modv_res, small_res), (small_res,))
                ts(dve, gv[:, k, :], modv[:, 16 + k, :], small[:, 8 + k:9 + k], None, ALU.mult, None,
                   (modv_res, small_res), (small_res,))
            cp(dve, bv, modv[:, 0:8, :], (modv_res, small_res), (small_res,))
            st(small_s, small[:, :], small_res)
            S_.barrier()
            for (t0, N) in qtiles:
                v = 1 if t0 < CTX else 0
                x_t, x_r = stage_ring.next()
                ld(x_t[:, :, 0:N], x_tile_ap(t0, N), x_r)
                h_t, h_r = hT_ring.next()
                for k in range(8):
                    q_t, q_r = sq_ring.next()
                    actf(q_t[:, 0:N], x_t[:, k, 0:N], AF.Square, (x_r,), (q_r,))
                    mm(psum[:, 6, 0:N], ones_bf[:, :], q_t[:, 0:N], k == 0, k == 7, (q_r, const_res), (bank[6],))
                r_t, r_r = rstd_from(psum[:, 6, 0:N], bank[6], D, EPS, N)
                for k in range(8):
                    f_t, f_r = f32_ring.next()
                    stt(dve, f_t[:, 0:N], x_t[:, k, 0:N], av[:, k, v:v + 1], r_t[:, 0:N], ALU.mult, ALU.mult,
                        (x_r, r_r, small_res), (f_r,))
                    actf(h_t[:, k, 0:N], f_t[:, 0:N], AF.Identity, (f_r, small_res), (h_r,), bias=bv[:, k, v:v + 1])
                st(hT_v[:, :, t0:t0 + N], h_t[:, :, 0:N], h_r)
            S_.barrier()
        else:
            ld(small[:, :], small_s, small_res)
            S_.barrier()

        def gates_chunk(w_t, w_res, c0, sgi, bi, h_t, h_r, N):
            proj(bi, w_t, w_res, c0, 128, h_t, h_r, N)
            th_t, th_r = f32_ring.next()
            actf(th_t[:, 0:N], psum[:, bi, 0:N], AF.Tanh, (bank[bi],), (th_r,), scale=0.5)
            stt(dve, sg[:, sgi, 0:N], th_t[:, 0:N], 1.0, psum[:, bi, 0:N], ALU.add, ALU.mult,
                (th_r, bank[bi]), (sg_res[sgi],))

        if has("M"):
            ygm_v = ygm_s.rearrange("(c p) t -> p c t", p=128)
            KTm = [big[:, i * TT:(i + 1) * TT] for i in range(2)]
            VB = 2 * TT

            def vm_ap(kt, hh):
                base = VB + kt * 192 + hh * 64
                return big[:, base:base + 128]

            Vall = big[:, VB:VB + NKT * 192].rearrange("p (t s d) -> p t s d", t=NKT, s=3, d=64)
            load_w(wA[:, :, 0:128], wA_res, win_cols(l, C_CKV, 128), 128)
            memset(pool, wA[:, :, 128:320], 0.0, (wA_res,))
            load_w(wA[:, :, 192:224], wA_res, win_cols(l, C_KR, 32), 32)
            cp(pool, wA[:, :, 288:304], wA[:, :, 208:224], (wA_res,), (wA_res,))
            cp(pool, wA[:, :, 304:320], wA[:, :, 192:208], (wA_res,), (wA_res,))
            load_w(wA[:, :, 320:512], wA_res, win_cols(l, C_CQ, 192), 192)
            load_w(wA[:, :, 512:768], wA_res, win_cols(l, C_G + 256, 256), 256)
            s_t, s_r = stage_ring.next()
            memset(pool, s_t[:, 0:3, :], 0.0, (s_r,))
            ld(s_t[:, 0, 0:256], w_uk[l], s_r)
            ld(s_t[:, 0, 256:512], w_uv[l], s_r)
            ld(s_t[:, 1, 0:384], w_uq[l, 0:128, :], s_r)
            ld(s_t[0:64, 2, 0:384], w_uq[l, 128:192, :], s_r)
            cp(pool, wB[:, 0:3, 0:512], s_t[:, 0:3, 0:512], (s_r,), (wB_res,))
            cp(pool, wB[:, 3:5, 0:384], wB[:, 1:3, 0:384], (wB_res,), (wB_res,))
            for kk in range(2):
                sv_ = wB[:, 1 + kk, 0:384].rearrange("p (h c) -> p h c", c=96)
                dv_ = wB[:, 3 + kk, 0:384].rearrange("p (h c) -> p h c", c=96)
                cp(pool, dv_[:, :, 64:80], sv_[:, :, 80:96], (wB_res,), (wB_res,))
                cp(pool, dv_[:, :, 80:96], sv_[:, :, 64:80], (wB_res,), (wB_res,))
            for pr in range(2):
                memset(pool, Vall[:, :, 1, :], 1.0, ())
                for t0, N, h_t, h_r, tb_t, tb_r in tiles_prefetch(qtiles, 0):
                    is_ctx = t0 < CTX
                    kts = list(range(2)) if t0 < CTX else list(range(NKT))
                    proj(6, wA, wA_res, 0, 128, h_t, h_r, N)
                    q_t, q_r = sq_ring.next()
                    actf(q_t[:, 0:N], psum[:, 6, 0:N], AF.Square, (bank[6],), (q_r,))
                    mm(psum[:, 7, 0:N], ones_bf[:, :], q_t[:, 0:N], True, True, (q_r, const_res), (bank[7],))
                    r_t, r_r = rstd_from(psum[:, 7, 0:N], bank[7], 128, EPS, N)
                    cn_t, cn_r = bfx_ring.next()
                    stt(dve, cn_t[:, 0:N], psum[:, 6, 0:N], small[:, 42:43], r_t[:, 0:N], ALU.mult, ALU.mult,
                        (bank[6], r_r, small_res), (cn_r,))
                    proj(6, wA, wA_res, 128, 96, h_t, h_r, N)
                    proj(7, wA, wA_res, 224, 96, h_t, h_r, N)
                    kr_t, kr_r = bfx_ring.next()
                    rope(psum[64:96, 6, 0:N], psum[64:96, 7, 0:N], bank[6], bank[7], tb_t, tb_r,
                         kr_t[64:96, 0:N], kr_r, 64, 96, N)
                    for hh in range(2):
                        h = 2 * pr + hh
                        cp(pool, KTm[hh][64:96, t0:t0 + N], kr_t[64:96, 0:N], (kr_r,), ())
                        bi = 4 + hh
                        mm(psum[0:64, bi, 0:N], wB[:, 0, h * 64:(h + 1) * 64], cn_t[:, 0:N], True, True,
                           (wB_res, cn_r), (bank[bi],))
                        cp(act, KTm[hh][0:64, t0:t0 + N], psum[0:64, bi, 0:N], (bank[bi],), ())
                    for s in range(N // 128):
                        bi = 6 + (s % 2)
                        kt = t0 // 128 + s
                        mm(psum[:, bi, 0:128], cn_t[:, s * 128:(s + 1) * 128], wB[:, 0, 256 + pr * 128:256 + (pr + 1) * 128],
                           True, True, (cn_r, wB_res), (bank[bi],))
                        cp(dve, Vall[:, kt, 0:3:2, :], psum[:, bi, 0:128].rearrange("p (s d) -> p s d", s=2),
                           (bank[bi],), ())
                S_.barrier()
                for t0, N, h_t, h_r, tb_t, tb_r in tiles_prefetch(my_qtiles, 0):
                    is_ctx = t0 < CTX
                    kts = list(range(2)) if t0 < CTX else list(range(NKT))
                    proj(6, wA, wA_res, 320, 128, h_t, h_r, N)
                    proj(7, wA, wA_res, 448, 64, h_t, h_r, N)
                    qa_t, qa_r = sq_ring.next()
                    qb_t, qb_r = sq_ring.next()
                    actf(qa_t[:, 0:N], psum[:, 6, 0:N], AF.Square, (bank[6],), (qa_r,))
                    actf(qb_t[0:64, 0:N], psum[0:64, 7, 0:N], AF.Square, (bank[7],), (qb_r,))
                    mm(psum[:, 5, 0:N], ones_bf[:, :], qa_t[:, 0:N], True, False, (qa_r, const_res), (bank[5],))
                    mm(psum[:, 5, 0:N], ones_bf[0:64, :], qb_t[0:64, 0:N], False, True, (qb_r, const_res), (bank[5],))
                    r_t, r_r = rstd_from(psum[:, 5, 0:N], bank[5], 192, EPS, N)
                    ca_t, ca_r = bfx_ring.next()
                    cb_t, cb_r = bfx_ring.next()
                    stt(dve, ca_t[:, 0:N], psum[:, 6, 0:N], small[:, 40:41], r_t[:, 0:N], ALU.mult, ALU.mult,
                        (bank[6], r_r, small_res), (ca_r,))
                    stt(dve, cb_t[0:64, 0:N], psum[0:64, 7, 0:N], small[0:64, 41:42], r_t[0:64, 0:N], ALU.mult, ALU.mult,
                        (bank[7], r_r, small_res), (cb_r,))
                    gates_chunk(wA, wA_res, 512 + pr * 128, pr, 6, h_t, h_r, N)
                    for hh in range(2):
                        h = 2 * pr + hh
                        for (bi, kb) in ((6, 1), (7, 3)):
                            mm(psum[0:96, bi, 0:N], wB[:, kb, h * 96:(h + 1) * 96], ca_t[:, 0:N], True, False,
                               (wB_res, ca_r), (bank[bi],))
                            mm(psum[0:96, bi, 0:N], wB[0:64, kb + 1, h * 96:(h + 1) * 96], cb_t[0:64, 0:N], False, True,
                               (wB_res, cb_r), (bank[bi],))
                        q_t, q_r = qT[hh]
                        rope(psum[0:96, 6, 0:N], psum[0:96, 7, 0:N], bank[6], bank[7], tb_t, tb_r, q_t[0:96, 0:N], q_r,
                             0, 96, N)
                    for hh in range(2):
                        q_t, q_r = qT[hh]
                        acc = 4 + hh
                        attn_core(kts, N, 96 ** -0.5, acc,
                                  lambda kt, hh=hh: KTm[hh][0:96, kt * 128:(kt + 1) * 128],
                                  q_t[0:96, 0:N], q_r, lambda kt, hh=hh: vm_ap(kt, hh))
                        o0 = hh * 64
                        y_t, y_r = finalize_div(acc, o0, N)
                        tt(pool, yg[o0:o0 + 64, 2 + pr, 0:N], y_t[o0:o0 + 64, 0:N], sg[o0:o0 + 64, pr, 0:N], ALU.mult,
                           (y_r, sg_res[pr]), (yg_res[2 + pr],))
                    st(ygm_v[:, pr, t0:t0 + N], yg[:, 2 + pr, 0:N], yg_res[2 + pr])
                S_.barrier()

        if has("D"):
            ygd_v = ygd_s.rearrange("(c p) t -> p c t", p=128)
            KTd = big[:, 0:TT]
            VBd = TT

            def vd_ap(kt, hh):
                base = VBd + kt * 192 + hh * 64
                return big[:, base:base + 128]

            Vd = big[:, VBd:VBd + NKT * 192].rearrange("p (t s d) -> p t s d", t=NKT, s=3, d=64)
            load_w(wA[:, :, 0:256], wA_res, win_cols(l, C_DK, 256), 256)
            swap_halves(wA[:, :, 256:512], wA[:, :, 0:256], wA_res, 256, 16)
            load_w(wA[:, :, 512:768], wA_res, win_cols(l, C_DV, 256), 256)
            load_w(wB[:, :, 0:256], wB_res, win_cols(l, C_DQ, 256), 256)
            swap_halves(wB[:, :, 256:512], wB[:, :, 0:256], wB_res, 256, 16)
            load_w(wB[:, :, 512:768], wB_res, win_cols(l, C_G + 512, 256), 256)
            for i in range(4):
                memset(pool, qm[i][:, :], 0.0, (qm_res[i],))
            for pr in range(2):
                memset(pool, Vd[:, :, 1, :], 1.0, ())
                for t0, N, h_t, h_r, tb_t, tb_r in tiles_prefetch(qtiles, 2):
                    is_ctx = t0 < CTX
                    kts = list(range(2)) if t0 < CTX else list(range(NKT))
                    proj(6, wA, wA_res, pr * 128, 128, h_t, h_r, N)
                    proj(7, wA, wA_res, 256 + pr * 128, 128, h_t, h_r, N)
                    rope(psum[:, 6, 0:N], psum[:, 7, 0:N], bank[6], bank[7], tb_t, tb_r, KTd[:, t0:t0 + N], Res("kd"),
                         0, 128, N)
                    for s in range(N // 128):
                        bi = 4 + (s % 2)
                        kt = t0 // 128 + s
                        for k in range(8):
                            mm(psum[:, bi, 0:128], h_t[:, k, s * 128:(s + 1) * 128],
                               wA[:, k, 512 + pr * 128:512 + (pr + 1) * 128], k == 0, k == 7, (h_r, wA_res), (bank[bi],))
                        cp(dve, Vd[:, kt, 0:3:2, :], psum[:, bi, 0:128].rearrange("p (s d) -> p s d", s=2),
                           (bank[bi],), ())
                S_.barrier()
                for t0, N, h_t, h_r, tb_t, tb_r in tiles_prefetch(my_qtiles, 2):
                    is_ctx = t0 < CTX
                    kts = list(range(2)) if t0 < CTX else list(range(NKT))
                    proj(6, wB, wB_res, pr * 128, 128, h_t, h_r, N)
                    proj(7, wB, wB_res, 256 + pr * 128, 128, h_t, h_r, N)
                    t1, r1 = f32_ring.next()
                    t2, r2 = f32_ring.next()
                    tt(dve, t1[:, 0:N], psum[:, 7, 0:N], tb_t[:, 1, 0:N], ALU.mult, (bank[7], tb_r), (r1,))
                    tt(dve, t2[:, 0:N], psum[:, 6, 0:N], tb_t[:, 0, 0:N], ALU.mult, (bank[6], tb_r), (r2,))
                    for i in range(4):
                        tt(pool, qm[i][32 * i:32 * i + 32, 0:N], t1[32 * i:32 * i + 32, 0:N], t2[32 * i:32 * i + 32, 0:N],
                           ALU.add, (r1, r2), (qm_res[i],))
                    gates_chunk(wB, wB_res, 512 + pr * 128, pr, 6, h_t, h_r, N)
                    for hh in range(2):
                        o0 = hh * 64
                        ys = []
                        for m in range(2):
                            i = 2 * hh + m
                            acc = 4 + m
                            attn_core(kts, N, 32 ** -0.5, acc,
                                      lambda kt: KTd[:, kt * 128:(kt + 1) * 128],
                                      qm[i][:, 0:N], qm_res[i], lambda kt, hh=hh: vd_ap(kt, hh))
                            ys.append(finalize_div(acc, o0, N))
                        (y1, r1), (y2, r2) = ys
                        stt(dve, yraw[o0:o0 + 64, 0, 0:N], y2[o0:o0 + 64, 0:N], small[o0:o0 + 64, 52:53],
                            y1[o0:o0 + 64, 0:N], ALU.mult, ALU.add, (r1, r2, small_res), (yraw_res[0],))
                    s_t2, s_r2 = sq_ring.next()
                    actf(s_t2[:, 0:N], yraw[:, 0, 0:N], AF.Square, (yraw_res[0],), (s_r2,))
                    mm(psum[:, 6, 0:N], bones_bf[:, :], s_t2[:, 0:N], True, True, (s_r2, const_res), (bank[6],))
                    r_t, r_r = rstd_from(psum[:, 6, 0:N], bank[6], 64, EPS, N)
                    f_t, f_r = f32_ring.next()
                    stt(dve, f_t[:, 0:N], yraw[:, 0, 0:N], small[:, 43:44], r_t[:, 0:N], ALU.mult, ALU.mult,
                        (yraw_res[0], r_r, small_res), (f_r,))
                    tt(pool, yg[:, 4 + pr, 0:N], f_t[:, 0:N], sg[:, pr, 0:N], ALU.mult, (f_r, sg_res[pr]), (yg_res[4 + pr],))
                    st(ygd_v[:, pr, t0:t0 + N], yg[:, 4 + pr, 0:N], yg_res[4 + pr])
                S_.barrier()

        if has("F1"):
            ygs_v = ygs_s.rearrange("(c p) t -> p c t", p=128)
            ygp_v = ygp_s.rearrange("(c p) t -> p c t", p=128)
            pool_v = pool_s.rearrange("(c p) t -> p c t", p=128)
            rcnt_v = rcnt.rearrange("(c p) t -> p c t", p=128)
            KTs = big[:, 0:TT]
            VBs = TT

            def vs_ap(kt, kv):
                base = VBs + kt * 192 + kv * 64
                return big[:, base:base + 128]

            Vs = big[:, VBs:VBs + NKT * 192].rearrange("p (t s d) -> p t s d", t=NKT, s=3, d=64)
            memset(pool, Vs[:, :, 1, :], 1.0, ())
            memset(pool, pw[:, :, 0, 0:16], 0.0, (pw_res,))
            st(pool_v[:, :, 0:PPAD0], pw[:, :, 0, 0:PPAD0], pw_res)
            st(pool_v[:, :, PPAD0 + CTX:PPAD0 + CTX + PPAD1], pw[:, :, 0, 0:PPAD1], pw_res)
            st(pool_v[:, :, PT - PPAD2:PT], pw[:, :, 0, 0:PPAD2], pw_res)
            load_w(wA[:, :, 0:128], wA_res, win_cols(l, C_SK, 128), 128)
            swap_halves(wA[:, :, 128:256], wA[:, :, 0:128], wA_res, 128, 32)
            load_w(wA[:, :, 256:384], wA_res, win_cols(l, C_SV, 128), 128)
            load_w(wA[:, :, 384:640], wA_res, win_cols(l, C_POOL, 256), 256)
            for g in range(2):
                for kv in range(2):
                    load_w(wB[:, :, g * 128 + kv * 64:g * 128 + kv * 64 + 64], wB_res,
                           win_cols(l, C_SQ + (kv * 2 + g) * 64, 64), 64)
                    load_w(wB[:, :, 512 + g * 128 + kv * 64:512 + g * 128 + kv * 64 + 64], wB_res,
                           win_cols(l, C_G + 768 + (kv * 2 + g) * 64, 64), 64)
            swap_halves(wB[:, :, 256:512], wB[:, :, 0:256], wB_res, 256, 32)
            load_w(wB[:, :, 768:1024], wB_res, win_cols(l, C_G, 256), 256)
            s_t, s_r = stage_ring.next()
            memset(pool, s_t[:, 0:2, 0:128], 0.0, (s_r,))
            for g4 in range(4):
                r0 = (g4 % 2) * 64
                ld(s_t[r0:r0 + 64, g4 // 2, r0:r0 + 64], w_pool[l, g4], s_r)
            cp(pool, wpd[:, :, :], s_t[:, 0:2, 0:128], (s_r,), (wpd_res,))
            for t0, N, h_t, h_r, tb_t, tb_r in tiles_prefetch(qtiles, 4):
                is_ctx = t0 < CTX
                kts = list(range(2)) if t0 < CTX else list(range(NKT))
                proj(6, wA, wA_res, 0, 128, h_t, h_r, N)
                proj(7, wA, wA_res, 128, 128, h_t, h_r, N)
                rope(psum[:, 6, 0:N], psum[:, 7, 0:N], bank[6], bank[7], tb_t, tb_r, KTs[:, t0:t0 + N], Res("ks"), 0, 128, N)
                for s in range(N // 128):
                    bi = 4 + (s % 2)
                    kt = t0 // 128 + s
                    for k in range(8):
                        mm(psum[:, bi, 0:128], h_t[:, k, s * 128:(s + 1) * 128], wA[:, k, 256:384], k == 0, k == 7,
                           (h_r, wA_res), (bank[bi],))
                    cp(dve, Vs[:, kt, 0:3:2, :], psum[:, bi, 0:128].rearrange("p (s d) -> p s d", s=2), (bank[bi],), ())
                for c_ in range(2):
                    proj(6 + c_, wA, wA_res, 384 + c_ * 128, 128, h_t, h_r, N)
                    cp(act, pu[:, c_, 0:N], psum[:, 6 + c_, 0:N], (bank[6 + c_],), (pu_res,))
                po = t0 + (PPAD0 if t0 < CTX else PPAD0 + PPAD1)
                st(pool_v[:, :, po:po + N], pu[:, :, 0:N], pu_res)
            S_.barrier()
            for t0, N, h_t, h_r, tb_t, tb_r in tiles_prefetch(my_qtiles, 4):
                is_ctx = t0 < CTX
                kts = list(range(2)) if t0 < CTX else list(range(NKT))
                po = t0 + (PPAD0 if is_ctx else PPAD0 + PPAD1)
                ld(pu[:, :, 0:N + 16], pool_v[:, :, po - 8:po + N + 8], pu_res)
                ld(prc[:, :, 0:N], rcnt_v[:, :, po:po + N], prc_res)
                for g in range(2):
                    proj(6, wB, wB_res, g * 128, 128, h_t, h_r, N)
                    proj(7, wB, wB_res, 256 + g * 128, 128, h_t, h_r, N)
                    rope(psum[:, 6, 0:N], psum[:, 7, 0:N], bank[6], bank[7], tb_t, tb_r, qT[g][0][:, 0:N], qT[g][1], 0, 128, N)
                for i4 in range(4):
                    gates_chunk(wB, wB_res, 512 + i4 * 128, i4, 6 + (i4 % 2), h_t, h_r, N)
                for g in range(2):
                    q_t, q_r = qT[g]
                    for kv in range(2):
                        acc = 4 + kv
                        o0 = kv * 64
                        s0 = 64 - o0
                        hq = kv * 2 + g
                        wins = []
                        if not is_ctx:
                            qb0 = (t0 - CTX) // 128
                            for ktl in range(max(qb0 - 1, 0), min(qb0 + 4, NL - 1) + 1):
                                qlo, qhi = max(ktl - 1, qb0), min(ktl + 1, qb0 + 3)
                                wins.append((ktl, (qlo - qb0) * 128, (qhi - qb0 + 1) * 128, (qlo - ktl + 1) * 128))
                        attn_core([0, 1], N, 0.125, acc,
                                  lambda kt, kv=kv: KTs[kv * 64:kv * 64 + 64, kt * 128:(kt + 1) * 128],
                                  q_t[o0:o0 + 64, 0:N], q_r, lambda kt, kv=kv: vs_ap(kt, kv), final=(len(wins) == 0))
                        for wi, (ktl, c0, c1, m0) in enumerate(wins):
                            gq = attn_core.gi % 2
                            attn_core.gi += 1
                            b0 = 2 * gq
                            kt = 2 + ktl
                            mm(psum[:, b0, c0:c1], KTs[o0:o0 + 64, kt * 128:(kt + 1) * 128], q_t[o0:o0 + 64, c0:c1],
                               True, False, (q_r,), (grp[gq],))
                            mm(psum[:, b0, c0:c1], ident[:, :], mask3[:, m0:m0 + (c1 - c0)], False, True, (idr, mkr),
                               (grp[gq],))
                            p_t, p_r = pT_ring.next()
                            actf(p_t[:, 0, c0:c1], psum[:, b0, c0:c1], AF.Exp, (grp[gq],), (p_r,), scale=0.125)
                            mm(psum[:, acc, c0:c1], vs_ap(kt, kv), p_t[:, 0, c0:c1], False, wi == len(wins) - 1,
                               (p_r,), (bank[acc],))
                        y_t, y_r = finalize_div(acc, o0, N, add_ap=small[s0:s0 + 64, 48 + hq:49 + hq])
                        tt(pool, yg[o0:o0 + 64, 6 + g, 0:N], y_t[o0:o0 + 64, 0:N], sg[o0:o0 + 64, g, 0:N], ALU.mult,
                           (y_r, sg_res[g]), (yg_res[6 + g],))
                for g in range(2):
                    st(ygs_v[:, g, t0:t0 + N], yg[:, 6 + g, 0:N], yg_res[6 + g])
                M_ = N + 16
                for c_ in range(2):
                    for half in range(2):
                        gi4 = 2 * c_ + half
                        nlev = gi4 + 1
                        R0, R1 = half * 64, half * 64 + 64
                        tt(pool, pw[R0:R1, c_, 1, 1:M_], pu[R0:R1, c_, 1:M_], pu[R0:R1, c_, 0:M_ - 1], ALU.add,
                           (pu_res,), (pw_res,))
                        for lev in range(2, nlev + 1):
                            lo = (1 << lev) - 1
                            sh = 1 << (lev - 1)
                            src = pw[R0:R1, c_, (lev - 1) % 2, :]
                            tt(pool, pw[R0:R1, c_, lev % 2, lo:M_], src[:, lo:M_], src[:, lo - sh:M_ - sh], ALU.add,
                               (pw_res,), (pw_res,))
                        off = 8 + (1 << (nlev - 1)) - 1
                        f_t, f_r = f32_ring.next()
                        tt(pool, f_t[R0:R1, 0:N], pw[R0:R1, c_, nlev % 2, off:off + N], prc[R0:R1, c_, 0:N], ALU.mult,
                           (pw_res, prc_res), (f_r,))
                        tt(pool, pl[R0:R1, c_, 0:N], f_t[R0:R1, 0:N], pu[R0:R1, c_, 8:8 + N], ALU.subtract,
                           (f_r, pu_res), (pl_res[c_],))
                    bi = 6 + c_
                    mm(psum[:, bi, 0:N], wpd[:, c_, :], pl[:, c_, 0:N], True, True, (wpd_res, pl_res[c_]), (bank[bi],))
                    stt(dve, yg[:, c_, 0:N], psum[:, bi, 0:N], small[:, 44 + c_:45 + c_], sg[:, 2 + c_, 0:N],
                        ALU.mult, ALU.mult, (bank[bi], small_res, sg_res[2 + c_]), (yg_res[c_],))
                    st(ygp_v[:, c_, t0:t0 + N], yg[:, c_, 0:N], yg_res[c_])
            S_.barrier()

        if has("F2"):
            ysrc = {0: ygp_s, 1: ygm_s, 2: ygd_s, 3: ygs_s}
            yv = {r: ysrc[r].rearrange("(c p) t -> p c t", p=128) for r in range(4)}
            wm = big[:, 0:8 * 4096].rearrange("p (k c) -> p k c", k=8)
            for blk in range(8):
                load_w(wm[:, :, blk * 512:(blk + 1) * 512], wm_res, win_cols(l, C_M + blk * 512, 512), 512)
            for ch in range(2):
                s_t, s_r = stage_ring.next()
                for r in range(3):
                    ld(s_t[:, 2 * r:2 * r + 2, :], w_br[l, r, :, ch * 512:(ch + 1) * 512].rearrange("(kc p) c -> p kc c", p=128),
                       s_r)
                for g in range(2):
                    for kv in range(2):
                        r0 = (kv * 2 + g) * 64
                        ld(s_t[kv * 64:kv * 64 + 64, 6 + g, :], w_br[l, 3, r0:r0 + 64, ch * 512:(ch + 1) * 512], s_r)
                cp(pool, wA[:, :, ch * 512:(ch + 1) * 512], s_t[:, :, :], (s_r,), (wA_res,))
                s_t, s_r = stage_ring.next()
                ld(s_t[:, :, :], w_out[l, :, ch * 512:(ch + 1) * 512].rearrange("(k p) c -> p k c", p=128), s_r)
                cp(pool, wB[:, :, ch * 512:(ch + 1) * 512], s_t[:, :, :], (s_r,), (wB_res,))
            for t0, N, h_t, h_r, tb_t, tb_r in tiles_prefetch(my_qtiles, None):
                is_ctx = t0 < CTX
                v = 1 if is_ctx else 0
                for r in range(4):
                    for kc in range(2):
                        ld(yg[:, 2 * r + kc, 0:N], yv[r][:, kc, t0:t0 + N], yg_res[2 * r + kc])
                for j in range(8):
                    for r in range(4):
                        bz = 4 + (r % 2)
                        bb = 6 + (r % 2)
                        proj(bz, wm, wm_res, r * 1024 + j * 128, 128, h_t, h_r, N)
                        tm_t, tm_r = bfx_ring.next()
                        actf(tm_t[:, 0:N], psum[:, bz, 0:N], AF.Tanh, (bank[bz],), (tm_r,), scale=0.5)
                        for kc in range(2):
                            mm(psum[:, bb, 0:N], wA[:, 2 * r + kc, j * 128:(j + 1) * 128], yg[:, 2 * r + kc, 0:N],
                               kc == 0, kc == 1, (wA_res, yg_res[2 * r + kc]), (bank[bb],))
                        if r == 0:
                            stt(dve, macc[:, j, 0:N], tm_t[:, 0:N], 1.0, psum[:, bb, 0:N], ALU.add, ALU.mult,
                                (tm_r, bank[bb]), (macc_res[j],))
                        else:
                            f_t, f_r = f32_ring.next()
                            stt(dve, f_t[:, 0:N], tm_t[:, 0:N], 1.0, psum[:, bb, 0:N], ALU.add, ALU.mult,
                                (tm_r, bank[bb]), (f_r,))
                            if r < 3:
                                tt(pool, macc[:, j, 0:N], macc[:, j, 0:N], f_t[:, 0:N], ALU.add, (macc_res[j], f_r),
                                   (macc_res[j],))
                            else:
                                tt(pool, mbf[:, j, 0:N], macc[:, j, 0:N], f_t[:, 0:N], ALU.add, (macc_res[j], f_r),
                                   (mbf_res[j],))
                for j in range(8):
                    bo = 4 + (j % 2)
                    for k in range(8):
                        mm(psum[:, bo, 0:N], wB[:, k, j * 128:(j + 1) * 128], mbf[:, k, 0:N], k == 0, k == 7,
                           (wB_res, mbf_res[k]), (bank[bo],))
                    cp(act, macc[:, j, 0:N], psum[:, bo, 0:N], (bank[bo],), (macc_res[j],))
                    q_t, q_r = sq_ring.next()
                    actf(q_t[:, 0:N], psum[:, bo, 0:N], AF.Square, (bank[bo],), (q_r,))
                    mm(psum[:, 6, 0:N], ones_bf[:, :], q_t[:, 0:N], j == 0, j == 7, (q_r, const_res), (bank[6],))
                r_t, r_r = rstd_from(psum[:, 6, 0:N], bank[6], D, 16 * EPS, N)
                dst_v = (xc_dst if is_ctx else x_dst).rearrange("(k p) t -> p k t", p=128)
                tq = 0 if is_ctx else t0 - CTX
                for j in range(8):
                    xs_t, xs_r = f32_ring.next()
                    ld(xs_t[:, 0:N], x_tile_ap(t0, N)[:, j, :], xs_r)
                    f_t, f_r = f32_ring.next()
                    stt(dve, f_t[:, 0:N], macc[:, j, 0:N], gv[:, j, v:v + 1], r_t[:, 0:N], ALU.mult, ALU.mult,
                        (macc_res[j], r_r, small_res), (f_r,))
                    tt(pool, xs_t[:, 0:N], f_t[:, 0:N], xs_t[:, 0:N], ALU.add, (f_r, xs_r), (xs_r,))
                    st(dst_v[:, j, tq:tq + N], xs_t[:, 0:N], xs_r)
            S_.barrier()

    S_.barrier()
    with nc.Block() as block:
        @block.sync
        def _(e):
            S_.replay("sp", e)

        @block.tensor
        def _(e):
            S_.replay("pe", e)

        @block.scalar
        def _(e):
            S_.replay("act", e)

        @block.vector
        def _(e):
            S_.replay("dve", e)

        @block.gpsimd
        def _(e):
            S_.replay("pool", e)
    stack.close()
    nc.in_names_ = in_names
    nc.sched_ = S_
    return nc


def make_inputs_core(inputs, b, S, consts=None):
    if consts is None:
        consts = make_consts(S)
    m = {
        "xT": np.ascontiguousarray(inputs["x"][b, :S].T),
        "ctxT": np.ascontiguousarray(inputs["ctx"][b].T),
        "c": np.ascontiguousarray(inputs["c"][b]),
        "c_ctx": np.ascontiguousarray(inputs["c_ctx"]),
    }
    for k in ("w_mod", "b_mod", "g_pre", "g_post", "w_in", "w_pool", "s_pool", "g_cq", "w_uq", "g_ckv", "w_uk", "w_uv",
              "lam_q1", "lam_k1", "lam_q2", "lam_k2", "g_diff", "sink", "w_br", "w_out"):
        m[k] = np.ascontiguousarray(inputs[k])
    m.update(consts)
    return m


def kernel(**inputs):
    inputs = {k: np.asarray(v) for k, v in inputs.items()}
    B, S = inputs["x"].shape[0], inputs["x"].shape[1]
    consts = make_consts(S)
    nc = build_fused(S)
    names = nc.in_names_
    in_maps = []
    for b in range(B):
        m = make_inputs_core(inputs, b, S, consts)
        in_maps.append({k: m[k] for k in names})
    res = run_bass_kernel_spmd(nc, in_maps, core_ids=list(range(B)))
    out = np.stack([np.ascontiguousarray(np.asarray(res.results[b]["yT"]).T) for b in range(B)], 0)
    return out.astype(np.float32)
```

```python
import math
from contextlib import ExitStack

import numpy as np
import ml_dtypes

import concourse.bass as bass
import concourse.mybir as mybir
from concourse.bass_utils import run_bass_kernel_spmd

F32 = mybir.dt.float32
BF16 = mybir.dt.bfloat16
AF = mybir.ActivationFunctionType
ALU = mybir.AluOpType

D = 1024
CTX = 256
DEPTH = 2
IN_COLS = 7008
EPS = 1e-6
C_POOL, C_CQ, C_CKV, C_KR, C_DQ, C_DK, C_DV, C_SQ, C_SK, C_SV, C_G, C_M = (
    0, 256, 448, 576, 608, 864, 1120, 1376, 1632, 1760, 1888, 2912)
POOL_WINDOWS = (2, 4, 8, 16)
PPAD0, PPAD1, PPAD2 = 8, 16, 8


class Res:
    __slots__ = ("name", "w", "r", "dsem", "dcnt")

    def __init__(self, name):
        self.name = name
        self.w = None
        self.r = {}
        self.dsem = None
        self.dcnt = 0


class Sched:
    ENGS = ("pe", "act", "dve", "pool", "sp")
    EPOCH = 30000

    def __init__(self, nc, stack):
        self.nc = nc
        self.stack = stack
        self.ops = {e: [] for e in self.ENGS}
        self.cnt = {e: 0 for e in self.ENGS}
        self.sems = {e: [] for e in self.ENGS}
        self.known = {e: {} for e in self.ENGS}
        self.dma_res = []
        self.nsem = 0

    def _sem(self, name):
        self.nsem += 1
        return self.stack.enter_context(self.nc.semaphore(name))

    def _eng_sem(self, eng, idx):
        ep = idx // self.EPOCH
        while len(self.sems[eng]) <= ep:
            self.sems[eng].append(self._sem(f"p_{eng}_{len(self.sems[eng])}"))
        return self.sems[eng][ep], idx % self.EPOCH + 1

    def _collect(self, eng, reads, writes):
        deps = []
        for r in reads:
            if r.w is not None:
                deps.append((r.w, True))
        for w in writes:
            if w.w is not None:
                deps.append((w.w, False))
            for t in w.r.values():
                deps.append((t, False))
        need = {}
        for tok, raw in deps:
            if tok[0] == "e":
                _, f, idx = tok
                if f == eng:
                    if eng in ("pe", "sp") or not raw:
                        continue
                key = ("e", f)
                val = idx
            else:
                _, res, val = tok
                key = ("d", id(res), res)
            if self.known[eng].get(key[:2], -1) >= val:
                continue
            if key not in need or need[key] < val:
                need[key] = val
        waits = []
        for key, val in need.items():
            self.known[eng][key[:2]] = val
            if key[0] == "e":
                sem, v = self._eng_sem(key[1], val)
                waits.append((sem, v))
            else:
                waits.append((key[2].dsem, val))
        return waits

    def _update(self, tok, key, reads, writes):
        for w in writes:
            w.w = tok
            w.r = {}
        for r in reads:
            r.r[key] = tok

    def op(self, eng, fn, reads=(), writes=()):
        waits = self._collect(eng, reads, writes)
        idx = self.cnt[eng]
        self.cnt[eng] += 1
        tok = ("e", eng, idx)
        self._update(tok, ("e", eng), reads, writes)
        sem, _ = self._eng_sem(eng, idx)
        self.ops[eng].append((fn, waits, sem, 1))

    def dma(self, queue, out, in_, sres, reads=(), writes=()):
        waits = self._collect(queue, reads, writes)
        if sres.dsem is None:
            sres.dsem = self._sem(f"d_{sres.name}")
            self.dma_res.append(sres)
        sres.dcnt += 16
        tok = ("d", sres, sres.dcnt)
        self._update(tok, ("d", id(sres)), reads, writes)
        self.ops[queue].append((lambda e, o=out, i=in_: e.dma_start(out=o, in_=i), waits, sres.dsem, 16))

    def barrier(self):
        for e in self.ENGS:
            waits = []
            for f in self.ENGS:
                if f == e or self.cnt[f] == 0:
                    continue
                idx = self.cnt[f] - 1
                if self.known[e].get(("e", f), -1) >= idx:
                    continue
                self.known[e][("e", f)] = idx
                waits.append(self._eng_sem(f, idx))
            for res in self.dma_res:
                if self.known[e].get(("d", id(res)), -1) >= res.dcnt:
                    continue
                self.known[e][("d", id(res))] = res.dcnt
                waits.append((res.dsem, res.dcnt))
            if waits:
                self.ops[e].append((None, waits, None, 0))

    def replay(self, eng, e):
        for fn, waits, sem, inc in self.ops[eng]:
            for s, v in waits:
                e.wait_ge(s, v)
            if fn is not None:
                ins = fn(e)
                ins.then_inc(sem, inc)


class Ring:
    def __init__(self, items):
        self.items = items
        self.i = 0

    def next(self):
        it = self.items[self.i % len(self.items)]
        self.i += 1
        return it


def _rope_tab(S, rot_dim):
    t = np.arange(S)
    row = (t // 64).astype(np.float64)
    col = (t % 64).astype(np.float64)
    n_freq = rot_dim // 4
    inv = np.exp(-math.log(10000.0) * np.arange(n_freq, dtype=np.float64) / n_freq).astype(np.float32).astype(np.float64)
    ang = np.concatenate([row[:, None] * inv, col[:, None] * inv], axis=-1).astype(np.float32)
    c = np.cos(ang.astype(np.float64)).T
    s = np.sin(ang.astype(np.float64)).T
    C = np.concatenate([c, c], 0)
    Sg = np.concatenate([-s, s], 0)
    return C.astype(np.float32), Sg.astype(np.float32)


def make_consts(S):
    TT = CTX + S
    c32, s32 = _rope_tab(S, 32)
    c64, s64 = _rope_tab(S, 64)

    def full(C, Sg, rep):
        Cf = np.ones((C.shape[0] * rep, TT), np.float32)
        Sf = np.zeros((C.shape[0] * rep, TT), np.float32)
        Cf[:, CTX:] = np.tile(C, (rep, 1))
        Sf[:, CTX:] = np.tile(Sg, (rep, 1))
        return Cf, Sf

    t32c, t32s = full(c32, s32, 4)
    t64c, t64s = full(c64, s64, 2)
    tmc = np.ones((128, TT), np.float32)
    tms = np.zeros((128, TT), np.float32)
    tmc[64:96] = t32c[0:32]
    tms[64:96] = t32s[0:32]
    tab = np.stack([tmc, tms, t32c, t32s, t64c, t64s], 0)
    PT = PPAD0 + CTX + PPAD1 + S + PPAD2
    rc = np.zeros((256, PT), np.float32)
    for gi, w in enumerate(POOL_WINDOWS):
        for (off, L) in ((PPAD0, CTX), (PPAD0 + CTX + PPAD1, S)):
            t = np.arange(L)
            lo = np.clip(t - w // 2, 0, L)
            hi = np.clip(t - w // 2 + w, 0, L)
            rc[gi * 64:(gi + 1) * 64, off:off + L] = (1.0 / (hi - lo).astype(np.float64)).astype(np.float32)[None, :]
    ident = np.eye(128, dtype=np.float32).astype(ml_dtypes.bfloat16)
    a = np.arange(128)[None, :]
    b = np.arange(128)[:, None]
    NEG = -30000.0
    m_next = np.where(b <= a, 0.0, NEG)
    m_prev = np.where(a <= b, 0.0, NEG)
    mask3 = np.concatenate([m_next, np.zeros((128, 128)), m_prev], 1).astype(np.float32).astype(ml_dtypes.bfloat16)
    return dict(tab=tab, rcnt=rc, ident=ident, mask3=mask3)


def build_fused(S):
    phases = {"0", "M", "D", "F1", "F2"}
    TT = CTX + S
    NKT = TT // 128
    NL = S // 128
    PT = PPAD0 + CTX + PPAD1 + S + PPAD2
    qtiles = [(0, CTX)] + [(CTX + i * 512, 512) for i in range(S // 512)]
    nc = bass.Bass("TRN2", target_bir_lowering=False)

    in_names = []

    def din(name, shape, dt=F32):
        in_names.append(name)
        return nc.dram_tensor(name, list(shape), dt, kind="ExternalInput").ap()

    def scr(name, shape, dt, producer, consumers):
        return nc.dram_tensor(name, list(shape), dt).ap()

    x_in = din("xT", [D, S])
    xc_in = din("ctxT", [D, CTX])
    c_in = din("c", [D])
    cc_in = din("c_ctx", [D])
    w_mod = din("w_mod", [DEPTH, D, 3 * D])
    b_mod = din("b_mod", [DEPTH, 3 * D])
    g_pre = din("g_pre", [DEPTH, D])
    g_post = din("g_post", [DEPTH, D])
    w_in = din("w_in", [DEPTH, D, IN_COLS])
    w_pool = din("w_pool", [DEPTH, 4, 64, 64])
    s_pool = din("s_pool", [DEPTH, 256])
    g_cq = din("g_cq", [DEPTH, 192])
    w_uq = din("w_uq", [DEPTH, 192, 384])
    g_ckv = din("g_ckv", [DEPTH, 128])
    w_uk = din("w_uk", [DEPTH, 128, 256])
    w_uv = din("w_uv", [DEPTH, 128, 256])
    lam_in = [din(n, [DEPTH, 32]) for n in ("lam_q1", "lam_k1", "lam_q2", "lam_k2")]
    g_diff = din("g_diff", [DEPTH, 64])
    sink = din("sink", [DEPTH, 4])
    w_br = din("w_br", [DEPTH, 4, 256, D])
    w_out = din("w_out", [DEPTH, D, D])
    tab = din("tab", [6, 128, TT])
    rcnt = din("rcnt", [256, PT])
    ident_in = din("ident", [128, 128], BF16)
    mask3_in = din("mask3", [128, 384], BF16)
    hT = scr("hT_s", [D, TT], BF16, "0", ("M", "D", "F1", "F2"))
    small_s = scr("small_s", [128, 256], F32, "0", ("M", "D", "F1", "F2"))
    ygm_s = scr("ygm_s", [256, TT], BF16, "M", ("F2",))
    ygd_s = scr("ygd_s", [256, TT], BF16, "D", ("F2",))
    ygs_s = scr("ygs_s", [256, TT], BF16, "F1", ("F2",))
    ygp_s = scr("ygp_s", [256, TT], BF16, "F1", ("F2",))
    y_out = nc.dram_tensor("yT", [D, S], F32, kind="ExternalOutput").ap()
    x1T = nc.dram_tensor("x1T_s", [D, S], F32).ap()
    xc1T = nc.dram_tensor("xc1T_s", [D, CTX], F32).ap()
    pool_s = nc.dram_tensor("pool_s", [256, PT], F32).ap()

    stack = ExitStack()
    S_ = Sched(nc, stack)
    stack.enter_context(nc.allow_non_contiguous_dma(reason="tiny per-partition vectors and strided weight views"))

    def sb(name, shape, dt):
        return stack.enter_context(nc.sbuf_tensor("s_" + name, list(shape), dt))

    def slots(name, n, shape, dt):
        return Ring([(sb(f"{name}{i}", shape, dt), Res(f"{name}{i}")) for i in range(n)])

    def has(*ps):
        return any(p in phases for p in ps)

    need_big = 0
    if has("M"):
        need_big = max(need_big, 2 * TT + NKT * 192)
    if has("D"):
        need_big = max(need_big, TT + NKT * 192)
    if has("F1"):
        need_big = max(need_big, TT + NKT * 192)
    if has("F2"):
        need_big = max(need_big, 8 * 4096)
    big = sb("big", [128, max(need_big, 64)], BF16)
    psum = stack.enter_context(nc.psum_tensor("psum", [128, 8, 512], F32))
    bank = [Res(f"bank{i}") for i in range(8)]
    grp = [Res("grp0"), Res("grp1")]
    hT_ring = slots("hTt", 2, [128, 8, 512], BF16)
    stage_ring = slots("stg", 1, [128, 8, 512], F32)
    ones_bf = sb("ones_bf", [128, 128], BF16)
    bones_bf = sb("bones_bf", [128, 128], BF16)
    const_res = Res("consts")
    small = sb("small", [128, 256], F32)
    small_res = Res("small")
    sq_ring = slots("sq", 2, [128, 512], BF16)
    rstd_ring = slots("rstd", 1, [128, 512], F32)
    f32_ring = slots("tf", 6, [128, 512], F32)
    bfx_ring = slots("tb", 4, [128, 512], BF16)
    wA = sb("wA", [128, 8, 1024], BF16)
    wA_res = Res("wA")
    wB = sb("wB", [128, 8, 1024], BF16)
    wB_res = Res("wB")
    if has("0"):
        modv = sb("modv", [128, 24, 2], F32)
        modv_res = Res("modv")
        csb = sb("csb", [128, 8, 2], F32)
        csb_res = Res("csb")
        stab = sb("stab", [128, 160], F32)
        stab_res = Res("stab")
    if has("M", "D", "F1"):
        arF = sb("arF", [128, 8, 512], F32)
        arB = sb("arB", [128, 10, 512], BF16)
        pT_ring = Ring([(arB[:, 2 * i:2 * i + 2, :], Res(f"pT{i}")) for i in range(3)])
        tabs_ring = Ring([(arF[:, 4 + 2 * i:6 + 2 * i, :], Res(f"tabs{i}")) for i in range(2)])
        qT = [(arB[:, 6 + i, :], Res(f"qT{i}")) for i in range(2)]
        qm = [arB[:, 6 + i, :] for i in range(4)]
        qm_res = [Res(f"qm{i}") for i in range(4)]
        sg = arF[:, 0:4, :]
        sg_res = [Res(f"sg{i}") for i in range(4)]
        yraw = sb("yraw", [128, 1, 512], F32)
        yraw_res = [Res("yraw0"), Res("yraw1")]
    yg = sb("yg", [128, 8, 512], BF16)
    yg_res = [Res(f"yg{i}") for i in range(8)]
    if has("F1"):
        ident = sb("ident", [128, 128], BF16)
        mask3 = sb("mask3", [128, 384], BF16)
        pu = sb("pu", [128, 2, 528], F32)
        pu_res = Res("pu")
        prc = sb("prc", [128, 2, 512], F32)
        prc_res = Res("prc")
        pw = sb("pw", [128, 2, 2, 528], F32)
        pw_res = Res("pw")
        pl = sb("pl", [128, 2, 512], BF16)
        pl_res = [Res("pl0"), Res("pl1")]
        wpd = sb("wpd", [128, 2, 128], BF16)
        wpd_res = Res("wpd")
    if has("F2"):
        macc = arF
        macc_res = [Res(f"macc{i}") for i in range(8)]
        mbf = arB
        mbf_res = [Res(f"mbf{i}") for i in range(8)]
        wm_res = Res("wm")

    sp, pe, act, dve, pool = "sp", "pe", "act", "dve", "pool"

    def mm(out, lhsT, rhs, start, stop, reads, writes, tp=None):
        def fn(e, out=out, lhsT=lhsT, rhs=rhs, start=start, stop=stop, tp=tp):
            if tp is None:
                return e.matmul(out, lhsT, rhs, start=start, stop=stop)
            return e.matmul(out, lhsT, rhs, start=start, stop=stop, tile_position=tp)
        S_.op(pe, fn, reads, writes)

    def actf(out, in_, func, reads, writes, scale=1.0, bias=0.0):
        S_.op(act, lambda e, o=out, i=in_, f=func, s=scale, b=bias: e.activation(out=o, in_=i, func=f, bias=b, scale=s),
              reads, writes)

    def tt(eng, out, in0, in1, op, reads, writes):
        S_.op(eng, lambda e, o=out, a=in0, b=in1, p=op: e.tensor_tensor(out=o, in0=a, in1=b, op=p), reads, writes)

    def stt(eng, out, in0, scalar, in1, op0, op1, reads, writes):
        S_.op(eng, lambda e, o=out, a=in0, s=scalar, b=in1, p0=op0, p1=op1:
              e.scalar_tensor_tensor(out=o, in0=a, scalar=s, in1=b, op0=p0, op1=p1), reads, writes)

    def ts(eng, out, in0, s1, s2, op0, op1, reads, writes):
        if s2 is None:
            S_.op(eng, lambda e, o=out, a=in0, s=s1, p0=op0: e.tensor_scalar(out=o, in0=a, scalar1=s, scalar2=None, op0=p0),
                  reads, writes)
        else:
            S_.op(eng, lambda e, o=out, a=in0, s=s1, t=s2, p0=op0, p1=op1:
                  e.tensor_scalar(out=o, in0=a, scalar1=s, scalar2=t, op0=p0, op1=p1), reads, writes)

    def cp(eng, out, in_, reads, writes):
        if eng == act:
            S_.op(eng, lambda e, o=out, i=in_: e.copy(out=o, in_=i), reads, writes)
        else:
            S_.op(eng, lambda e, o=out, i=in_: e.tensor_copy(out=o, in_=i), reads, writes)

    def recip(out, in_, reads, writes):
        S_.op(dve, lambda e, o=out, i=in_: e.reciprocal(out=o, in_=i), reads, writes)

    def memset(eng, ap, val, writes):
        S_.op(eng, lambda e, a=ap, v=val: e.memset(a, v), (), writes)

    def ld(out, in_, res, reads=(), q=sp):
        S_.dma(q, out, in_, res, reads=reads, writes=(res,))

    def st(out, in_, res, q=pool):
        S_.dma(q, out, in_, res, reads=(res,), writes=())

    def rstd_from(ssum_ap, bank_res, n, eps, N, P=128):
        r_t, r_res = rstd_ring.next()
        actf(r_t[0:P, 0:N], ssum_ap, AF.Ln, (bank_res,), (r_res,), scale=1.0 / n, bias=eps)
        actf(r_t[0:P, 0:N], r_t[0:P, 0:N], AF.Exp, (r_res,), (r_res,), scale=-0.5)
        return r_t, r_res


    def finalize_div(acc, o0, N, add_ap=None):
        s0 = 64 - o0
        rs_t, rs_r = f32_ring.next()
        if add_ap is not None:
            ts(dve, rs_t[s0:s0 + 64, 0:N], psum[s0:s0 + 64, acc, 0:N], add_ap, None, ALU.add, None,
               (bank[acc], small_res), (rs_r,))
            recip(rs_t[s0:s0 + 64, 0:N], rs_t[s0:s0 + 64, 0:N], (rs_r,), (rs_r,))
        else:
            recip(rs_t[s0:s0 + 64, 0:N], psum[s0:s0 + 64, acc, 0:N], (bank[acc],), (rs_r,))
        r2_t, r2_r = f32_ring.next()
        S_.dma(pool, r2_t[o0:o0 + 64, 0:N], rs_t[s0:s0 + 64, 0:N], r2_r, reads=(rs_r,), writes=(r2_r,))
        y_t, y_r = f32_ring.next()
        tt(dve, y_t[o0:o0 + 64, 0:N], psum[o0:o0 + 64, acc, 0:N], r2_t[o0:o0 + 64, 0:N], ALU.mult,
           (bank[acc], r2_r), (y_r,))
        return y_t, y_r

    memset(dve, ones_bf[:, :], 1.0, (const_res,))
    memset(dve, bones_bf[:, :], 0.0, (const_res,))
    memset(dve, bones_bf[0:64, 0:64], 1.0, (const_res,))
    memset(dve, bones_bf[64:128, 64:128], 1.0, (const_res,))
    idr, mkr = Res("ident"), Res("mask3")
    if has("F1"):
        ld(ident[:, :], ident_in, idr)
        ld(mask3[:, :], mask3_in, mkr)

    hT_v = hT.rearrange("(k p) t -> p k t", p=128)
    av = small[:, 64:80].rearrange("p (k v) -> p k v", v=2)
    bv = small[:, 96:112].rearrange("p (k v) -> p k v", v=2)
    gv = small[:, 128:144].rearrange("p (k v) -> p k v", v=2)

    def load_hT(t0, N):
        t, r = hT_ring.next()
        ld(t[:, :, 0:N], hT_v[:, :, t0:t0 + N], r)
        return t, r

    def load_tab(idx, t0, N):
        t, r = tabs_ring.next()
        ld(t[:, :, 0:N], tab[idx:idx + 2, :, t0:t0 + N].rearrange("a p t -> p a t"), r)
        return t, r

    def tiles_prefetch(tiles, tab_idx):
        def loads(t0, N):
            return (load_hT(t0, N), load_tab(tab_idx, t0, N) if tab_idx is not None else (None, None))
        nxt = None
        for idx, (t0, N) in enumerate(tiles):
            cur = nxt if nxt is not None else loads(t0, N)
            nxt = loads(*tiles[idx + 1]) if idx + 1 < len(tiles) else None
            (h_t, h_r), (tb_t, tb_r) = cur
            yield t0, N, h_t, h_r, tb_t, tb_r

    def load_w(dst, dres, src_cols_ap, ncols, kdim=8):
        s_t, s_r = stage_ring.next()
        ld(s_t[:, 0:kdim, 0:ncols], src_cols_ap, s_r)
        cp(pool, dst, s_t[:, 0:kdim, 0:ncols], (s_r,), (dres,))

    def win_cols(l, c0, n):
        return w_in[l, :, c0:c0 + n].rearrange("(k p) c -> p k c", p=128)

    def rope(z_ap, zsw_ap, zres, zswres, tabt, tabr, out_ap, out_res, P0, P1, N):
        t1, r1 = f32_ring.next()
        t2, r2 = f32_ring.next()
        tt(dve, t1[P0:P1, 0:N], zsw_ap, tabt[P0:P1, 1, 0:N], ALU.mult, (zswres, tabr), (r1,))
        tt(dve, t2[P0:P1, 0:N], z_ap, tabt[P0:P1, 0, 0:N], ALU.mult, (zres, tabr), (r2,))
        tt(pool, out_ap, t1[P0:P1, 0:N], t2[P0:P1, 0:N], ALU.add, (r1, r2), (out_res,))

    def proj(bank_i, w_t, w_res, c0, M, hT_t, hT_r, N, rows=None):
        for k in range(8):
            mm(psum[0:M, bank_i, 0:N], w_t[:, k, c0:c0 + M], hT_t[:, k, 0:N], k == 0, k == 7,
               (w_res, hT_r), (bank[bank_i],))

    def swap_halves(dst, src, res, ncols, half):
        nh = ncols // (2 * half)
        dv = dst.rearrange("p k (h j i) -> p k h j i", h=nh, j=2, i=half)
        sv = src.rearrange("p k (h j i) -> p k h j i", h=nh, j=2, i=half)
        for k in range(dst.shape[1]):
            cp(pool, dv[:, k, :, 0, :], sv[:, k, :, 1, :], (res,), (res,))
            cp(pool, dv[:, k, :, 1, :], sv[:, k, :, 0, :], (res,), (res,))

    def attn_core(kt_list, N, scale, acc_i, k_ap_fn, q_ap, q_res, v_ap_fn, tp=None, final=True):
        nk = len(kt_list)
        assert nk % 2 == 0
        npair = nk // 2
        base = attn_core.gi
        attn_core.gi += npair

        def qk(pi):
            g = (base + pi) % 2
            for j in range(2):
                kt = kt_list[2 * pi + j]
                mm(psum[:, 2 * g + j, 0:N], k_ap_fn(kt), q_ap, True, True, (q_res,), (grp[g],), tp=tp)

        qk(0)
        if npair > 1:
            qk(1)
        for pi in range(npair):
            g = (base + pi) % 2
            p_t, p_r = pT_ring.next()
            actf(p_t[:, :, 0:N], psum[:, 2 * g:2 * g + 2, 0:N], AF.Exp, (grp[g],), (p_r,), scale=scale)
            if pi + 2 < npair:
                qk(pi + 2)
            for j in range(2):
                kt = kt_list[2 * pi + j]
                first = (pi == 0 and j == 0)
                last = final and (pi == npair - 1 and j == 1)
                mm(psum[:, acc_i, 0:N], v_ap_fn(kt), p_t[:, j, 0:N], first, last, (p_r,), (bank[acc_i],))
    attn_core.gi = 0


    for l in range(DEPTH):
        final = (l == DEPTH - 1)
        need_ctx = not final
        lam_init = 0.8 - 0.6 * math.exp(-0.3 * l)
        my_qtiles = qtiles if need_ctx else qtiles[1:]
        x_src = x_in if l == 0 else x1T
        xc_src = xc_in if l == 0 else xc1T
        x_dst = y_out if final else x1T
        xc_dst = xc1T
        xk_src = x_src.rearrange("(k p) t -> p k t", p=128)
        xck_src = xc_src.rearrange("(k p) t -> p k t", p=128)

        def x_tile_ap(t0, N):
            if t0 < CTX:
                return xck_src[:, :, 0:N]
            return xk_src[:, :, t0 - CTX:t0 - CTX + N]

        if has("0"):
            ld(small[:, 0:8], g_pre[l].rearrange("(k p) -> p k", p=128), small_res)
            ld(small[:, 8:16], g_post[l].rearrange("(k p) -> p k", p=128), small_res)
            ld(small[:, 16:40], b_mod[l].rearrange("(k p) -> p k", p=128), small_res)
            ld(small[:, 40:41], g_cq[l, 0:128].rearrange("(k p) -> p k", p=128), small_res)
            ld(small[0:64, 41:42], g_cq[l, 128:192].rearrange("(k p) -> p k", p=64), small_res)
            ld(small[:, 42:43], g_ckv[l].rearrange("(k p) -> p k", p=128), small_res)
            ld(small[0:64, 43:44], g_diff[l].rearrange("(k p) -> p k", p=64), small_res)
            ld(small[64:128, 43:44], g_diff[l].rearrange("(k p) -> p k", p=64), small_res)
            ld(small[:, 44:46], s_pool[l].rearrange("(k p) -> p k", p=128), small_res)
            ld(small[:, 48:52], sink[l].partition_broadcast(128), small_res)
            for i, la in enumerate(lam_in):
                ld(stab[:, i * 32:(i + 1) * 32], la[l].partition_broadcast(128), stab_res)
            ld(csb[:, :, 0], c_in.rearrange("(k p) -> p k", p=128), csb_res)
            ld(csb[:, :, 1], cc_in.rearrange("(k p) -> p k", p=128), csb_res)
            S_.barrier()
            ts(dve, small[:, 43:44], small[:, 43:44], 1.0 - lam_init, None, ALU.mult, None, (small_res,), (small_res,))
            actf(small[:, 48:52], small[:, 48:52], AF.Exp, (small_res,), (small_res,))
            tt(dve, stab[:, 128:160], stab[:, 0:32], stab[:, 32:64], ALU.mult, (stab_res,), (stab_res,))
            S_.op(dve, lambda e: e.reduce_sum(out=small[:, 53:54], in_=stab[:, 128:160], axis=mybir.AxisListType.X),
                  (stab_res,), (small_res,))
            tt(dve, stab[:, 128:160], stab[:, 64:96], stab[:, 96:128], ALU.mult, (stab_res, small_res), (stab_res,))
            S_.op(dve, lambda e: e.reduce_sum(out=small[:, 54:55], in_=stab[:, 128:160], axis=mybir.AxisListType.X),
                  (stab_res,), (small_res,))
            actf(small[:, 53:55], small[:, 53:55], AF.Exp, (small_res,), (small_res,))
            stt(dve, small[:, 52:53], small[:, 54:55], -lam_init, small[:, 53:54], ALU.add, ALU.subtract,
                (small_res,), (small_res,))
            t_t, t_r = f32_ring.next()
            tv = t_t[:, 0:16].rearrange("p (k v) -> p k v", v=2)
            actf(tv, csb[:, :, :], AF.Tanh, (csb_res,), (t_r,), scale=0.5)
            stt(dve, tv, tv, 1.0, csb[:, :, :], ALU.add, ALU.mult, (t_r, csb_res), (t_r,))
            ts(dve, csb[:, :, :], tv, 0.5, None, ALU.mult, None, (t_r,), (csb_res,))
            for j in range(24):
                s_t, s_r = stage_ring.next()
                ld(s_t[:, :, 0:128], w_mod[l, :, j * 128:(j + 1) * 128].rearrange("(k p) c -> p k c", p=128), s_r)
                bi = 6 + (j % 2)
                for k in range(8):
                    mm(psum[:, bi, 0:2], s_t[:, k, 0:128], csb[:, k, :], k == 0, k == 7, (s_r, csb_res), (bank[bi],))
                ts(dve, modv[:, j, :], psum[:, bi, 0:2], small[:, 16 + j:17 + j], None, ALU.add, None,
                   (bank[bi], small_res), (modv_res,))
            for k in range(8):
                ts(dve, av[:, k, :], modv[:, 8 + k, :], 1.0, small[:, k:k + 1], ALU.add, ALU.mult,
                   (modv_res, small_res), (small_res,))
                ts(dve, gv[:, k, :], modv[:, 16 + k, :], small[:, 8 + k:9 + k], None, ALU.mult, None,
                   (modv_res, small_res), (small_res,))
            cp(dve, bv, modv[:, 0:8, :], (modv_res, small_res), (small_res,))
            st(small_s, small[:, :], small_res)
            S_.barrier()
            for (t0, N) in qtiles:
                v = 1 if t0 < CTX else 0
                x_t, x_r = stage_ring.next()
                ld(x_t[:, :, 0:N], x_tile_ap(t0, N), x_r)
                h_t, h_r = hT_ring.next()
                for k in range(8):
                    q_t, q_r = sq_ring.next()
                    actf(q_t[:, 0:N], x_t[:, k, 0:N], AF.Square, (x_r,), (q_r,))
                    mm(psum[:, 6, 0:N], ones_bf[:, :], q_t[:, 0:N], k == 0, k == 7, (q_r, const_res), (bank[6],))
                r_t, r_r = rstd_from(psum[:, 6, 0:N], bank[6], D, EPS, N)
                for k in range(8):
                    f_t, f_r = f32_ring.next()
                    stt(dve, f_t[:, 0:N], x_t[:, k, 0:N], av[:, k, v:v + 1], r_t[:, 0:N], ALU.mult, ALU.mult,
                        (x_r, r_r, small_res), (f_r,))
                    actf(h_t[:, k, 0:N], f_t[:, 0:N], AF.Identity, (f_r, small_res), (h_r,), bias=bv[:, k, v:v + 1])
                st(hT_v[:, :, t0:t0 + N], h_t[:, :, 0:N], h_r)
            S_.barrier()
        else:
            ld(small[:, :], small_s, small_res)
            S_.barrier()

        def gates_chunk(w_t, w_res, c0, sgi, bi, h_t, h_r, N):
            proj(bi, w_t, w_res, c0, 128, h_t, h_r, N)
            th_t, th_r = f32_ring.next()
            actf(th_t[:, 0:N], psum[:, bi, 0:N], AF.Tanh, (bank[bi],), (th_r,), scale=0.5)
            stt(dve, sg[:, sgi, 0:N], th_t[:, 0:N], 1.0, psum[:, bi, 0:N], ALU.add, ALU.mult,
                (th_r, bank[bi]), (sg_res[sgi],))

        if has("M"):
            ygm_v = ygm_s.rearrange("(c p) t -> p c t", p=128)
            KTm = [big[:, i * TT:(i + 1) * TT] for i in range(2)]
            VB = 2 * TT

            def vm_ap(kt, hh):
                base = VB + kt * 192 + hh * 64
                return big[:, base:base + 128]

            Vall = big[:, VB:VB + NKT * 192].rearrange("p (t s d) -> p t s d", t=NKT, s=3, d=64)
            load_w(wA[:, :, 0:128], wA_res, win_cols(l, C_CKV, 128), 128)
            memset(pool, wA[:, :, 128:320], 0.0, (wA_res,))
            load_w(wA[:, :, 192:224], wA_res, win_cols(l, C_KR, 32), 32)
            cp(pool, wA[:, :, 288:304], wA[:, :, 208:224], (wA_res,), (wA_res,))
            cp(pool, wA[:, :, 304:320], wA[:, :, 192:208], (wA_res,), (wA_res,))
            load_w(wA[:, :, 320:512], wA_res, win_cols(l, C_CQ, 192), 192)
            load_w(wA[:, :, 512:768], wA_res, win_cols(l, C_G + 256, 256), 256)
            s_t, s_r = stage_ring.next()
            memset(pool, s_t[:, 0:3, :], 0.0, (s_r,))
            ld(s_t[:, 0, 0:256], w_uk[l], s_r)
            ld(s_t[:, 0, 256:512], w_uv[l], s_r)
            ld(s_t[:, 1, 0:384], w_uq[l, 0:128, :], s_r)
            ld(s_t[0:64, 2, 0:384], w_uq[l, 128:192, :], s_r)
            cp(pool, wB[:, 0:3, 0:512], s_t[:, 0:3, 0:512], (s_r,), (wB_res,))
            cp(pool, wB[:, 3:5, 0:384], wB[:, 1:3, 0:384], (wB_res,), (wB_res,))
            for kk in range(2):
                sv_ = wB[:, 1 + kk, 0:384].rearrange("p (h c) -> p h c", c=96)
                dv_ = wB[:, 3 + kk, 0:384].rearrange("p (h c) -> p h c", c=96)
                cp(pool, dv_[:, :, 64:80], sv_[:, :, 80:96], (wB_res,), (wB_res,))
                cp(pool, dv_[:, :, 80:96], sv_[:, :, 64:80], (wB_res,), (wB_res,))
            for pr in range(2):
                memset(pool, Vall[:, :, 1, :], 1.0, ())
                for t0, N, h_t, h_r, tb_t, tb_r in tiles_prefetch(qtiles, 0):
                    is_ctx = t0 < CTX
                    kts = list(range(2)) if t0 < CTX else list(range(NKT))
                    proj(6, wA, wA_res, 0, 128, h_t, h_r, N)
                    q_t, q_r = sq_ring.next()
                    actf(q_t[:, 0:N], psum[:, 6, 0:N], AF.Square, (bank[6],), (q_r,))
                    mm(psum[:, 7, 0:N], ones_bf[:, :], q_t[:, 0:N], True, True, (q_r, const_res), (bank[7],))
                    r_t, r_r = rstd_from(psum[:, 7, 0:N], bank[7], 128, EPS, N)
                    cn_t, cn_r = bfx_ring.next()
                    stt(dve, cn_t[:, 0:N], psum[:, 6, 0:N], small[:, 42:43], r_t[:, 0:N], ALU.mult, ALU.mult,
                        (bank[6], r_r, small_res), (cn_r,))
                    proj(6, wA, wA_res, 128, 96, h_t, h_r, N)
                    proj(7, wA, wA_res, 224, 96, h_t, h_r, N)
                    kr_t, kr_r = bfx_ring.next()
                    rope(psum[64:96, 6, 0:N], psum[64:96, 7, 0:N], bank[6], bank[7], tb_t, tb_r,
                         kr_t[64:96, 0:N], kr_r, 64, 96, N)
                    for hh in range(2):
                        h = 2 * pr + hh
                        cp(pool, KTm[hh][64:96, t0:t0 + N], kr_t[64:96, 0:N], (kr_r,), ())
                        bi = 4 + hh
                        mm(psum[0:64, bi, 0:N], wB[:, 0, h * 64:(h + 1) * 64], cn_t[:, 0:N], True, True,
                           (wB_res, cn_r), (bank[bi],))
                        cp(act, KTm[hh][0:64, t0:t0 + N], psum[0:64, bi, 0:N], (bank[bi],), ())
                    for s in range(N // 128):
                        bi = 6 + (s % 2)
                        kt = t0 // 128 + s
                        mm(psum[:, bi, 0:128], cn_t[:, s * 128:(s + 1) * 128], wB[:, 0, 256 + pr * 128:256 + (pr + 1) * 128],
                           True, True, (cn_r, wB_res), (bank[bi],))
                        cp(dve, Vall[:, kt, 0:3:2, :], psum[:, bi, 0:128].rearrange("p (s d) -> p s d", s=2),
                           (bank[bi],), ())
                S_.barrier()
                for t0, N, h_t, h_r, tb_t, tb_r in tiles_prefetch(my_qtiles, 0):
                    is_ctx = t0 < CTX
                    kts = list(range(2)) if t0 < CTX else list(range(NKT))
                    proj(6, wA, wA_res, 320, 128, h_t, h_r, N)
                    proj(7, wA, wA_res, 448, 64, h_t, h_r, N)
                    qa_t, qa_r = sq_ring.next()
                    qb_t, qb_r = sq_ring.next()
                    actf(qa_t[:, 0:N], psum[:, 6, 0:N], AF.Square, (bank[6],), (qa_r,))
                    actf(qb_t[0:64, 0:N], psum[0:64, 7, 0:N], AF.Square, (bank[7],), (qb_r,))
                    mm(psum[:, 5, 0:N], ones_bf[:, :], qa_t[:, 0:N], True, False, (qa_r, const_res), (bank[5],))
                    mm(psum[:, 5, 0:N], ones_bf[0:64, :], qb_t[0:64, 0:N], False, True, (qb_r, const_res), (bank[5],))
                    r_t, r_r = rstd_from(psum[:, 5, 0:N], bank[5], 192, EPS, N)
                    ca_t, ca_r = bfx_ring.next()
                    cb_t, cb_r = bfx_ring.next()
                    stt(dve, ca_t[:, 0:N], psum[:, 6, 0:N], small[:, 40:41], r_t[:, 0:N], ALU.mult, ALU.mult,
                        (bank[6], r_r, small_res), (ca_r,))
                    stt(dve, cb_t[0:64, 0:N], psum[0:64, 7, 0:N], small[0:64, 41:42], r_t[0:64, 0:N], ALU.mult, ALU.mult,
                        (bank[7], r_r, small_res), (cb_r,))
                    gates_chunk(wA, wA_res, 512 + pr * 128, pr, 6, h_t, h_r, N)
                    for hh in range(2):
                        h = 2 * pr + hh
                        for (bi, kb) in ((6, 1), (7, 3)):
                            mm(psum[0:96, bi, 0:N], wB[:, kb, h * 96:(h + 1) * 96], ca_t[:, 0:N], True, False,
                               (wB_res, ca_r), (bank[bi],))
                            mm(psum[0:96, bi, 0:N], wB[0:64, kb + 1, h * 96:(h + 1) * 96], cb_t[0:64, 0:N], False, True,
                               (wB_res, cb_r), (bank[bi],))
                        q_t, q_r = qT[hh]
                        rope(psum[0:96, 6, 0:N], psum[0:96, 7, 0:N], bank[6], bank[7], tb_t, tb_r, q_t[0:96, 0:N], q_r,
                             0, 96, N)
                    for hh in range(2):
                        q_t, q_r = qT[hh]
                        acc = 4 + hh
                        attn_core(kts, N, 96 ** -0.5, acc,
                                  lambda kt, hh=hh: KTm[hh][0:96, kt * 128:(kt + 1) * 128],
                                  q_t[0:96, 0:N], q_r, lambda kt, hh=hh: vm_ap(kt, hh))
                        o0 = hh * 64
                        y_t, y_r = finalize_div(acc, o0, N)
                        tt(pool, yg[o0:o0 + 64, 2 + pr, 0:N], y_t[o0:o0 + 64, 0:N], sg[o0:o0 + 64, pr, 0:N], ALU.mult,
                           (y_r, sg_res[pr]), (yg_res[2 + pr],))
                    st(ygm_v[:, pr, t0:t0 + N], yg[:, 2 + pr, 0:N], yg_res[2 + pr])
                S_.barrier()

        if has("D"):
            ygd_v = ygd_s.rearrange("(c p) t -> p c t", p=128)
            KTd = big[:, 0:TT]
            VBd = TT

            def vd_ap(kt, hh):
                base = VBd + kt * 192 + hh * 64
                return big[:, base:base + 128]

            Vd = big[:, VBd:VBd + NKT * 192].rearrange("p (t s d) -> p t s d", t=NKT, s=3, d=64)
            load_w(wA[:, :, 0:256], wA_res, win_cols(l, C_DK, 256), 256)
            swap_halves(wA[:, :, 256:512], wA[:, :, 0:256], wA_res, 256, 16)
            load_w(wA[:, :, 512:768], wA_res, win_cols(l, C_DV, 256), 256)
            load_w(wB[:, :, 0:256], wB_res, win_cols(l, C_DQ, 256), 256)
            swap_halves(wB[:, :, 256:512], wB[:, :, 0:256], wB_res, 256, 16)
            load_w(wB[:, :, 512:768], wB_res, win_cols(l, C_G + 512, 256), 256)
            for i in range(4):
                memset(pool, qm[i][:, :], 0.0, (qm_res[i],))
            for pr in range(2):
                memset(pool, Vd[:, :, 1, :], 1.0, ())
                for t0, N, h_t, h_r, tb_t, tb_r in tiles_prefetch(qtiles, 2):
                    is_ctx = t0 < CTX
                    kts = list(range(2)) if t0 < CTX else list(range(NKT))
                    proj(6, wA, wA_res, pr * 128, 128, h_t, h_r, N)
                    proj(7, wA, wA_res, 256 + pr * 128, 128, h_t, h_r, N)
                    rope(psum[:, 6, 0:N], psum[:, 7, 0:N], bank[6], bank[7], tb_t, tb_r, KTd[:, t0:t0 + N], Res("kd"),
                         0, 128, N)
                    for s in range(N // 128):
                        bi = 4 + (s % 2)
                        kt = t0 // 128 + s
                        for k in range(8):
                            mm(psum[:, bi, 0:128], h_t[:, k, s * 128:(s + 1) * 128],
                               wA[:, k, 512 + pr * 128:512 + (pr + 1) * 128], k == 0, k == 7, (h_r, wA_res), (bank[bi],))
                        cp(dve, Vd[:, kt, 0:3:2, :], psum[:, bi, 0:128].rearrange("p (s d) -> p s d", s=2),
                           (bank[bi],), ())
                S_.barrier()
                for t0, N, h_t, h_r, tb_t, tb_r in tiles_prefetch(my_qtiles, 2):
                    is_ctx = t0 < CTX
                    kts = list(range(2)) if t0 < CTX else list(range(NKT))
                    proj(6, wB, wB_res, pr * 128, 128, h_t, h_r, N)
                    proj(7, wB, wB_res, 256 + pr * 128, 128, h_t, h_r, N)
                    t1, r1 = f32_ring.next()
                    t2, r2 = f32_ring.next()
                    tt(dve, t1[:, 0:N], psum[:, 7, 0:N], tb_t[:, 1, 0:N], ALU.mult, (bank[7], tb_r), (r1,))
                    tt(dve, t2[:, 0:N], psum[:, 6, 0:N], tb_t[:, 0, 0:N], ALU.mult, (bank[6], tb_r), (r2,))
                    for i in range(4):
                        tt(pool, qm[i][32 * i:32 * i + 32, 0:N], t1[32 * i:32 * i + 32, 0:N], t2[32 * i:32 * i + 32, 0:N],
                           ALU.add, (r1, r2), (qm_res[i],))
                    gates_chunk(wB, wB_res, 512 + pr * 128, pr, 6, h_t, h_r, N)
                    for hh in range(2):
                        o0 = hh * 64
                        ys = []
                        for m in range(2):
                            i = 2 * hh + m
                            acc = 4 + m
                            attn_core(kts, N, 32 ** -0.5, acc,
                                      lambda kt: KTd[:, kt * 128:(kt + 1) * 128],
                                      qm[i][:, 0:N], qm_res[i], lambda kt, hh=hh: vd_ap(kt, hh))
                            ys.append(finalize_div(acc, o0, N))
                        (y1, r1), (y2, r2) = ys
                        stt(dve, yraw[o0:o0 + 64, 0, 0:N], y2[o0:o0 + 64, 0:N], small[o0:o0 + 64, 52:53],
                            y1[o0:o0 + 64, 0:N], ALU.mult, ALU.add, (r1, r2, small_res), (yraw_res[0],))
                    s_t2, s_r2 = sq_ring.next()
                    actf(s_t2[:, 0:N], yraw[:, 0, 0:N], AF.Square, (yraw_res[0],), (s_r2,))
                    mm(psum[:, 6, 0:N], bones_bf[:, :], s_t2[:, 0:N], True, True, (s_r2, const_res), (bank[6],))
                    r_t, r_r = rstd_from(psum[:, 6, 0:N], bank[6], 64, EPS, N)
                    f_t, f_r = f32_ring.next()
                    stt(dve, f_t[:, 0:N], yraw[:, 0, 0:N], small[:, 43:44], r_t[:, 0:N], ALU.mult, ALU.mult,
                        (yraw_res[0], r_r, small_res), (f_r,))
                    tt(pool, yg[:, 4 + pr, 0:N], f_t[:, 0:N], sg[:, pr, 0:N], ALU.mult, (f_r, sg_res[pr]), (yg_res[4 + pr],))
                    st(ygd_v[:, pr, t0:t0 + N], yg[:, 4 + pr, 0:N], yg_res[4 + pr])
                S_.barrier()

        if has("F1"):
            ygs_v = ygs_s.rearrange("(c p) t -> p c t", p=128)
            ygp_v = ygp_s.rearrange("(c p) t -> p c t", p=128)
            pool_v = pool_s.rearrange("(c p) t -> p c t", p=128)
            rcnt_v = rcnt.rearrange("(c p) t -> p c t", p=128)
            KTs = big[:, 0:TT]
            VBs = TT

            def vs_ap(kt, kv):
                base = VBs + kt * 192 + kv * 64
                return big[:, base:base + 128]

            Vs = big[:, VBs:VBs + NKT * 192].rearrange("p (t s d) -> p t s d", t=NKT, s=3, d=64)
            memset(pool, Vs[:, :, 1, :], 1.0, ())
            memset(pool, pw[:, :, 0, 0:16], 0.0, (pw_res,))
            st(pool_v[:, :, 0:PPAD0], pw[:, :, 0, 0:PPAD0], pw_res)
            st(pool_v[:, :, PPAD0 + CTX:PPAD0 + CTX + PPAD1], pw[:, :, 0, 0:PPAD1], pw_res)
            st(pool_v[:, :, PT - PPAD2:PT], pw[:, :, 0, 0:PPAD2], pw_res)
            load_w(wA[:, :, 0:128], wA_res, win_cols(l, C_SK, 128), 128)
            swap_halves(wA[:, :, 128:256], wA[:, :, 0:128], wA_res, 128, 32)
            load_w(wA[:, :, 256:384], wA_res, win_cols(l, C_SV, 128), 128)
            load_w(wA[:, :, 384:640], wA_res, win_cols(l, C_POOL, 256), 256)
            for g in range(2):
                for kv in range(2):
                    load_w(wB[:, :, g * 128 + kv * 64:g * 128 + kv * 64 + 64], wB_res,
                           win_cols(l, C_SQ + (kv * 2 + g) * 64, 64), 64)
                    load_w(wB[:, :, 512 + g * 128 + kv * 64:512 + g * 128 + kv * 64 + 64], wB_res,
                           win_cols(l, C_G + 768 + (kv * 2 + g) * 64, 64), 64)
            swap_halves(wB[:, :, 256:512], wB[:, :, 0:256], wB_res, 256, 32)
            load_w(wB[:, :, 768:1024], wB_res, win_cols(l, C_G, 256), 256)
            s_t, s_r = stage_ring.next()
            memset(pool, s_t[:, 0:2, 0:128], 0.0, (s_r,))
            for g4 in range(4):
                r0 = (g4 % 2) * 64
                ld(s_t[r0:r0 + 64, g4 // 2, r0:r0 + 64], w_pool[l, g4], s_r)
            cp(pool, wpd[:, :, :], s_t[:, 0:2, 0:128], (s_r,), (wpd_res,))
            for t0, N, h_t, h_r, tb_t, tb_r in tiles_prefetch(qtiles, 4):
                is_ctx = t0 < CTX
                kts = list(range(2)) if t0 < CTX else list(range(NKT))
                proj(6, wA, wA_res, 0, 128, h_t, h_r, N)
                proj(7, wA, wA_res, 128, 128, h_t, h_r, N)
                rope(psum[:, 6, 0:N], psum[:, 7, 0:N], bank[6], bank[7], tb_t, tb_r, KTs[:, t0:t0 + N], Res("ks"), 0, 128, N)
                for s in range(N // 128):
                    bi = 4 + (s % 2)
                    kt = t0 // 128 + s
                    for k in range(8):
                        mm(psum[:, bi, 0:128], h_t[:, k, s * 128:(s + 1) * 128], wA[:, k, 256:384], k == 0, k == 7,
                           (h_r, wA_res), (bank[bi],))
                    cp(dve, Vs[:, kt, 0:3:2, :], psum[:, bi, 0:128].rearrange("p (s d) -> p s d", s=2), (bank[bi],), ())
                for c_ in range(2):
                    proj(6 + c_, wA, wA_res, 384 + c_ * 128, 128, h_t, h_r, N)
                    cp(act, pu[:, c_, 0:N], psum[:, 6 + c_, 0:N], (bank[6 + c_],), (pu_res,))
                po = t0 + (PPAD0 if t0 < CTX else PPAD0 + PPAD1)
                st(pool_v[:, :, po:po + N], pu[:, :, 0:N], pu_res)
            S_.barrier()
            for t0, N, h_t, h_r, tb_t, tb_r in tiles_prefetch(my_qtiles, 4):
                is_ctx = t0 < CTX
                kts = list(range(2)) if t0 < CTX else list(range(NKT))
                po = t0 + (PPAD0 if is_ctx else PPAD0 + PPAD1)
                ld(pu[:, :, 0:N + 16], pool_v[:, :, po - 8:po + N + 8], pu_res)
                ld(prc[:, :, 0:N], rcnt_v[:, :, po:po + N], prc_res)
                for g in range(2):
                    proj(6, wB, wB_res, g * 128, 128, h_t, h_r, N)
                    proj(7, wB, wB_res, 256 + g * 128, 128, h_t, h_r, N)
                    rope(psum[:, 6, 0:N], psum[:, 7, 0:N], bank[6], bank[7], tb_t, tb_r, qT[g][0][:, 0:N], qT[g][1], 0, 128, N)
                for i4 in range(4):
                    gates_chunk(wB, wB_res, 512 + i4 * 128, i4, 6 + (i4 % 2), h_t, h_r, N)
                for g in range(2):
                    q_t, q_r = qT[g]
                    for kv in range(2):
                        acc = 4 + kv
                        o0 = kv * 64
                        s0 = 64 - o0
                        hq = kv * 2 + g
                        wins = []
                        if not is_ctx:
                            qb0 = (t0 - CTX) // 128
                            for ktl in range(max(qb0 - 1, 0), min(qb0 + 4, NL - 1) + 1):
                                qlo, qhi = max(ktl - 1, qb0), min(ktl + 1, qb0 + 3)
                                wins.append((ktl, (qlo - qb0) * 128, (qhi - qb0 + 1) * 128, (qlo - ktl + 1) * 128))
                        attn_core([0, 1], N, 0.125, acc,
                                  lambda kt, kv=kv: KTs[kv * 64:kv * 64 + 64, kt * 128:(kt + 1) * 128],
                                  q_t[o0:o0 + 64, 0:N], q_r, lambda kt, kv=kv: vs_ap(kt, kv), final=(len(wins) == 0))
                        for wi, (ktl, c0, c1, m0) in enumerate(wins):
                            gq = attn_core.gi % 2
                            attn_core.gi += 1
                            b0 = 2 * gq
                            kt = 2 + ktl
                            mm(psum[:, b0, c0:c1], KTs[o0:o0 + 64, kt * 128:(kt + 1) * 128], q_t[o0:o0 + 64, c0:c1],
                               True, False, (q_r,), (grp[gq],))
                            mm(psum[:, b0, c0:c1], ident[:, :], mask3[:, m0:m0 + (c1 - c0)], False, True, (idr, mkr),
                               (grp[gq],))
                            p_t, p_r = pT_ring.next()
                            actf(p_t[:, 0, c0:c1], psum[:, b0, c0:c1], AF.Exp, (grp[gq],), (p_r,), scale=0.125)
                            mm(psum[:, acc, c0:c1], vs_ap(kt, kv), p_t[:, 0, c0:c1], False, wi == len(wins) - 1,
                               (p_r,), (bank[acc],))
                        y_t, y_r = finalize_div(acc, o0, N, add_ap=small[s0:s0 + 64, 48 + hq:49 + hq])
                        tt(pool, yg[o0:o0 + 64, 6 + g, 0:N], y_t[o0:o0 + 64, 0:N], sg[o0:o0 + 64, g, 0:N], ALU.mult,
                           (y_r, sg_res[g]), (yg_res[6 + g],))
                for g in range(2):
                    st(ygs_v[:, g, t0:t0 + N], yg[:, 6 + g, 0:N], yg_res[6 + g])
                M_ = N + 16
                for c_ in range(2):
                    for half in range(2):
                        gi4 = 2 * c_ + half
                        nlev = gi4 + 1
                        R0, R1 = half * 64, half * 64 + 64
                        tt(pool, pw[R0:R1, c_, 1, 1:M_], pu[R0:R1, c_, 1:M_], pu[R0:R1, c_, 0:M_ - 1], ALU.add,
                           (pu_res,), (pw_res,))
                        for lev in range(2, nlev + 1):
                            lo = (1 << lev) - 1
                            sh = 1 << (lev - 1)
                            src = pw[R0:R1, c_, (lev - 1) % 2, :]
                            tt(pool, pw[R0:R1, c_, lev % 2, lo:M_], src[:, lo:M_], src[:, lo - sh:M_ - sh], ALU.add,
                               (pw_res,), (pw_res,))
                        off = 8 + (1 << (nlev - 1)) - 1
                        f_t, f_r = f32_ring.next()
                        tt(pool, f_t[R0:R1, 0:N], pw[R0:R1, c_, nlev % 2, off:off + N], prc[R0:R1, c_, 0:N], ALU.mult,
                           (pw_res, prc_res), (f_r,))
                        tt(pool, pl[R0:R1, c_, 0:N], f_t[R0:R1, 0:N], pu[R0:R1, c_, 8:8 + N], ALU.subtract,
                           (f_r, pu_res), (pl_res[c_],))
                    bi = 6 + c_
                    mm(psum[:, bi, 0:N], wpd[:, c_, :], pl[:, c_, 0:N], True, True, (wpd_res, pl_res[c_]), (bank[bi],))
                    stt(dve, yg[:, c_, 0:N], psum[:, bi, 0:N], small[:, 44 + c_:45 + c_], sg[:, 2 + c_, 0:N],
                        ALU.mult, ALU.mult, (bank[bi], small_res, sg_res[2 + c_]), (yg_res[c_],))
                    st(ygp_v[:, c_, t0:t0 + N], yg[:, c_, 0:N], yg_res[c_])
            S_.barrier()

        if has("F2"):
            ysrc = {0: ygp_s, 1: ygm_s, 2: ygd_s, 3: ygs_s}
            yv = {r: ysrc[r].rearrange("(c p) t -> p c t", p=128) for r in range(4)}
            wm = big[:, 0:8 * 4096].rearrange("p (k c) -> p k c", k=8)
            for blk in range(8):
                load_w(wm[:, :, blk * 512:(blk + 1) * 512], wm_res, win_cols(l, C_M + blk * 512, 512), 512)
            for ch in range(2):
                s_t, s_r = stage_ring.next()
                for r in range(3):
                    ld(s_t[:, 2 * r:2 * r + 2, :], w_br[l, r, :, ch * 512:(ch + 1) * 512].rearrange("(kc p) c -> p kc c", p=128),
                       s_r)
                for g in range(2):
                    for kv in range(2):
                        r0 = (kv * 2 + g) * 64
                        ld(s_t[kv * 64:kv * 64 + 64, 6 + g, :], w_br[l, 3, r0:r0 + 64, ch * 512:(ch + 1) * 512], s_r)
                cp(pool, wA[:, :, ch * 512:(ch + 1) * 512], s_t[:, :, :], (s_r,), (wA_res,))
                s_t, s_r = stage_ring.next()
                ld(s_t[:, :, :], w_out[l, :, ch * 512:(ch + 1) * 512].rearrange("(k p) c -> p k c", p=128), s_r)
                cp(pool, wB[:, :, ch * 512:(ch + 1) * 512], s_t[:, :, :], (s_r,), (wB_res,))
            for t0, N, h_t, h_r, tb_t, tb_r in tiles_prefetch(my_qtiles, None):
                is_ctx = t0 < CTX
                v = 1 if is_ctx else 0
                for r in range(4):
                    for kc in range(2):
                        ld(yg[:, 2 * r + kc, 0:N], yv[r][:, kc, t0:t0 + N], yg_res[2 * r + kc])
                for j in range(8):
                    for r in range(4):
                        bz = 4 + (r % 2)
                        bb = 6 + (r % 2)
                        proj(bz, wm, wm_res, r * 1024 + j * 128, 128, h_t, h_r, N)
                        tm_t, tm_r = bfx_ring.next()
                        actf(tm_t[:, 0:N], psum[:, bz, 0:N], AF.Tanh, (bank[bz],), (tm_r,), scale=0.5)
                        for kc in range(2):
                            mm(psum[:, bb, 0:N], wA[:, 2 * r + kc, j * 128:(j + 1) * 128], yg[:, 2 * r + kc, 0:N],
                               kc == 0, kc == 1, (wA_res, yg_res[2 * r + kc]), (bank[bb],))
                        if r == 0:
                            stt(dve, macc[:, j, 0:N], tm_t[:, 0:N], 1.0, psum[:, bb, 0:N], ALU.add, ALU.mult,
                                (tm_r, bank[bb]), (macc_res[j],))
                        else:
                            f_t, f_r = f32_ring.next()
                            stt(dve, f_t[:, 0:N], tm_t[:, 0:N], 1.0, psum[:, bb, 0:N], ALU.add, ALU.mult,
                                (tm_r, bank[bb]), (f_r,))
                            if r < 3:
                                tt(pool, macc[:, j, 0:N], macc[:, j, 0:N], f_t[:, 0:N], ALU.add, (macc_res[j], f_r),
                                   (macc_res[j],))
                            else:
                                tt(pool, mbf[:, j, 0:N], macc[:, j, 0:N], f_t[:, 0:N], ALU.add, (macc_res[j], f_r),
                                   (mbf_res[j],))
                for j in range(8):
                    bo = 4 + (j % 2)
                    for k in range(8):
                        mm(psum[:, bo, 0:N], wB[:, k, j * 128:(j + 1) * 128], mbf[:, k, 0:N], k == 0, k == 7,
                           (wB_res, mbf_res[k]), (bank[bo],))
                    cp(act, macc[:, j, 0:N], psum[:, bo, 0:N], (bank[bo],), (macc_res[j],))
                    q_t, q_r = sq_ring.next()
                    actf(q_t[:, 0:N], psum[:, bo, 0:N], AF.Square, (bank[bo],), (q_r,))
                    mm(psum[:, 6, 0:N], ones_bf[:, :], q_t[:, 0:N], j == 0, j == 7, (q_r, const_res), (bank[6],))
                r_t, r_r = rstd_from(psum[:, 6, 0:N], bank[6], D, 16 * EPS, N)
                dst_v = (xc_dst if is_ctx else x_dst).rearrange("(k p) t -> p k t", p=128)
                tq = 0 if is_ctx else t0 - CTX
                for j in range(8):
                    xs_t, xs_r = f32_ring.next()
                    ld(xs_t[:, 0:N], x_tile_ap(t0, N)[:, j, :], xs_r)
                    f_t, f_r = f32_ring.next()
                    stt(dve, f_t[:, 0:N], macc[:, j, 0:N], gv[:, j, v:v + 1], r_t[:, 0:N], ALU.mult, ALU.mult,
                        (macc_res[j], r_r, small_res), (f_r,))
                    tt(pool, xs_t[:, 0:N], f_t[:, 0:N], xs_t[:, 0:N], ALU.add, (f_r, xs_r), (xs_r,))
                    st(dst_v[:, j, tq:tq + N], xs_t[:, 0:N], xs_r)
            S_.barrier()

    S_.barrier()
    with nc.Block() as block:
        @block.sync
        def _(e):
            S_.replay("sp", e)

        @block.tensor
        def _(e):
            S_.replay("pe", e)

        @block.scalar
        def _(e):
            S_.replay("act", e)

        @block.vector
        def _(e):
            S_.replay("dve", e)

        @block.gpsimd
        def _(e):
            S_.replay("pool", e)
    stack.close()
    nc.in_names_ = in_names
    nc.sched_ = S_
    return nc


def make_inputs_core(inputs, b, S, consts=None):
    if consts is None:
        consts = make_consts(S)
    m = {
        "xT": np.ascontiguousarray(inputs["x"][b, :S].T),
        "ctxT": np.ascontiguousarray(inputs["ctx"][b].T),
        "c": np.ascontiguousarray(inputs["c"][b]),
        "c_ctx": np.ascontiguousarray(inputs["c_ctx"]),
    }
    for k in ("w_mod", "b_mod", "g_pre", "g_post", "w_in", "w_pool", "s_pool", "g_cq", "w_uq", "g_ckv", "w_uk", "w_uv",
              "lam_q1", "lam_k1", "lam_q2", "lam_k2", "g_diff", "sink", "w_br", "w_out"):
        m[k] = np.ascontiguousarray(inputs[k])
    m.update(consts)
    return m


def kernel(**inputs):
    inputs = {k: np.asarray(v) for k, v in inputs.items()}
    B, S = inputs["x"].shape[0], inputs["x"].shape[1]
    consts = make_consts(S)
    nc = build_fused(S)
    names = nc.in_names_
    in_maps = []
    for b in range(B):
        m = make_inputs_core(inputs, b, S, consts)
        in_maps.append({k: m[k] for k in names})
    res = run_bass_kernel_spmd(nc, in_maps, core_ids=list(range(B)))
    out = np.stack([np.ascontiguousarray(np.asarray(res.results[b]["yT"]).T) for b in range(B)], 0)
    return out.astype(np.float32)
```

```python
import math
from contextlib import ExitStack

import numpy as np
import ml_dtypes

import concourse.bass as bass
import concourse.mybir as mybir
from concourse.bass_utils import run_bass_kernel_spmd

F32 = mybir.dt.float32
BF16 = mybir.dt.bfloat16
AF = mybir.ActivationFunctionType
ALU = mybir.AluOpType

D = 1024
CTX = 256
DEPTH = 2
IN_COLS = 7008
EPS = 1e-6
C_POOL, C_CQ, C_CKV, C_KR, C_DQ, C_DK, C_DV, C_SQ, C_SK, C_SV, C_G, C_M = (
    0, 256, 448, 576, 608, 864, 1120, 1376, 1632, 1760, 1888, 2912)
POOL_WINDOWS = (2, 4, 8, 16)
PPAD0, PPAD1, PPAD2 = 8, 16, 8


class Res:
    __slots__ = ("name", "w", "r", "dsem", "dcnt")

    def __init__(self, name):
        self.name = name
        self.w = None
        self.r = {}
        self.dsem = None
        self.dcnt = 0


class Sched:
    ENGS = ("pe", "act", "dve", "pool", "sp")
    EPOCH = 30000

    def __init__(self, nc, stack):
        self.nc = nc
        self.stack = stack
        self.ops = {e: [] for e in self.ENGS}
        self.cnt = {e: 0 for e in self.ENGS}
        self.sems = {e: [] for e in self.ENGS}
        self.known = {e: {} for e in self.ENGS}
        self.dma_res = []
        self.nsem = 0

    def _sem(self, name):
        self.nsem += 1
        return self.stack.enter_context(self.nc.semaphore(name))

    def _eng_sem(self, eng, idx):
        ep = idx // self.EPOCH
        while len(self.sems[eng]) <= ep:
            self.sems[eng].append(self._sem(f"p_{eng}_{len(self.sems[eng])}"))
        return self.sems[eng][ep], idx % self.EPOCH + 1

    def _collect(self, eng, reads, writes):
        deps = []
        for r in reads:
            if r.w is not None:
                deps.append((r.w, True))
        for w in writes:
            if w.w is not None:
                deps.append((w.w, False))
            for t in w.r.values():
                deps.append((t, False))
        need = {}
        for tok, raw in deps:
            if tok[0] == "e":
                _, f, idx = tok
                if f == eng:
                    if eng in ("pe", "sp") or not raw:
                        continue
                key = ("e", f)
                val = idx
            else:
                _, res, val = tok
                key = ("d", id(res), res)
            if self.known[eng].get(key[:2], -1) >= val:
                continue
            if key not in need or need[key] < val:
                need[key] = val
        waits = []
        for key, val in need.items():
            self.known[eng][key[:2]] = val
            if key[0] == "e":
                sem, v = self._eng_sem(key[1], val)
                waits.append((sem, v))
            else:
                waits.append((key[2].dsem, val))
        return waits

    def _update(self, tok, key, reads, writes):
        for w in writes:
            w.w = tok
            w.r = {}
        for r in reads:
            r.r[key] = tok

    def op(self, eng, fn, reads=(), writes=()):
        waits = self._collect(eng, reads, writes)
        idx = self.cnt[eng]
        self.cnt[eng] += 1
        tok = ("e", eng, idx)
        self._update(tok, ("e", eng), reads, writes)
        sem, _ = self._eng_sem(eng, idx)
        self.ops[eng].append((fn, waits, sem, 1))

    def dma(self, queue, out, in_, sres, reads=(), writes=()):
        waits = self._collect(queue, reads, writes)
        if sres.dsem is None:
            sres.dsem = self._sem(f"d_{sres.name}")
            self.dma_res.append(sres)
        sres.dcnt += 16
        tok = ("d", sres, sres.dcnt)
        self._update(tok, ("d", id(sres)), reads, writes)
        self.ops[queue].append((lambda e, o=out, i=in_: e.dma_start(out=o, in_=i), waits, sres.dsem, 16))

    def barrier(self):
        for e in self.ENGS:
            waits = []
            for f in self.ENGS:
                if f == e or self.cnt[f] == 0:
                    continue
                idx = self.cnt[f] - 1
                if self.known[e].get(("e", f), -1) >= idx:
                    continue
                self.known[e][("e", f)] = idx
                waits.append(self._eng_sem(f, idx))
            for res in self.dma_res:
                if self.known[e].get(("d", id(res)), -1) >= res.dcnt:
                    continue
                self.known[e][("d", id(res))] = res.dcnt
                waits.append((res.dsem, res.dcnt))
            if waits:
                self.ops[e].append((None, waits, None, 0))

    ATTACH = True

    def replay(self, eng, e):
        for fn, waits, sem, inc in self.ops[eng]:
            attach = self.ATTACH and fn is not None and inc == 1 and len(waits) > 0
            for s, v in (waits[:-1] if attach else waits):
                e.wait_ge(s, v)
            if fn is not None:
                ins = fn(e)
                if attach:
                    ins._wait_ge(*waits[-1])
                ins.then_inc(sem, inc)


class Ring:
    def __init__(self, items):
        self.items = items
        self.i = 0

    def next(self):
        it = self.items[self.i % len(self.items)]
        self.i += 1
        return it


def _rope_tab(S, rot_dim):
    t = np.arange(S)
    row = (t // 64).astype(np.float64)
    col = (t % 64).astype(np.float64)
    n_freq = rot_dim // 4
    inv = np.exp(-math.log(10000.0) * np.arange(n_freq, dtype=np.float64) / n_freq).astype(np.float32).astype(np.float64)
    ang = np.concatenate([row[:, None] * inv, col[:, None] * inv], axis=-1).astype(np.float32)
    c = np.cos(ang.astype(np.float64)).T
    s = np.sin(ang.astype(np.float64)).T
    C = np.concatenate([c, c], 0)
    Sg = np.concatenate([-s, s], 0)
    return C.astype(np.float32), Sg.astype(np.float32)


def make_consts(S):
    TT = CTX + S
    c32, s32 = _rope_tab(S, 32)
    c64, s64 = _rope_tab(S, 64)

    def full(C, Sg, rep):
        Cf = np.ones((C.shape[0] * rep, TT), np.float32)
        Sf = np.zeros((C.shape[0] * rep, TT), np.float32)
        Cf[:, CTX:] = np.tile(C, (rep, 1))
        Sf[:, CTX:] = np.tile(Sg, (rep, 1))
        return Cf, Sf

    t32c, t32s = full(c32, s32, 4)
    t64c, t64s = full(c64, s64, 2)
    tmc = np.ones((128, TT), np.float32)
    tms = np.zeros((128, TT), np.float32)
    tmc[64:96] = t32c[0:32]
    tms[64:96] = t32s[0:32]
    tab = np.stack([tmc, tms, t32c, t32s, t64c, t64s], 0)
    PT = PPAD0 + CTX + PPAD1 + S + PPAD2
    rc = np.zeros((256, PT), np.float32)
    for gi, w in enumerate(POOL_WINDOWS):
        for (off, L) in ((PPAD0, CTX), (PPAD0 + CTX + PPAD1, S)):
            t = np.arange(L)
            lo = np.clip(t - w // 2, 0, L)
            hi = np.clip(t - w // 2 + w, 0, L)
            rc[gi * 64:(gi + 1) * 64, off:off + L] = (1.0 / (hi - lo).astype(np.float64)).astype(np.float32)[None, :]
    ident = np.eye(128, dtype=np.float32).astype(ml_dtypes.bfloat16)
    a = np.arange(128)[None, :]
    b = np.arange(128)[:, None]
    NEG = -30000.0
    m_next = np.where(b <= a, 0.0, NEG)
    m_prev = np.where(a <= b, 0.0, NEG)
    mask3 = np.concatenate([m_next, np.zeros((128, 128)), m_prev], 1).astype(np.float32).astype(ml_dtypes.bfloat16)
    return dict(tab=tab, rcnt=rc, ident=ident, mask3=mask3)


def build_fused(S):
    phases = {"0", "M", "D", "F1", "F2"}
    TT = CTX + S
    NKT = TT // 128
    NL = S // 128
    PT = PPAD0 + CTX + PPAD1 + S + PPAD2
    qtiles = [(0, CTX)] + [(CTX + i * 512, 512) for i in range(S // 512)]
    nc = bass.Bass("TRN2", target_bir_lowering=False)

    in_names = []

    def din(name, shape, dt=F32):
        in_names.append(name)
        return nc.dram_tensor(name, list(shape), dt, kind="ExternalInput").ap()

    def scr(name, shape, dt, producer, consumers):
        return nc.dram_tensor(name, list(shape), dt).ap()

    x_in = din("xT", [D, S])
    xc_in = din("ctxT", [D, CTX])
    c_in = din("c", [D])
    cc_in = din("c_ctx", [D])
    w_mod = din("w_mod", [DEPTH, D, 3 * D])
    b_mod = din("b_mod", [DEPTH, 3 * D])
    g_pre = din("g_pre", [DEPTH, D])
    g_post = din("g_post", [DEPTH, D])
    w_in = din("w_in", [DEPTH, D, IN_COLS])
    w_pool = din("w_pool", [DEPTH, 4, 64, 64])
    s_pool = din("s_pool", [DEPTH, 256])
    g_cq = din("g_cq", [DEPTH, 192])
    w_uq = din("w_uq", [DEPTH, 192, 384])
    g_ckv = din("g_ckv", [DEPTH, 128])
    w_uk = din("w_uk", [DEPTH, 128, 256])
    w_uv = din("w_uv", [DEPTH, 128, 256])
    lam_in = [din(n, [DEPTH, 32]) for n in ("lam_q1", "lam_k1", "lam_q2", "lam_k2")]
    g_diff = din("g_diff", [DEPTH, 64])
    sink = din("sink", [DEPTH, 4])
    w_br = din("w_br", [DEPTH, 4, 256, D])
    w_out = din("w_out", [DEPTH, D, D])
    tab = din("tab", [6, 128, TT])
    rcnt = din("rcnt", [256, PT])
    ident_in = din("ident", [128, 128], BF16)
    mask3_in = din("mask3", [128, 384], BF16)
    hT = scr("hT_s", [D, TT], BF16, "0", ("M", "D", "F1", "F2"))
    small_s = scr("small_s", [128, 256], F32, "0", ("M", "D", "F1", "F2"))
    ygm_s = scr("ygm_s", [256, TT], BF16, "M", ("F2",))
    ygd_s = scr("ygd_s", [256, TT], BF16, "D", ("F2",))
    ygs_s = scr("ygs_s", [256, TT], BF16, "F1", ("F2",))
    ygp_s = scr("ygp_s", [256, TT], BF16, "F1", ("F2",))
    y_out = nc.dram_tensor("yT", [D, S], F32, kind="ExternalOutput").ap()
    x1T = nc.dram_tensor("x1T_s", [D, S], F32).ap()
    xc1T = nc.dram_tensor("xc1T_s", [D, CTX], F32).ap()
    pool_s = nc.dram_tensor("pool_s", [256, PT], F32).ap()

    stack = ExitStack()
    S_ = Sched(nc, stack)
    stack.enter_context(nc.allow_non_contiguous_dma(reason="tiny per-partition vectors and strided weight views"))

    def sb(name, shape, dt):
        return stack.enter_context(nc.sbuf_tensor("s_" + name, list(shape), dt))

    def slots(name, n, shape, dt):
        return Ring([(sb(f"{name}{i}", shape, dt), Res(f"{name}{i}")) for i in range(n)])

    def has(*ps):
        return any(p in phases for p in ps)

    need_big = 0
    if has("M"):
        need_big = max(need_big, 2 * TT + NKT * 192)
    if has("D"):
        need_big = max(need_big, TT + NKT * 192)
    if has("F1"):
        need_big = max(need_big, TT + NKT * 192)
    if has("F2"):
        need_big = max(need_big, 8 * 4096)
    big = sb("big", [128, max(need_big, 64)], BF16)
    psum = stack.enter_context(nc.psum_tensor("psum", [128, 8, 512], F32))
    bank = [Res(f"bank{i}") for i in range(8)]
    grp = [Res("grp0"), Res("grp1")]
    hT_ring = slots("hTt", 2, [128, 8, 512], BF16)
    stage_ring = slots("stg", 1, [128, 8, 512], F32)
    ones_bf = sb("ones_bf", [128, 128], BF16)
    bones_bf = sb("bones_bf", [128, 128], BF16)
    const_res = Res("consts")
    small = sb("small", [128, 256], F32)
    small_res = Res("small")
    sq_ring = slots("sq", 2, [128, 512], BF16)
    rstd_ring = slots("rstd", 1, [128, 512], F32)
    f32_ring = slots("tf", 6, [128, 512], F32)
    bfx_ring = slots("tb", 4, [128, 512], BF16)
    wA = sb("wA", [128, 8, 1024], BF16)
    wA_res = Res("wA")
    wB = sb("wB", [128, 8, 1024], BF16)
    wB_res = Res("wB")
    if has("0"):
        modv = sb("modv", [128, 24, 2], F32)
        modv_res = Res("modv")
        csb = sb("csb", [128, 8, 2], F32)
        csb_res = Res("csb")
        stab = sb("stab", [128, 160], F32)
        stab_res = Res("stab")
    if has("M", "D", "F1"):
        arF = sb("arF", [128, 8, 512], F32)
        arB = sb("arB", [128, 10, 512], BF16)
        pT_ring = Ring([(arB[:, 2 * i:2 * i + 2, :], Res(f"pT{i}")) for i in range(3)])
        tabs_ring = Ring([(arF[:, 4 + 2 * i:6 + 2 * i, :], Res(f"tabs{i}")) for i in range(2)])
        qT = [(arB[:, 6 + i, :], Res(f"qT{i}")) for i in range(2)]
        qm = [arB[:, 6 + i, :] for i in range(4)]
        qm_res = [Res(f"qm{i}") for i in range(4)]
        sg = arF[:, 0:4, :]
        sg_res = [Res(f"sg{i}") for i in range(4)]
        yraw = sb("yraw", [128, 1, 512], F32)
        yraw_res = [Res("yraw0"), Res("yraw1")]
    yg = sb("yg", [128, 8, 512], BF16)
    yg_res = [Res(f"yg{i}") for i in range(8)]
    if has("F1"):
        ident = sb("ident", [128, 128], BF16)
        mask3 = sb("mask3", [128, 384], BF16)
        pu = sb("pu", [128, 2, 528], F32)
        pu_res = Res("pu")
        prc = sb("prc", [128, 2, 512], F32)
        prc_res = Res("prc")
        pw = sb("pw", [128, 2, 2, 528], F32)
        pw_res = Res("pw")
        pl = sb("pl", [128, 2, 512], BF16)
        pl_res = [Res("pl0"), Res("pl1")]
        wpd = sb("wpd", [128, 2, 128], BF16)
        wpd_res = Res("wpd")
    if has("F2"):
        macc = arF
        macc_res = [Res(f"macc{i}") for i in range(8)]
        mbf = arB
        mbf_res = [Res(f"mbf{i}") for i in range(8)]
        wm_res = Res("wm")

    sp, pe, act, dve, pool = "sp", "pe", "act", "dve", "pool"

    def mm(out, lhsT, rhs, start, stop, reads, writes, tp=None):
        def fn(e, out=out, lhsT=lhsT, rhs=rhs, start=start, stop=stop, tp=tp):
            if tp is None:
                return e.matmul(out, lhsT, rhs, start=start, stop=stop)
            return e.matmul(out, lhsT, rhs, start=start, stop=stop, tile_position=tp)
        S_.op(pe, fn, reads, writes)

    def actf(out, in_, func, reads, writes, scale=1.0, bias=0.0):
        S_.op(act, lambda e, o=out, i=in_, f=func, s=scale, b=bias: e.activation(out=o, in_=i, func=f, bias=b, scale=s),
              reads, writes)

    def tt(eng, out, in0, in1, op, reads, writes):
        S_.op(eng, lambda e, o=out, a=in0, b=in1, p=op: e.tensor_tensor(out=o, in0=a, in1=b, op=p), reads, writes)

    def stt(eng, out, in0, scalar, in1, op0, op1, reads, writes):
        S_.op(eng, lambda e, o=out, a=in0, s=scalar, b=in1, p0=op0, p1=op1:
              e.scalar_tensor_tensor(out=o, in0=a, scalar=s, in1=b, op0=p0, op1=p1), reads, writes)

    def ts(eng, out, in0, s1, s2, op0, op1, reads, writes):
        if s2 is None:
            S_.op(eng, lambda e, o=out, a=in0, s=s1, p0=op0: e.tensor_scalar(out=o, in0=a, scalar1=s, scalar2=None, op0=p0),
                  reads, writes)
        else:
            S_.op(eng, lambda e, o=out, a=in0, s=s1, t=s2, p0=op0, p1=op1:
                  e.tensor_scalar(out=o, in0=a, scalar1=s, scalar2=t, op0=p0, op1=p1), reads, writes)

    def cp(eng, out, in_, reads, writes):
        if eng == act:
            S_.op(eng, lambda e, o=out, i=in_: e.copy(out=o, in_=i), reads, writes)
        else:
            S_.op(eng, lambda e, o=out, i=in_: e.tensor_copy(out=o, in_=i), reads, writes)

    def recip(out, in_, reads, writes):
        S_.op(dve, lambda e, o=out, i=in_: e.reciprocal(out=o, in_=i), reads, writes)

    def memset(eng, ap, val, writes):
        S_.op(eng, lambda e, a=ap, v=val: e.memset(a, v), (), writes)

    def ld(out, in_, res, reads=(), q=sp):
        S_.dma(q, out, in_, res, reads=reads, writes=(res,))

    def st(out, in_, res, q=pool):
        S_.dma(q, out, in_, res, reads=(res,), writes=())

    def rstd_from(ssum_ap, bank_res, n, eps, N, P=128):
        r_t, r_res = rstd_ring.next()
        actf(r_t[0:P, 0:N], ssum_ap, AF.Ln, (bank_res,), (r_res,), scale=1.0 / n, bias=eps)
        actf(r_t[0:P, 0:N], r_t[0:P, 0:N], AF.Exp, (r_res,), (r_res,), scale=-0.5)
        return r_t, r_res


    def finalize_div(acc, o0, N, add_ap=None):
        s0 = 64 - o0
        rs_t, rs_r = f32_ring.next()
        if add_ap is not None:
            ts(dve, rs_t[s0:s0 + 64, 0:N], psum[s0:s0 + 64, acc, 0:N], add_ap, None, ALU.add, None,
               (bank[acc], small_res), (rs_r,))
            recip(rs_t[s0:s0 + 64, 0:N], rs_t[s0:s0 + 64, 0:N], (rs_r,), (rs_r,))
        else:
            recip(rs_t[s0:s0 + 64, 0:N], psum[s0:s0 + 64, acc, 0:N], (bank[acc],), (rs_r,))
        r2_t, r2_r = f32_ring.next()
        S_.dma(pool, r2_t[o0:o0 + 64, 0:N], rs_t[s0:s0 + 64, 0:N], r2_r, reads=(rs_r,), writes=(r2_r,))
        y_t, y_r = f32_ring.next()
        tt(dve, y_t[o0:o0 + 64, 0:N], psum[o0:o0 + 64, acc, 0:N], r2_t[o0:o0 + 64, 0:N], ALU.mult,
           (bank[acc], r2_r), (y_r,))
        return y_t, y_r

    memset(dve, ones_bf[:, :], 1.0, (const_res,))
    memset(dve, bones_bf[:, :], 0.0, (const_res,))
    memset(dve, bones_bf[0:64, 0:64], 1.0, (const_res,))
    memset(dve, bones_bf[64:128, 64:128], 1.0, (const_res,))
    idr, mkr = Res("ident"), Res("mask3")
    if has("F1"):
        ld(ident[:, :], ident_in, idr)
        ld(mask3[:, :], mask3_in, mkr)

    hT_v = hT.rearrange("(k p) t -> p k t", p=128)
    av = small[:, 64:80].rearrange("p (k v) -> p k v", v=2)
    bv = small[:, 96:112].rearrange("p (k v) -> p k v", v=2)
    gv = small[:, 128:144].rearrange("p (k v) -> p k v", v=2)

    def load_hT(t0, N):
        t, r = hT_ring.next()
        ld(t[:, :, 0:N], hT_v[:, :, t0:t0 + N], r)
        return t, r

    def load_tab(idx, t0, N):
        t, r = tabs_ring.next()
        ld(t[:, :, 0:N], tab[idx:idx + 2, :, t0:t0 + N].rearrange("a p t -> p a t"), r)
        return t, r

    def tiles_prefetch(tiles, tab_idx):
        def loads(t0, N):
            return (load_hT(t0, N), load_tab(tab_idx, t0, N) if tab_idx is not None else (None, None))
        nxt = None
        for idx, (t0, N) in enumerate(tiles):
            cur = nxt if nxt is not None else loads(t0, N)
            nxt = loads(*tiles[idx + 1]) if idx + 1 < len(tiles) else None
            (h_t, h_r), (tb_t, tb_r) = cur
            yield t0, N, h_t, h_r, tb_t, tb_r

    def load_w(dst, dres, src_cols_ap, ncols, kdim=8):
        s_t, s_r = stage_ring.next()
        ld(s_t[:, 0:kdim, 0:ncols], src_cols_ap, s_r)
        cp(pool, dst, s_t[:, 0:kdim, 0:ncols], (s_r,), (dres,))

    def win_cols(l, c0, n):
        return w_in[l, :, c0:c0 + n].rearrange("(k p) c -> p k c", p=128)

    def rope(z_ap, zsw_ap, zres, zswres, tabt, tabr, out_ap, out_res, P0, P1, N):
        t1, r1 = f32_ring.next()
        t2, r2 = f32_ring.next()
        tt(dve, t1[P0:P1, 0:N], zsw_ap, tabt[P0:P1, 1, 0:N], ALU.mult, (zswres, tabr), (r1,))
        tt(dve, t2[P0:P1, 0:N], z_ap, tabt[P0:P1, 0, 0:N], ALU.mult, (zres, tabr), (r2,))
        tt(pool, out_ap, t1[P0:P1, 0:N], t2[P0:P1, 0:N], ALU.add, (r1, r2), (out_res,))

    def proj(bank_i, w_t, w_res, c0, M, hT_t, hT_r, N, rows=None):
        for k in range(8):
            mm(psum[0:M, bank_i, 0:N], w_t[:, k, c0:c0 + M], hT_t[:, k, 0:N], k == 0, k == 7,
               (w_res, hT_r), (bank[bank_i],))

    def swap_halves(dst, src, res, ncols, half):
        nh = ncols // (2 * half)
        dv = dst.rearrange("p k (h j i) -> p k h j i", h=nh, j=2, i=half)
        sv = src.rearrange("p k (h j i) -> p k h j i", h=nh, j=2, i=half)
        for k in range(dst.shape[1]):
            cp(pool, dv[:, k, :, 0, :], sv[:, k, :, 1, :], (res,), (res,))
            cp(pool, dv[:, k, :, 1, :], sv[:, k, :, 0, :], (res,), (res,))

    def attn_core(kt_list, N, scale, acc_i, k_ap_fn, q_ap, q_res, v_ap_fn, tp=None, final=True):
        nk = len(kt_list)
        assert nk % 2 == 0
        npair = nk // 2
        base = attn_core.gi
        attn_core.gi += npair

        def qk(pi):
            g = (base + pi) % 2
            for j in range(2):
                kt = kt_list[2 * pi + j]
                mm(psum[:, 2 * g + j, 0:N], k_ap_fn(kt), q_ap, True, True, (q_res,), (grp[g],), tp=tp)

        qk(0)
        if npair > 1:
            qk(1)
        for pi in range(npair):
            g = (base + pi) % 2
            p_t, p_r = pT_ring.next()
            actf(p_t[:, :, 0:N], psum[:, 2 * g:2 * g + 2, 0:N], AF.Exp, (grp[g],), (p_r,), scale=scale)
            if pi + 2 < npair:
                qk(pi + 2)
            for j in range(2):
                kt = kt_list[2 * pi + j]
                first = (pi == 0 and j == 0)
                last = final and (pi == npair - 1 and j == 1)
                mm(psum[:, acc_i, 0:N], v_ap_fn(kt), p_t[:, j, 0:N], first, last, (p_r,), (bank[acc_i],))
    attn_core.gi = 0


    for l in range(DEPTH):
        final = (l == DEPTH - 1)
        need_ctx = not final
        lam_init = 0.8 - 0.6 * math.exp(-0.3 * l)
        my_qtiles = qtiles if need_ctx else qtiles[1:]
        x_src = x_in if l == 0 else x1T
        xc_src = xc_in if l == 0 else xc1T
        x_dst = y_out if final else x1T
        xc_dst = xc1T
        xk_src = x_src.rearrange("(k p) t -> p k t", p=128)
        xck_src = xc_src.rearrange("(k p) t -> p k t", p=128)

        def x_tile_ap(t0, N):
            if t0 < CTX:
                return xck_src[:, :, 0:N]
            return xk_src[:, :, t0 - CTX:t0 - CTX + N]

        if has("0"):
            ld(small[:, 0:8], g_pre[l].rearrange("(k p) -> p k", p=128), small_res)
            ld(small[:, 8:16], g_post[l].rearrange("(k p) -> p k", p=128), small_res)
            ld(small[:, 16:40], b_mod[l].rearrange("(k p) -> p k", p=128), small_res)
            ld(small[:, 40:41], g_cq[l, 0:128].rearrange("(k p) -> p k", p=128), small_res)
            ld(small[0:64, 41:42], g_cq[l, 128:192].rearrange("(k p) -> p k", p=64), small_res)
            ld(small[:, 42:43], g_ckv[l].rearrange("(k p) -> p k", p=128), small_res)
            ld(small[0:64, 43:44], g_diff[l].rearrange("(k p) -> p k", p=64), small_res)
            ld(small[64:128, 43:44], g_diff[l].rearrange("(k p) -> p k", p=64), small_res)
            ld(small[:, 44:46], s_pool[l].rearrange("(k p) -> p k", p=128), small_res)
            ld(small[:, 48:52], sink[l].partition_broadcast(128), small_res)
            for i, la in enumerate(lam_in):
                ld(stab[:, i * 32:(i + 1) * 32], la[l].partition_broadcast(128), stab_res)
            ld(csb[:, :, 0], c_in.rearrange("(k p) -> p k", p=128), csb_res)
            ld(csb[:, :, 1], cc_in.rearrange("(k p) -> p k", p=128), csb_res)
            S_.barrier()
            ts(dve, small[:, 43:44], small[:, 43:44], 1.0 - lam_init, None, ALU.mult, None, (small_res,), (small_res,))
            actf(small[:, 48:52], small[:, 48:52], AF.Exp, (small_res,), (small_res,))
            tt(dve, stab[:, 128:160], stab[:, 0:32], stab[:, 32:64], ALU.mult, (stab_res,), (stab_res,))
            S_.op(dve, lambda e: e.reduce_sum(out=small[:, 53:54], in_=stab[:, 128:160], axis=mybir.AxisListType.X),
                  (stab_res,), (small_res,))
            tt(dve, stab[:, 128:160], stab[:, 64:96], stab[:, 96:128], ALU.mult, (stab_res, small_res), (stab_res,))
            S_.op(dve, lambda e: e.reduce_sum(out=small[:, 54:55], in_=stab[:, 128:160], axis=mybir.AxisListType.X),
                  (stab_res,), (small_res,))
            actf(small[:, 53:55], small[:, 53:55], AF.Exp, (small_res,), (small_res,))
            stt(dve, small[:, 52:53], small[:, 54:55], -lam_init, small[:, 53:54], ALU.add, ALU.subtract,
                (small_res,), (small_res,))
            t_t, t_r = f32_ring.next()
            tv = t_t[:, 0:16].rearrange("p (k v) -> p k v", v=2)
            actf(tv, csb[:, :, :], AF.Tanh, (csb_res,), (t_r,), scale=0.5)
            stt(dve, tv, tv, 1.0, csb[:, :, :], ALU.add, ALU.mult, (t_r, csb_res), (t_r,))
            ts(dve, csb[:, :, :], tv, 0.5, None, ALU.mult, None, (t_r,), (csb_res,))
            for j in range(24):
                s_t, s_r = stage_ring.next()
                ld(s_t[:, :, 0:128], w_mod[l, :, j * 128:(j + 1) * 128].rearrange("(k p) c -> p k c", p=128), s_r)
                bi = 6 + (j % 2)
                for k in range(8):
                    mm(psum[:, bi, 0:2], s_t[:, k, 0:128], csb[:, k, :], k == 0, k == 7, (s_r, csb_res), (bank[bi],))
                ts(dve, modv[:, j, :], psum[:, bi, 0:2], small[:, 16 + j:17 + j], None, ALU.add, None,
                   (bank[bi], small_res), (modv_res,))
            for k in range(8):
                ts(dve, av[:, k, :], modv[:, 8 + k, :], 1.0, small[:, k:k + 1], ALU.add, ALU.mult,
                   (modv_res, small_res), (small_res,))
                ts(dve, gv[:, k, :], modv[:, 16 + k, :], small[:, 8 + k:9 + k], None, ALU.mult, None,
                   (modv_res, small_res), (small_res,))
            cp(dve, bv, modv[:, 0:8, :], (modv_res, small_res), (small_res,))
            st(small_s, small[:, :], small_res)
            S_.barrier()
            for (t0, N) in qtiles:
                v = 1 if t0 < CTX else 0
                x_t, x_r = stage_ring.next()
                ld(x_t[:, :, 0:N], x_tile_ap(t0, N), x_r)
                h_t, h_r = hT_ring.next()
                for k in range(8):
                    q_t, q_r = sq_ring.next()
                    actf(q_t[:, 0:N], x_t[:, k, 0:N], AF.Square, (x_r,), (q_r,))
                    mm(psum[:, 6, 0:N], ones_bf[:, :], q_t[:, 0:N], k == 0, k == 7, (q_r, const_res), (bank[6],))
                r_t, r_r = rstd_from(psum[:, 6, 0:N], bank[6], D, EPS, N)
                for k in range(8):
                    f_t, f_r = f32_ring.next()
                    stt(dve, f_t[:, 0:N], x_t[:, k, 0:N], av[:, k, v:v + 1], r_t[:, 0:N], ALU.mult, ALU.mult,
                        (x_r, r_r, small_res), (f_r,))
                    actf(h_t[:, k, 0:N], f_t[:, 0:N], AF.Identity, (f_r, small_res), (h_r,), bias=bv[:, k, v:v + 1])
                st(hT_v[:, :, t0:t0 + N], h_t[:, :, 0:N], h_r)
            S_.barrier()
        else:
            ld(small[:, :], small_s, small_res)
            S_.barrier()

        def gates_chunk(w_t, w_res, c0, sgi, bi, h_t, h_r, N):
            proj(bi, w_t, w_res, c0, 128, h_t, h_r, N)
            th_t, th_r = f32_ring.next()
            actf(th_t[:, 0:N], psum[:, bi, 0:N], AF.Tanh, (bank[bi],), (th_r,), scale=0.5)
            stt(dve, sg[:, sgi, 0:N], th_t[:, 0:N], 1.0, psum[:, bi, 0:N], ALU.add, ALU.mult,
                (th_r, bank[bi]), (sg_res[sgi],))

        if has("M"):
            ygm_v = ygm_s.rearrange("(c p) t -> p c t", p=128)
            KTm = [big[:, i * TT:(i + 1) * TT] for i in range(2)]
            VB = 2 * TT

            def vm_ap(kt, hh):
                base = VB + kt * 192 + hh * 64
                return big[:, base:base + 128]

            Vall = big[:, VB:VB + NKT * 192].rearrange("p (t s d) -> p t s d", t=NKT, s=3, d=64)
            load_w(wA[:, :, 0:128], wA_res, win_cols(l, C_CKV, 128), 128)
            memset(pool, wA[:, :, 128:320], 0.0, (wA_res,))
            load_w(wA[:, :, 192:224], wA_res, win_cols(l, C_KR, 32), 32)
            cp(pool, wA[:, :, 288:304], wA[:, :, 208:224], (wA_res,), (wA_res,))
            cp(pool, wA[:, :, 304:320], wA[:, :, 192:208], (wA_res,), (wA_res,))
            load_w(wA[:, :, 320:512], wA_res, win_cols(l, C_CQ, 192), 192)
            load_w(wA[:, :, 512:768], wA_res, win_cols(l, C_G + 256, 256), 256)
            s_t, s_r = stage_ring.next()
            memset(pool, s_t[:, 0:3, :], 0.0, (s_r,))
            ld(s_t[:, 0, 0:256], w_uk[l], s_r)
            ld(s_t[:, 0, 256:512], w_uv[l], s_r)
            ld(s_t[:, 1, 0:384], w_uq[l, 0:128, :], s_r)
            ld(s_t[0:64, 2, 0:384], w_uq[l, 128:192, :], s_r)
            cp(pool, wB[:, 0:3, 0:512], s_t[:, 0:3, 0:512], (s_r,), (wB_res,))
            cp(pool, wB[:, 3:5, 0:384], wB[:, 1:3, 0:384], (wB_res,), (wB_res,))
            for kk in range(2):
                sv_ = wB[:, 1 + kk, 0:384].rearrange("p (h c) -> p h c", c=96)
                dv_ = wB[:, 3 + kk, 0:384].rearrange("p (h c) -> p h c", c=96)
                cp(pool, dv_[:, :, 64:80], sv_[:, :, 80:96], (wB_res,), (wB_res,))
                cp(pool, dv_[:, :, 80:96], sv_[:, :, 64:80], (wB_res,), (wB_res,))
            for pr in range(2):
                memset(pool, Vall[:, :, 1, :], 1.0, ())
                for t0, N, h_t, h_r, tb_t, tb_r in tiles_prefetch(qtiles, 0):
                    is_ctx = t0 < CTX
                    kts = list(range(2)) if t0 < CTX else list(range(NKT))
                    proj(6, wA, wA_res, 0, 128, h_t, h_r, N)
                    q_t, q_r = sq_ring.next()
                    actf(q_t[:, 0:N], psum[:, 6, 0:N], AF.Square, (bank[6],), (q_r,))
                    mm(psum[:, 7, 0:N], ones_bf[:, :], q_t[:, 0:N], True, True, (q_r, const_res), (bank[7],))
                    r_t, r_r = rstd_from(psum[:, 7, 0:N], bank[7], 128, EPS, N)
                    cn_t, cn_r = bfx_ring.next()
                    stt(dve, cn_t[:, 0:N], psum[:, 6, 0:N], small[:, 42:43], r_t[:, 0:N], ALU.mult, ALU.mult,
                        (bank[6], r_r, small_res), (cn_r,))
                    proj(6, wA, wA_res, 128, 96, h_t, h_r, N)
                    proj(7, wA, wA_res, 224, 96, h_t, h_r, N)
                    kr_t, kr_r = bfx_ring.next()
                    rope(psum[64:96, 6, 0:N], psum[64:96, 7, 0:N], bank[6], bank[7], tb_t, tb_r,
                         kr_t[64:96, 0:N], kr_r, 64, 96, N)
                    for hh in range(2):
                        h = 2 * pr + hh
                        cp(pool, KTm[hh][64:96, t0:t0 + N], kr_t[64:96, 0:N], (kr_r,), ())
                        bi = 4 + hh
                        mm(psum[0:64, bi, 0:N], wB[:, 0, h * 64:(h + 1) * 64], cn_t[:, 0:N], True, True,
                           (wB_res, cn_r), (bank[bi],))
                        cp(act, KTm[hh][0:64, t0:t0 + N], psum[0:64, bi, 0:N], (bank[bi],), ())
                    for s in range(N // 128):
                        bi = 6 + (s % 2)
                        kt = t0 // 128 + s
                        mm(psum[:, bi, 0:128], cn_t[:, s * 128:(s + 1) * 128], wB[:, 0, 256 + pr * 128:256 + (pr + 1) * 128],
                           True, True, (cn_r, wB_res), (bank[bi],))
                        cp(dve, Vall[:, kt, 0:3:2, :], psum[:, bi, 0:128].rearrange("p (s d) -> p s d", s=2),
                           (bank[bi],), ())
                S_.barrier()
                for t0, N, h_t, h_r, tb_t, tb_r in tiles_prefetch(my_qtiles, 0):
                    is_ctx = t0 < CTX
                    kts = list(range(2)) if t0 < CTX else list(range(NKT))
                    proj(6, wA, wA_res, 320, 128, h_t, h_r, N)
                    proj(7, wA, wA_res, 448, 64, h_t, h_r, N)
                    qa_t, qa_r = sq_ring.next()
                    qb_t, qb_r = sq_ring.next()
                    actf(qa_t[:, 0:N], psum[:, 6, 0:N], AF.Square, (bank[6],), (qa_r,))
                    actf(qb_t[0:64, 0:N], psum[0:64, 7, 0:N], AF.Square, (bank[7],), (qb_r,))
                    mm(psum[:, 5, 0:N], ones_bf[:, :], qa_t[:, 0:N], True, False, (qa_r, const_res), (bank[5],))
                    mm(psum[:, 5, 0:N], ones_bf[0:64, :], qb_t[0:64, 0:N], False, True, (qb_r, const_res), (bank[5],))
                    r_t, r_r = rstd_from(psum[:, 5, 0:N], bank[5], 192, EPS, N)
                    ca_t, ca_r = bfx_ring.next()
                    cb_t, cb_r = bfx_ring.next()
                    stt(dve, ca_t[:, 0:N], psum[:, 6, 0:N], small[:, 40:41], r_t[:, 0:N], ALU.mult, ALU.mult,
                        (bank[6], r_r, small_res), (ca_r,))
                    stt(dve, cb_t[0:64, 0:N], psum[0:64, 7, 0:N], small[0:64, 41:42], r_t[0:64, 0:N], ALU.mult, ALU.mult,
                        (bank[7], r_r, small_res), (cb_r,))
                    gates_chunk(wA, wA_res, 512 + pr * 128, pr, 6, h_t, h_r, N)
                    for hh in range(2):
                        h = 2 * pr + hh
                        for (bi, kb) in ((6, 1), (7, 3)):
                            mm(psum[0:96, bi, 0:N], wB[:, kb, h * 96:(h + 1) * 96], ca_t[:, 0:N], True, False,
                               (wB_res, ca_r), (bank[bi],))
                            mm(psum[0:96, bi, 0:N], wB[0:64, kb + 1, h * 96:(h + 1) * 96], cb_t[0:64, 0:N], False, True,
                               (wB_res, cb_r), (bank[bi],))
                        q_t, q_r = qT[hh]
                        rope(psum[0:96, 6, 0:N], psum[0:96, 7, 0:N], bank[6], bank[7], tb_t, tb_r, q_t[0:96, 0:N], q_r,
                             0, 96, N)
                    for hh in range(2):
                        q_t, q_r = qT[hh]
                        acc = 4 + hh
                        attn_core(kts, N, 96 ** -0.5, acc,
                                  lambda kt, hh=hh: KTm[hh][0:96, kt * 128:(kt + 1) * 128],
                                  q_t[0:96, 0:N], q_r, lambda kt, hh=hh: vm_ap(kt, hh))
                        o0 = hh * 64
                        y_t, y_r = finalize_div(acc, o0, N)
                        tt(pool, yg[o0:o0 + 64, 2 + pr, 0:N], y_t[o0:o0 + 64, 0:N], sg[o0:o0 + 64, pr, 0:N], ALU.mult,
                           (y_r, sg_res[pr]), (yg_res[2 + pr],))
                    st(ygm_v[:, pr, t0:t0 + N], yg[:, 2 + pr, 0:N], yg_res[2 + pr])
                S_.barrier()

        if has("D"):
            ygd_v = ygd_s.rearrange("(c p) t -> p c t", p=128)
            KTd = big[:, 0:TT]
            VBd = TT

            def vd_ap(kt, hh):
                base = VBd + kt * 192 + hh * 64
                return big[:, base:base + 128]

            Vd = big[:, VBd:VBd + NKT * 192].rearrange("p (t s d) -> p t s d", t=NKT, s=3, d=64)
            load_w(wA[:, :, 0:256], wA_res, win_cols(l, C_DK, 256), 256)
            swap_halves(wA[:, :, 256:512], wA[:, :, 0:256], wA_res, 256, 16)
            load_w(wA[:, :, 512:768], wA_res, win_cols(l, C_DV, 256), 256)
            load_w(wB[:, :, 0:256], wB_res, win_cols(l, C_DQ, 256), 256)
            swap_halves(wB[:, :, 256:512], wB[:, :, 0:256], wB_res, 256, 16)
            load_w(wB[:, :, 512:768], wB_res, win_cols(l, C_G + 512, 256), 256)
            for i in range(4):
                memset(pool, qm[i][:, :], 0.0, (qm_res[i],))
            for pr in range(2):
                memset(pool, Vd[:, :, 1, :], 1.0, ())
                for t0, N, h_t, h_r, tb_t, tb_r in tiles_prefetch(qtiles, 2):
                    is_ctx = t0 < CTX
                    kts = list(range(2)) if t0 < CTX else list(range(NKT))
                    proj(6, wA, wA_res, pr * 128, 128, h_t, h_r, N)
                    proj(7, wA, wA_res, 256 + pr * 128, 128, h_t, h_r, N)
                    rope(psum[:, 6, 0:N], psum[:, 7, 0:N], bank[6], bank[7], tb_t, tb_r, KTd[:, t0:t0 + N], Res("kd"),
                         0, 128, N)
                    for s in range(N // 128):
                        bi = 4 + (s % 2)
                        kt = t0 // 128 + s
                        for k in range(8):
                            mm(psum[:, bi, 0:128], h_t[:, k, s * 128:(s + 1) * 128],
                               wA[:, k, 512 + pr * 128:512 + (pr + 1) * 128], k == 0, k == 7, (h_r, wA_res), (bank[bi],))
                        cp(dve, Vd[:, kt, 0:3:2, :], psum[:, bi, 0:128].rearrange("p (s d) -> p s d", s=2),
                           (bank[bi],), ())
                S_.barrier()
                for t0, N, h_t, h_r, tb_t, tb_r in tiles_prefetch(my_qtiles, 2):
                    is_ctx = t0 < CTX
                    kts = list(range(2)) if t0 < CTX else list(range(NKT))
                    proj(6, wB, wB_res, pr * 128, 128, h_t, h_r, N)
                    proj(7, wB, wB_res, 256 + pr * 128, 128, h_t, h_r, N)
                    t1, r1 = f32_ring.next()
                    t2, r2 = f32_ring.next()
                    tt(dve, t1[:, 0:N], psum[:, 7, 0:N], tb_t[:, 1, 0:N], ALU.mult, (bank[7], tb_r), (r1,))
                    tt(dve, t2[:, 0:N], psum[:, 6, 0:N], tb_t[:, 0, 0:N], ALU.mult, (bank[6], tb_r), (r2,))
                    for i in range(4):
                        tt(pool, qm[i][32 * i:32 * i + 32, 0:N], t1[32 * i:32 * i + 32, 0:N], t2[32 * i:32 * i + 32, 0:N],
                           ALU.add, (r1, r2), (qm_res[i],))
                    gates_chunk(wB, wB_res, 512 + pr * 128, pr, 6, h_t, h_r, N)
                    for hh in range(2):
                        o0 = hh * 64
                        ys = []
                        for m in range(2):
                            i = 2 * hh + m
                            acc = 4 + m
                            attn_core(kts, N, 32 ** -0.5, acc,
                                      lambda kt: KTd[:, kt * 128:(kt + 1) * 128],
                                      qm[i][:, 0:N], qm_res[i], lambda kt, hh=hh: vd_ap(kt, hh))
                            ys.append(finalize_div(acc, o0, N))
                        (y1, r1), (y2, r2) = ys
                        stt(dve, yraw[o0:o0 + 64, 0, 0:N], y2[o0:o0 + 64, 0:N], small[o0:o0 + 64, 52:53],
                            y1[o0:o0 + 64, 0:N], ALU.mult, ALU.add, (r1, r2, small_res), (yraw_res[0],))
                    s_t2, s_r2 = sq_ring.next()
                    actf(s_t2[:, 0:N], yraw[:, 0, 0:N], AF.Square, (yraw_res[0],), (s_r2,))
                    mm(psum[:, 6, 0:N], bones_bf[:, :], s_t2[:, 0:N], True, True, (s_r2, const_res), (bank[6],))
                    r_t, r_r = rstd_from(psum[:, 6, 0:N], bank[6], 64, EPS, N)
                    f_t, f_r = f32_ring.next()
                    stt(dve, f_t[:, 0:N], yraw[:, 0, 0:N], small[:, 43:44], r_t[:, 0:N], ALU.mult, ALU.mult,
                        (yraw_res[0], r_r, small_res), (f_r,))
                    tt(pool, yg[:, 4 + pr, 0:N], f_t[:, 0:N], sg[:, pr, 0:N], ALU.mult, (f_r, sg_res[pr]), (yg_res[4 + pr],))
                    st(ygd_v[:, pr, t0:t0 + N], yg[:, 4 + pr, 0:N], yg_res[4 + pr])
                S_.barrier()

        if has("F1"):
            ygs_v = ygs_s.rearrange("(c p) t -> p c t", p=128)
            ygp_v = ygp_s.rearrange("(c p) t -> p c t", p=128)
            pool_v = pool_s.rearrange("(c p) t -> p c t", p=128)
            rcnt_v = rcnt.rearrange("(c p) t -> p c t", p=128)
            KTs = big[:, 0:TT]
            VBs = TT

            def vs_ap(kt, kv):
                base = VBs + kt * 192 + kv * 64
                return big[:, base:base + 128]

            Vs = big[:, VBs:VBs + NKT * 192].rearrange("p (t s d) -> p t s d", t=NKT, s=3, d=64)
            memset(pool, Vs[:, :, 1, :], 1.0, ())
            memset(pool, pw[:, :, 0, 0:16], 0.0, (pw_res,))
            st(pool_v[:, :, 0:PPAD0], pw[:, :, 0, 0:PPAD0], pw_res)
            st(pool_v[:, :, PPAD0 + CTX:PPAD0 + CTX + PPAD1], pw[:, :, 0, 0:PPAD1], pw_res)
            st(pool_v[:, :, PT - PPAD2:PT], pw[:, :, 0, 0:PPAD2], pw_res)
            load_w(wA[:, :, 0:128], wA_res, win_cols(l, C_SK, 128), 128)
            swap_halves(wA[:, :, 128:256], wA[:, :, 0:128], wA_res, 128, 32)
            load_w(wA[:, :, 256:384], wA_res, win_cols(l, C_SV, 128), 128)
            load_w(wA[:, :, 384:640], wA_res, win_cols(l, C_POOL, 256), 256)
            for g in range(2):
                for kv in range(2):
                    load_w(wB[:, :, g * 128 + kv * 64:g * 128 + kv * 64 + 64], wB_res,
                           win_cols(l, C_SQ + (kv * 2 + g) * 64, 64), 64)
                    load_w(wB[:, :, 512 + g * 128 + kv * 64:512 + g * 128 + kv * 64 + 64], wB_res,
                           win_cols(l, C_G + 768 + (kv * 2 + g) * 64, 64), 64)
            swap_halves(wB[:, :, 256:512], wB[:, :, 0:256], wB_res, 256, 32)
            load_w(wB[:, :, 768:1024], wB_res, win_cols(l, C_G, 256), 256)
            s_t, s_r = stage_ring.next()
            memset(pool, s_t[:, 0:2, 0:128], 0.0, (s_r,))
            for g4 in range(4):
                r0 = (g4 % 2) * 64
                ld(s_t[r0:r0 + 64, g4 // 2, r0:r0 + 64], w_pool[l, g4], s_r)
            cp(pool, wpd[:, :, :], s_t[:, 0:2, 0:128], (s_r,), (wpd_res,))
            for t0, N, h_t, h_r, tb_t, tb_r in tiles_prefetch(qtiles, 4):
                is_ctx = t0 < CTX
                kts = list(range(2)) if t0 < CTX else list(range(NKT))
                proj(6, wA, wA_res, 0, 128, h_t, h_r, N)
                proj(7, wA, wA_res, 128, 128, h_t, h_r, N)
                rope(psum[:, 6, 0:N], psum[:, 7, 0:N], bank[6], bank[7], tb_t, tb_r, KTs[:, t0:t0 + N], Res("ks"), 0, 128, N)
                for s in range(N // 128):
                    bi = 4 + (s % 2)
                    kt = t0 // 128 + s
                    for k in range(8):
                        mm(psum[:, bi, 0:128], h_t[:, k, s * 128:(s + 1) * 128], wA[:, k, 256:384], k == 0, k == 7,
                           (h_r, wA_res), (bank[bi],))
                    cp(dve, Vs[:, kt, 0:3:2, :], psum[:, bi, 0:128].rearrange("p (s d) -> p s d", s=2), (bank[bi],), ())
                for c_ in range(2):
                    proj(6 + c_, wA, wA_res, 384 + c_ * 128, 128, h_t, h_r, N)
                    cp(act, pu[:, c_, 0:N], psum[:, 6 + c_, 0:N], (bank[6 + c_],), (pu_res,))
                po = t0 + (PPAD0 if t0 < CTX else PPAD0 + PPAD1)
                st(pool_v[:, :, po:po + N], pu[:, :, 0:N], pu_res)
            S_.barrier()
            for t0, N, h_t, h_r, tb_t, tb_r in tiles_prefetch(my_qtiles, 4):
                is_ctx = t0 < CTX
                kts = list(range(2)) if t0 < CTX else list(range(NKT))
                po = t0 + (PPAD0 if is_ctx else PPAD0 + PPAD1)
                ld(pu[:, :, 0:N + 16], pool_v[:, :, po - 8:po + N + 8], pu_res)
                ld(prc[:, :, 0:N], rcnt_v[:, :, po:po + N], prc_res)
                for g in range(2):
                    proj(6, wB, wB_res, g * 128, 128, h_t, h_r, N)
                    proj(7, wB, wB_res, 256 + g * 128, 128, h_t, h_r, N)
                    rope(psum[:, 6, 0:N], psum[:, 7, 0:N], bank[6], bank[7], tb_t, tb_r, qT[g][0][:, 0:N], qT[g][1], 0, 128, N)
                for i4 in range(4):
                    gates_chunk(wB, wB_res, 512 + i4 * 128, i4, 6 + (i4 % 2), h_t, h_r, N)
                for g in range(2):
                    q_t, q_r = qT[g]
                    for kv in range(2):
                        acc = 4 + kv
                        o0 = kv * 64
                        s0 = 64 - o0
                        hq = kv * 2 + g
                        wins = []
                        if not is_ctx:
                            qb0 = (t0 - CTX) // 128
                            for ktl in range(max(qb0 - 1, 0), min(qb0 + 4, NL - 1) + 1):
                                qlo, qhi = max(ktl - 1, qb0), min(ktl + 1, qb0 + 3)
                                wins.append((ktl, (qlo - qb0) * 128, (qhi - qb0 + 1) * 128, (qlo - ktl + 1) * 128))
                        attn_core([0, 1], N, 0.125, acc,
                                  lambda kt, kv=kv: KTs[kv * 64:kv * 64 + 64, kt * 128:(kt + 1) * 128],
                                  q_t[o0:o0 + 64, 0:N], q_r, lambda kt, kv=kv: vs_ap(kt, kv), final=(len(wins) == 0))
                        for wi, (ktl, c0, c1, m0) in enumerate(wins):
                            gq = attn_core.gi % 2
                            attn_core.gi += 1
                            b0 = 2 * gq
                            kt = 2 + ktl
                            mm(psum[:, b0, c0:c1], KTs[o0:o0 + 64, kt * 128:(kt + 1) * 128], q_t[o0:o0 + 64, c0:c1],
                               True, False, (q_r,), (grp[gq],))
                            mm(psum[:, b0, c0:c1], ident[:, :], mask3[:, m0:m0 + (c1 - c0)], False, True, (idr, mkr),
                               (grp[gq],))
                            p_t, p_r = pT_ring.next()
                            actf(p_t[:, 0, c0:c1], psum[:, b0, c0:c1], AF.Exp, (grp[gq],), (p_r,), scale=0.125)
                            mm(psum[:, acc, c0:c1], vs_ap(kt, kv), p_t[:, 0, c0:c1], False, wi == len(wins) - 1,
                               (p_r,), (bank[acc],))
                        y_t, y_r = finalize_div(acc, o0, N, add_ap=small[s0:s0 + 64, 48 + hq:49 + hq])
                        tt(pool, yg[o0:o0 + 64, 6 + g, 0:N], y_t[o0:o0 + 64, 0:N], sg[o0:o0 + 64, g, 0:N], ALU.mult,
                           (y_r, sg_res[g]), (yg_res[6 + g],))
                for g in range(2):
                    st(ygs_v[:, g, t0:t0 + N], yg[:, 6 + g, 0:N], yg_res[6 + g])
                M_ = N + 16
                for c_ in range(2):
                    for half in range(2):
                        gi4 = 2 * c_ + half
                        nlev = gi4 + 1
                        R0, R1 = half * 64, half * 64 + 64
                        tt(pool, pw[R0:R1, c_, 1, 1:M_], pu[R0:R1, c_, 1:M_], pu[R0:R1, c_, 0:M_ - 1], ALU.add,
                           (pu_res,), (pw_res,))
                        for lev in range(2, nlev + 1):
                            lo = (1 << lev) - 1
                            sh = 1 << (lev - 1)
                            src = pw[R0:R1, c_, (lev - 1) % 2, :]
                            tt(pool, pw[R0:R1, c_, lev % 2, lo:M_], src[:, lo:M_], src[:, lo - sh:M_ - sh], ALU.add,
                               (pw_res,), (pw_res,))
                        off = 8 + (1 << (nlev - 1)) - 1
                        f_t, f_r = f32_ring.next()
                        tt(pool, f_t[R0:R1, 0:N], pw[R0:R1, c_, nlev % 2, off:off + N], prc[R0:R1, c_, 0:N], ALU.mult,
                           (pw_res, prc_res), (f_r,))
                        tt(pool, pl[R0:R1, c_, 0:N], f_t[R0:R1, 0:N], pu[R0:R1, c_, 8:8 + N], ALU.subtract,
                           (f_r, pu_res), (pl_res[c_],))
                    bi = 6 + c_
                    mm(psum[:, bi, 0:N], wpd[:, c_, :], pl[:, c_, 0:N], True, True, (wpd_res, pl_res[c_]), (bank[bi],))
                    stt(dve, yg[:, c_, 0:N], psum[:, bi, 0:N], small[:, 44 + c_:45 + c_], sg[:, 2 + c_, 0:N],
                        ALU.mult, ALU.mult, (bank[bi], small_res, sg_res[2 + c_]), (yg_res[c_],))
                    st(ygp_v[:, c_, t0:t0 + N], yg[:, c_, 0:N], yg_res[c_])
            S_.barrier()

        if has("F2"):
            ysrc = {0: ygp_s, 1: ygm_s, 2: ygd_s, 3: ygs_s}
            yv = {r: ysrc[r].rearrange("(c p) t -> p c t", p=128) for r in range(4)}
            wm = big[:, 0:8 * 4096].rearrange("p (k c) -> p k c", k=8)
            for blk in range(8):
                load_w(wm[:, :, blk * 512:(blk + 1) * 512], wm_res, win_cols(l, C_M + blk * 512, 512), 512)
            for ch in range(2):
                s_t, s_r = stage_ring.next()
                for r in range(3):
                    ld(s_t[:, 2 * r:2 * r + 2, :], w_br[l, r, :, ch * 512:(ch + 1) * 512].rearrange("(kc p) c -> p kc c", p=128),
                       s_r)
                for g in range(2):
                    for kv in range(2):
                        r0 = (kv * 2 + g) * 64
                        ld(s_t[kv * 64:kv * 64 + 64, 6 + g, :], w_br[l, 3, r0:r0 + 64, ch * 512:(ch + 1) * 512], s_r)
                cp(pool, wA[:, :, ch * 512:(ch + 1) * 512], s_t[:, :, :], (s_r,), (wA_res,))
                s_t, s_r = stage_ring.next()
                ld(s_t[:, :, :], w_out[l, :, ch * 512:(ch + 1) * 512].rearrange("(k p) c -> p k c", p=128), s_r)
                cp(pool, wB[:, :, ch * 512:(ch + 1) * 512], s_t[:, :, :], (s_r,), (wB_res,))
            for t0, N, h_t, h_r, tb_t, tb_r in tiles_prefetch(my_qtiles, None):
                is_ctx = t0 < CTX
                v = 1 if is_ctx else 0
                for r in range(4):
                    for kc in range(2):
                        ld(yg[:, 2 * r + kc, 0:N], yv[r][:, kc, t0:t0 + N], yg_res[2 * r + kc])
                for j in range(8):
                    for r in range(4):
                        bz = 4 + (r % 2)
                        bb = 6 + (r % 2)
                        proj(bz, wm, wm_res, r * 1024 + j * 128, 128, h_t, h_r, N)
                        tm_t, tm_r = bfx_ring.next()
                        actf(tm_t[:, 0:N], psum[:, bz, 0:N], AF.Tanh, (bank[bz],), (tm_r,), scale=0.5)
                        for kc in range(2):
                            mm(psum[:, bb, 0:N], wA[:, 2 * r + kc, j * 128:(j + 1) * 128], yg[:, 2 * r + kc, 0:N],
                               kc == 0, kc == 1, (wA_res, yg_res[2 * r + kc]), (bank[bb],))
                        if r == 0:
                            stt(dve, macc[:, j, 0:N], tm_t[:, 0:N], 1.0, psum[:, bb, 0:N], ALU.add, ALU.mult,
                                (tm_r, bank[bb]), (macc_res[j],))
                        else:
                            f_t, f_r = f32_ring.next()
                            stt(dve, f_t[:, 0:N], tm_t[:, 0:N], 1.0, psum[:, bb, 0:N], ALU.add, ALU.mult,
                                (tm_r, bank[bb]), (f_r,))
                            if r < 3:
                                tt(pool, macc[:, j, 0:N], macc[:, j, 0:N], f_t[:, 0:N], ALU.add, (macc_res[j], f_r),
                                   (macc_res[j],))
                            else:
                                tt(pool, mbf[:, j, 0:N], macc[:, j, 0:N], f_t[:, 0:N], ALU.add, (macc_res[j], f_r),
                                   (mbf_res[j],))
                for j in range(8):
                    bo = 4 + (j % 2)
                    for k in range(8):
                        mm(psum[:, bo, 0:N], wB[:, k, j * 128:(j + 1) * 128], mbf[:, k, 0:N], k == 0, k == 7,
                           (wB_res, mbf_res[k]), (bank[bo],))
                    cp(act, macc[:, j, 0:N], psum[:, bo, 0:N], (bank[bo],), (macc_res[j],))
                    q_t, q_r = sq_ring.next()
                    actf(q_t[:, 0:N], psum[:, bo, 0:N], AF.Square, (bank[bo],), (q_r,))
                    mm(psum[:, 6, 0:N], ones_bf[:, :], q_t[:, 0:N], j == 0, j == 7, (q_r, const_res), (bank[6],))
                r_t, r_r = rstd_from(psum[:, 6, 0:N], bank[6], D, 16 * EPS, N)
                dst_v = (xc_dst if is_ctx else x_dst).rearrange("(k p) t -> p k t", p=128)
                tq = 0 if is_ctx else t0 - CTX
                for j in range(8):
                    xs_t, xs_r = f32_ring.next()
                    ld(xs_t[:, 0:N], x_tile_ap(t0, N)[:, j, :], xs_r)
                    f_t, f_r = f32_ring.next()
                    stt(dve, f_t[:, 0:N], macc[:, j, 0:N], gv[:, j, v:v + 1], r_t[:, 0:N], ALU.mult, ALU.mult,
                        (macc_res[j], r_r, small_res), (f_r,))
                    tt(pool, xs_t[:, 0:N], f_t[:, 0:N], xs_t[:, 0:N], ALU.add, (f_r, xs_r), (xs_r,))
                    st(dst_v[:, j, tq:tq + N], xs_t[:, 0:N], xs_r)
            S_.barrier()

    S_.barrier()
    with nc.Block() as block:
        @block.sync
        def _(e):
            S_.replay("sp", e)

        @block.tensor
        def _(e):
            S_.replay("pe", e)

        @block.scalar
        def _(e):
            S_.replay("act", e)

        @block.vector
        def _(e):
            S_.replay("dve", e)

        @block.gpsimd
        def _(e):
            S_.replay("pool", e)
    stack.close()
    nc.in_names_ = in_names
    nc.sched_ = S_
    return nc


def make_inputs_core(inputs, b, S, consts=None):
    if consts is None:
        consts = make_consts(S)
    m = {
        "xT": np.ascontiguousarray(inputs["x"][b, :S].T),
        "ctxT": np.ascontiguousarray(inputs["ctx"][b].T),
        "c": np.ascontiguousarray(inputs["c"][b]),
        "c_ctx": np.ascontiguousarray(inputs["c_ctx"]),
    }
    for k in ("w_mod", "b_mod", "g_pre", "g_post", "w_in", "w_pool", "s_pool", "g_cq", "w_uq", "g_ckv", "w_uk", "w_uv",
              "lam_q1", "lam_k1", "lam_q2", "lam_k2", "g_diff", "sink", "w_br", "w_out"):
        m[k] = np.ascontiguousarray(inputs[k])
    m.update(consts)
    return m


def kernel(**inputs):
    inputs = {k: np.asarray(v) for k, v in inputs.items()}
    B, S = inputs["x"].shape[0], inputs["x"].shape[1]
    consts = make_consts(S)
    nc = build_fused(S)
    names = nc.in_names_
    in_maps = []
    for b in range(B):
        m = make_inputs_core(inputs, b, S, consts)
        in_maps.append({k: m[k] for k in names})
    res = run_bass_kernel_spmd(nc, in_maps, core_ids=list(range(B)))
    out = np.stack([np.ascontiguousarray(np.asarray(res.results[b]["yT"]).T) for b in range(B)], 0)
    return out.astype(np.float32)
```
